# Optimizing a Trainium2 kernel written in Bass

```python
import math
import jax, jax.numpy as jnp
from jax import lax
import numpy as np

D_MODEL = 1024
BATCH = 16
SEQ = 256
DEPTH = 2
DEC_BATCH = 2
DEC_SEQ = 2048
PAST_LEN = 256

GRID_W = 64
N_MIXERS = 2
N_ATTN_LAYERS = (DEPTH + 1) // 2
N_SSM_LAYERS = DEPTH // 2
N_HEADS = 16
N_KV_HEADS = 4
HEAD_DIM = D_MODEL // N_HEADS
Q_PER_KV = N_HEADS // N_KV_HEADS
QKV_DIM = (N_HEADS + 2 * N_KV_HEADS) * HEAD_DIM
WINDOW = 128
BLOCK = 128
ATTN_SCALE = HEAD_DIM ** -0.5
ROPE_BASE = 10000.0
GROUP_CH = 16
N_GROUPS = D_MODEL // GROUP_CH
STATE_DIM = 64
D_FF = 4 * D_MODEL
N_MOD = 6
RMS_EPS = 1e-6
NEG_INF = -1e30

kernel_name = "hybrid_swa_s5_diffusion_step"


def _rmsnorm(x, g):
    x32 = x.astype(jnp.float32)
    y = x32 * lax.rsqrt(jnp.mean(x32 * x32, axis=-1, keepdims=True) + RMS_EPS)
    return (y * g.astype(jnp.float32)).astype(x.dtype)


def _modulation(cvec, w_mod, b_mod):
    return jnp.split(jax.nn.silu(cvec) @ w_mod + b_mod, N_MOD, axis=-1)


def _modulate(x, g, shift, scale):
    return _rmsnorm(x, g) * (1 + scale) + shift


def _mlp(h, w1, w2):
    a = jax.nn.relu(h @ w1)
    return (a * a) @ w2


def _split_qkv(qkv):
    b, l = qkv.shape[:2]
    q = qkv[..., :N_HEADS * HEAD_DIM].reshape(b, l, N_HEADS, HEAD_DIM)
    k = qkv[..., N_HEADS * HEAD_DIM:(N_HEADS + N_KV_HEADS) * HEAD_DIM].reshape(b, l, N_KV_HEADS, HEAD_DIM)
    v = qkv[..., (N_HEADS + N_KV_HEADS) * HEAD_DIM:].reshape(b, l, N_KV_HEADS, HEAD_DIM)
    return q, k, v


def _grid_positions(n_tokens):
    rows = n_tokens // GRID_W
    row = jnp.repeat(jnp.arange(rows, dtype=jnp.float32), GRID_W)
    col = jnp.tile(jnp.arange(GRID_W, dtype=jnp.float32), rows)
    return row, col


def _rotate(x, ang):
    cos = jnp.cos(ang)[None, :, None, :].astype(x.dtype)
    sin = jnp.sin(ang)[None, :, None, :].astype(x.dtype)
    x1, x2 = jnp.split(x, 2, axis=-1)
    return jnp.concatenate([x1 * cos - x2 * sin, x1 * sin + x2 * cos], axis=-1)


def _axial_rope(x, row, col):
    n_freq = HEAD_DIM // 4
    freqs = ROPE_BASE ** (-jnp.arange(n_freq, dtype=jnp.float32) / n_freq)
    half = HEAD_DIM // 2
    return jnp.concatenate([_rotate(x[..., :half], row[:, None] * freqs),
                            _rotate(x[..., half:], col[:, None] * freqs)], axis=-1)


def _sink_column(sink, lead_shape):
    s = sink.astype(jnp.float32).reshape((1,) * (len(lead_shape) - 3) + (N_KV_HEADS, Q_PER_KV, 1, 1))
    return jnp.broadcast_to(s, lead_shape + (1,))


def _context_attention(q, k, v, sink):
    b, l = q.shape[:2]
    nb = l // BLOCK
    qb = q.reshape(b, nb, BLOCK, N_KV_HEADS, Q_PER_KV, HEAD_DIM).transpose(1, 0, 2, 3, 4, 5)

    def one_block(q_blk):
        s = jnp.einsum('bqkgd,bckd->bkgqc', q_blk, k).astype(jnp.float32) * ATTN_SCALE
        logits = jnp.concatenate([s, _sink_column(sink, s.shape[:-1])], axis=-1)
        p = jax.nn.softmax(logits, axis=-1)[..., :-1].astype(v.dtype)
        o = jnp.einsum('bkgqc,bckd->bqkgd', p, v)
        return o.reshape(b, BLOCK, N_HEADS * HEAD_DIM)

    out = lax.map(one_block, qb)
    return out.transpose(1, 0, 2, 3).reshape(b, l, N_HEADS * HEAD_DIM)


def _latent_window_attention(q, k, v, k_ctx, v_ctx, sink):
    b, l = q.shape[:2]
    nb = l // BLOCK
    lc = k_ctx.shape[1]
    pad = ((0, 0), (BLOCK, BLOCK), (0, 0), (0, 0))
    kp = jnp.pad(k, pad).reshape(b, nb + 2, BLOCK, N_KV_HEADS, HEAD_DIM)
    vp = jnp.pad(v, pad).reshape(b, nb + 2, BLOCK, N_KV_HEADS, HEAD_DIM)
    kw = jnp.concatenate([kp[:, :-2], kp[:, 1:-1], kp[:, 2:]], axis=2)
    vw = jnp.concatenate([vp[:, :-2], vp[:, 1:-1], vp[:, 2:]], axis=2)
    qb = q.reshape(b, nb, BLOCK, N_KV_HEADS, Q_PER_KV, HEAD_DIM)
    s_win = jnp.einsum('bnqkgd,bnskd->bnkgqs', qb, kw).astype(jnp.float32) * ATTN_SCALE
    s_ctx = jnp.einsum('bnqkgd,bckd->bnkgqc', qb, k_ctx).astype(jnp.float32) * ATTN_SCALE
    r = jnp.arange(BLOCK)[:, None]
    j = jnp.arange(3 * BLOCK)[None, :]
    band = jnp.abs(j - BLOCK - r) <= WINDOW
    kpos = jnp.arange(nb)[:, None] * BLOCK - BLOCK + jnp.arange(3 * BLOCK)[None, :]
    valid = (kpos >= 0) & (kpos < l)
    mask = (band[None, :, :] & valid[:, None, :])[None, :, None, None]
    logits = jnp.concatenate([jnp.where(mask, s_win, NEG_INF), s_ctx,
                              _sink_column(sink, s_ctx.shape[:-1])], axis=-1)
    p = jax.nn.softmax(logits, axis=-1)
    p_win = p[..., :3 * BLOCK].astype(v.dtype)
    p_ctx = p[..., 3 * BLOCK:3 * BLOCK + lc].astype(v.dtype)
    o = (jnp.einsum('bnkgqs,bnskd->bnqkgd', p_win, vw)
         + jnp.einsum('bnkgqc,bckd->bnqkgd', p_ctx, v_ctx))
    return o.reshape(b, l, N_HEADS * HEAD_DIM)


def _complex_affine_combine(left, right):
    ar1, ai1, br1, bi1 = left
    ar2, ai2, br2, bi2 = right
    return (ar1 * ar2 - ai1 * ai2,
            ar1 * ai2 + ai1 * ar2,
            ar2 * br1 - ai2 * bi1 + br2,
            ar2 * bi1 + ai2 * br1 + bi2)


def _ssm_scan(u, lam_re, lam_im, log_dt, b_re, b_im, s0_re, s0_im, reverse):
    lam_re = lam_re.astype(jnp.float32)
    lam_im = lam_im.astype(jnp.float32)
    dt = jnp.exp(log_dt.astype(jnp.float32))[:, None]
    mag = jnp.exp(lam_re * dt)
    ang = lam_im * dt
    ab_re = mag * jnp.cos(ang)
    ab_im = mag * jnp.sin(ang)
    den = lam_re * lam_re + lam_im * lam_im
    num_re = ab_re - 1.0
    num_im = ab_im
    f_re = (num_re * lam_re + num_im * lam_im) / den
    f_im = (num_im * lam_re - num_re * lam_im) / den
    b_re = b_re.astype(jnp.float32)
    b_im = b_im.astype(jnp.float32)
    bb_re = f_re[..., None] * b_re - f_im[..., None] * b_im
    bb_im = f_re[..., None] * b_im + f_im[..., None] * b_re
    bu_re = jnp.einsum('blgc,gpc->blgp', u, bb_re)
    bu_im = jnp.einsum('blgc,gpc->blgp', u, bb_im)
    if reverse:
        bu_re = jnp.flip(bu_re, axis=1)
        bu_im = jnp.flip(bu_im, axis=1)
    s0_re = s0_re.astype(jnp.float32)
    s0_im = s0_im.astype(jnp.float32)
    first_re = ab_re * s0_re - ab_im * s0_im + bu_re[:, 0]
    first_im = ab_re * s0_im + ab_im * s0_re + bu_im[:, 0]
    bu_re = bu_re.at[:, 0].set(first_re)
    bu_im = bu_im.at[:, 0].set(first_im)
    a_re = jnp.broadcast_to(ab_re, bu_re.shape)
    a_im = jnp.broadcast_to(ab_im, bu_im.shape)
    _, _, s_re, s_im = lax.associative_scan(_complex_affine_combine, (a_re, a_im, bu_re, bu_im), axis=1)
    if reverse:
        s_re = jnp.flip(s_re, axis=1)
        s_im = jnp.flip(s_im, axis=1)
    return s_re, s_im


def _bidir_s5(h, s0, lam_re, lam_im, log_dt, b_re, b_im, c_re, c_im, d_skip, w_a, w_b):
    b, l, _ = h.shape
    u = h.astype(jnp.float32).reshape(b, l, N_GROUPS, GROUP_CH)
    y = u * d_skip.astype(jnp.float32).reshape(N_GROUPS, GROUP_CH)
    states = []
    for d in range(2):
        s_re, s_im = _ssm_scan(u, lam_re[d], lam_im[d], log_dt[d], b_re[d], b_im[d],
                               s0[:, d, 0], s0[:, d, 1], reverse=(d == 1))
        y = (y + jnp.einsum('blgp,gcp->blgc', s_re, c_re[d].astype(jnp.float32))
             - jnp.einsum('blgp,gcp->blgc', s_im, c_im[d].astype(jnp.float32)))
        states.append((s_re, s_im))
    y = jax.nn.gelu(y.reshape(b, l, D_MODEL).astype(h.dtype))
    return (y @ w_a) * jax.nn.sigmoid(y @ w_b), states


def setup_inputs(seed: int = 0) -> dict:
    key = jax.random.key(seed)
    ks = jax.random.split(key, 27)

    def nrm(k, shape, scale=1.0):
        return jax.random.normal(k, shape, jnp.float32) * scale

    lam_n = jnp.arange(STATE_DIM, dtype=jnp.float32)
    lam_re = -0.5 + nrm(ks[14], (N_SSM_LAYERS, 2, N_GROUPS, STATE_DIM), 0.01)
    lam_im = math.pi * lam_n + nrm(ks[15], (N_SSM_LAYERS, 2, N_GROUPS, STATE_DIM), 0.01)
    log_dt = jax.random.uniform(ks[16], (N_SSM_LAYERS, 2, N_GROUPS), jnp.float32,
                                math.log(1e-3), math.log(1e-1))
    return {
        "x_prompt": nrm(ks[0], (BATCH, SEQ, D_MODEL)),
        "x_sample": nrm(ks[1], (DEC_BATCH, DEC_SEQ, D_MODEL)),
        "cache_k": nrm(ks[2], (DEC_BATCH, N_ATTN_LAYERS, PAST_LEN, N_KV_HEADS, HEAD_DIM)),
        "cache_v": nrm(ks[3], (DEC_BATCH, N_ATTN_LAYERS, PAST_LEN, N_KV_HEADS, HEAD_DIM)),
        "state_ssm": nrm(ks[4], (DEC_BATCH, N_SSM_LAYERS, 2, 2, N_GROUPS, STATE_DIM), 0.1),
        "c": nrm(ks[5], (DEC_BATCH, D_MODEL)),
        "c_ctx": nrm(ks[6], (D_MODEL,)),
        "norm1_g": 1.0 + nrm(ks[7], (DEPTH, D_MODEL), 0.01),
        "norm2_g": 1.0 + nrm(ks[8], (DEPTH, D_MODEL), 0.01),
        "w_mod": nrm(ks[9], (DEPTH, D_MODEL, N_MOD * D_MODEL), 0.5 * D_MODEL ** -0.5),
        "b_mod": nrm(ks[10], (DEPTH, N_MOD * D_MODEL), 0.01),
        "w_qkv": nrm(ks[11], (N_ATTN_LAYERS, D_MODEL, QKV_DIM), D_MODEL ** -0.5),
        "w_o": nrm(ks[12], (N_ATTN_LAYERS, N_HEADS * HEAD_DIM, D_MODEL), (N_HEADS * HEAD_DIM) ** -0.5),
        "attn_sink": nrm(ks[13], (N_ATTN_LAYERS, N_HEADS), 0.5),
        "ssm_lam_re": lam_re,
        "ssm_lam_im": lam_im,
        "ssm_log_dt": log_dt,
        "ssm_b_re": nrm(ks[17], (N_SSM_LAYERS, 2, N_GROUPS, STATE_DIM, GROUP_CH), (2 * GROUP_CH) ** -0.5),
        "ssm_b_im": nrm(ks[18], (N_SSM_LAYERS, 2, N_GROUPS, STATE_DIM, GROUP_CH), (2 * GROUP_CH) ** -0.5),
        "ssm_c_re": nrm(ks[19], (N_SSM_LAYERS, 2, N_GROUPS, GROUP_CH, STATE_DIM), STATE_DIM ** -0.5),
        "ssm_c_im": nrm(ks[20], (N_SSM_LAYERS, 2, N_GROUPS, GROUP_CH, STATE_DIM), STATE_DIM ** -0.5),
        "ssm_d": nrm(ks[21], (N_SSM_LAYERS, D_MODEL)),
        "glu_w_a": nrm(ks[22], (N_SSM_LAYERS, D_MODEL, D_MODEL), D_MODEL ** -0.5),
        "glu_w_b": nrm(ks[23], (N_SSM_LAYERS, D_MODEL, D_MODEL), D_MODEL ** -0.5),
        "mlp_w1": nrm(ks[24], (DEPTH, D_MODEL, D_FF), D_MODEL ** -0.5),
        "mlp_w2": nrm(ks[25], (DEPTH, D_FF, D_MODEL), D_FF ** -0.5),
        "final_norm_g": 1.0 + nrm(ks[26], (D_MODEL,), 0.01),
    }


def reference(x_prompt, x_sample, cache_k, cache_v, state_ssm, c, c_ctx,
              norm1_g, norm2_g, w_mod, b_mod, w_qkv, w_o, attn_sink,
              ssm_lam_re, ssm_lam_im, ssm_log_dt, ssm_b_re, ssm_b_im, ssm_c_re, ssm_c_im,
              ssm_d, glu_w_a, glu_w_b, mlp_w1, mlp_w2, final_norm_g):
    xp = x_prompt
    xx = x_sample
    bp = xp.shape[0]
    row, col = _grid_positions(xx.shape[1])
    new_k, new_v, new_s = [], [], []
    for i in range(DEPTH):
        j = i // N_MIXERS
        sh1p, sc1p, g1p, sh2p, sc2p, g2p = _modulation(c_ctx[None, None, :], w_mod[i], b_mod[i])
        sh1x, sc1x, g1x, sh2x, sc2x, g2x = _modulation(c[:, None, :], w_mod[i], b_mod[i])
        hp = _modulate(xp, norm1_g[i], sh1p, sc1p)
        hx = _modulate(xx, norm1_g[i], sh1x, sc1x)
        if i % N_MIXERS == 0:
            qp, kp, vp = _split_qkv(hp @ w_qkv[j])
            new_k.append(kp)
            new_v.append(vp)
            op = _context_attention(qp, kp, vp, attn_sink[j]) @ w_o[j]
            qx, kx, vx = _split_qkv(hx @ w_qkv[j])
            qx = _axial_rope(qx, row, col)
            kx = _axial_rope(kx, row, col)
            ox = _latent_window_attention(qx, kx, vx, cache_k[:, j], cache_v[:, j], attn_sink[j]) @ w_o[j]
        else:
            ssm_args = (ssm_lam_re[j], ssm_lam_im[j], ssm_log_dt[j], ssm_b_re[j], ssm_b_im[j],
                        ssm_c_re[j], ssm_c_im[j], ssm_d[j], glu_w_a[j], glu_w_b[j])
            s0p = jnp.zeros((bp, 2, 2, N_GROUPS, STATE_DIM), jnp.float32)
            op, st = _bidir_s5(hp, s0p, *ssm_args)
            fwd_final = jnp.stack([st[0][0][:, -1], st[0][1][:, -1]], axis=1)
            bwd_final = jnp.stack([st[1][0][:, 0], st[1][1][:, 0]], axis=1)
            new_s.append(jnp.stack([fwd_final, bwd_final], axis=1))
            ox, _ = _bidir_s5(hx, state_ssm[:, j], *ssm_args)
        xp = xp + g1p * op
        xx = xx + g1x * ox
        xp = xp + g2p * _mlp(_modulate(xp, norm2_g[i], sh2p, sc2p), mlp_w1[i], mlp_w2[i])
        xx = xx + g2x * _mlp(_modulate(xx, norm2_g[i], sh2x, sc2x), mlp_w1[i], mlp_w2[i])
    y_prompt = _rmsnorm(xp, final_norm_g)
    y_sample = _rmsnorm(xx, final_norm_g)
    new_cache_k = jnp.stack(new_k, axis=1)
    new_cache_v = jnp.stack(new_v, axis=1)
    new_state_ssm = jnp.stack(new_s, axis=1)
    return (y_prompt, y_sample, new_cache_k, new_cache_v, new_state_ssm)
```

```python
import math
import os
import numpy as np
import concourse.bass as bass
import concourse.mybir as mybir
from concourse.bass_utils import run_bass_kernel_spmd

F32 = mybir.dt.float32
BF16 = mybir.dt.bfloat16
ALU = mybir.AluOpType
AF = mybir.ActivationFunctionType

NCORE = 8
D = 1024
KC = 8
NT = 1024
EPS = 1e-6
TWO_PI = 2.0 * math.pi


class Op:
    __slots__ = ("eng", "fn", "kind", "deps", "signal", "sig_val", "dsem", "dval", "prev_dma", "idx")

    def __init__(self, eng, fn, kind):
        self.eng = eng
        self.fn = fn
        self.kind = kind
        self.deps = []
        self.signal = False
        self.sig_val = None
        self.dsem = None
        self.dval = None
        self.prev_dma = None
        self.idx = None


class Prog:
    ENGS = ("pe", "act", "dve", "pool", "sp")
    NDMA = {"sp": 20, "pool": 20, "act": 6}

    def __init__(self):
        self.nc = bass.Bass("TRN2", target_bir_lowering=False)
        self.ops = {e: [] for e in self.ENGS}
        self.last_write = {}
        self.readers = {}
        self.dma_count = {e: 0 for e in self.ENGS}
        self.dma_hist = {e: [] for e in self.ENGS}
        self.cc_ops = []

    def op(self, eng, fn, reads=(), writes=(), dma=False, cc=False):
        kind = "cc" if cc else ("dma" if dma else None)
        o = Op(eng, fn, kind)
        deps = {}
        for k in reads:
            w = self.last_write.get(k)
            if w is not None:
                deps[id(w)] = (w, True)
        for k in writes:
            w = self.last_write.get(k)
            if w is not None and id(w) not in deps:
                deps[id(w)] = (w, False)
            for r in self.readers.get(k, ()):
                if id(r) not in deps:
                    deps[id(r)] = (r, False)
        for d, raw in deps.values():
            if d is o:
                continue
            if d.kind is None and kind is None and d.eng == eng:
                if eng == "pe" or (eng in ("dve", "act") and not raw):
                    continue
            o.deps.append(d)
            if d.kind is None:
                d.signal = True
        for k in reads:
            self.readers.setdefault(k, []).append(o)
        for k in writes:
            self.last_write[k] = o
            self.readers[k] = []
        if kind == "dma":
            n = self.NDMA[eng]
            i = self.dma_count[eng]
            self.dma_count[eng] += 1
            o.idx = i
            hist = self.dma_hist[eng]
            if i >= n:
                o.prev_dma = hist[i - n]
            hist.append(o)
        elif kind == "cc":
            o.idx = len(self.cc_ops)
            self.cc_ops.append(o)
        self.ops[eng].append(o)
        return o

    def fence(self, engs=("pe", "act", "dve", "pool", "sp")):
        deps = []
        for e in ("pe", "act", "dve", "pool"):
            comp = [o for o in self.ops[e] if o.kind is None]
            if comp:
                deps.append(comp[-1])
        for e, n in self.NDMA.items():
            deps += self.dma_hist[e][-n:]
        deps += self.cc_ops[-1:]
        for e in engs:
            o = Op(e, lambda eng: eng.nop(), None)
            o.deps = list(deps)
            for d in o.deps:
                if d.kind is None:
                    d.signal = True
            self.ops[e].append(o)

    def emit(self):
        nc = self.nc
        sems = {e: nc.alloc_semaphore("c_" + e) for e in ("pe", "act", "dve", "pool")}
        dsems = {e: [nc.alloc_semaphore(f"d_{e}{i}") for i in range(n)] for e, n in self.NDMA.items()}
        ccsem = nc.alloc_semaphore("ccsem")
        for e in self.ENGS:
            c = 0
            for o in self.ops[e]:
                if o.kind == "dma":
                    n = self.NDMA[e]
                    o.dsem = dsems[e][o.idx % n]
                    o.dval = 16 * (o.idx // n + 1)
                elif o.kind == "cc":
                    o.dsem = ccsem
                    o.dval = o.idx + 1
                elif o.signal:
                    c += 1
                    o.sig_val = c
        engobj = {"pe": "tensor", "act": "scalar", "dve": "vector", "pool": "gpsimd", "sp": "sync"}
        with nc.Block() as block:
            for e in self.ENGS:
                ops = self.ops[e]

                def body(eng, e=e, ops=ops):
                    waited = {}
                    for o in ops:
                        need = []
                        for d in o.deps:
                            if d.kind is not None:
                                need.append((d.dsem, d.dval))
                            else:
                                need.append((sems[d.eng], d.sig_val))
                        if o.prev_dma is not None:
                            need.append((o.prev_dma.dsem, o.prev_dma.dval))
                        for s, v in need:
                            k = id(s)
                            if waited.get(k, 0) >= v:
                                continue
                            waited[k] = v
                            eng.wait_ge(s, v)
                        ins = o.fn(eng)
                        if o.kind == "dma":
                            ins.then_inc(o.dsem, 16)
                        elif o.kind == "cc":
                            ins.then_inc(o.dsem)
                        elif o.signal:
                            ins.then_inc(sems[e], 1)
                    last = {}
                    for o in ops:
                        if o.kind is not None:
                            last[id(o.dsem)] = (o.dsem, o.dval)
                    for s, v in last.values():
                        if waited.get(id(s), 0) < v:
                            eng.wait_ge(s, v)

                getattr(block, engobj[e])(body)
        return nc


def build_program(stage=99):
    P = Prog()
    nc = P.nc
    dbg = {}

    def din(name, shape, dt=F32):
        return nc.dram_tensor(name, list(shape), dt, kind="ExternalInput")

    def dout(name, shape, dt=F32):
        return nc.dram_tensor(name, list(shape), dt, kind="ExternalOutput")

    x_own = din("x_own", [128, KC, NT])
    x_halo = din("x_halo", [128, KC, 256])
    cvec = din("cvec", [128, KC, 2])
    ng_d = din("ng", [128, 5, KC])
    bmod_d = din("bmodT", [128, 2, 48])
    rope_d = din("rope", [128, 2, 768])
    amask_d = din("amask", [128, 4, 2, 128], BF16)
    ident_d = din("ident", [128, 128])
    sink_d = din("sink", [128, 16])
    ck_d = din("ck", [256, 256])
    cv_d = din("cv", [256, 256])
    w_mod = din("w_mod", [2, D, 6 * D])
    wq_d = din("wq", [D, D])
    wqp_d = din("wqp", [D, D])
    wkd_d = din("wkd", [D, 512])
    wkdp_d = din("wkdp", [D, 512])
    wkv_d = din("wkv", [D, 512])
    wo_d = din("w_o", [D, D])
    w1_d = din("mlp_w1", [2, D, 4 * D])
    w2_d = din("mlp_w2", [2, 4 * D, D])
    y_d = dout("y", [128, KC, NT])
    nk_d = dout("new_k", [512, 256])
    nv_d = dout("new_v", [512, 256])

    X = nc.alloc_sbuf_tensor("s_X", [128, KC, NT], F32)
    H = nc.alloc_sbuf_tensor("s_H", [128, KC, NT], BF16)
    BIG = nc.alloc_sbuf_tensor("s_BIG", [128, 32768], BF16)
    WB = [nc.alloc_sbuf_tensor(f"s_WB{i}", [128, 4096], BF16) for i in range(3)]
    PT = [nc.alloc_sbuf_tensor(f"s_PT{i}", [128, 5, 512], BF16) for i in range(2)]
    SCR = [nc.alloc_sbuf_tensor(f"s_SCR{i}", [128, 512], F32) for i in range(4)]
    SQ = nc.alloc_sbuf_tensor("s_SQ", [128, KC, 512], BF16)
    RSTD = nc.alloc_sbuf_tensor("s_RSTD", [128, 512], F32)
    ones_bf = nc.alloc_sbuf_tensor("s_ones_bf", [128, 128], BF16)
    ident = nc.alloc_sbuf_tensor("s_ident", [128, 128], F32)
    cs_f = nc.alloc_sbuf_tensor("s_cs_f", [128, KC, 2], F32)
    cs_b = nc.alloc_sbuf_tensor("s_cs_b", [128, KC, 2], BF16)
    ng = nc.alloc_sbuf_tensor("s_ng", [128, 5, KC], F32)
    bmod = nc.alloc_sbuf_tensor("s_bmod", [128, 2, 48], F32)
    modT = nc.alloc_sbuf_tensor("s_modT", [128, 2, 48, 2], F32)
    gs = nc.alloc_sbuf_tensor("s_gs", [128, 2, 2, KC, 2], F32)
    L0B = nc.alloc_sbuf_tensor("s_L0B", [128, 3584], F32)
    rope = L0B[:, 0:1536].rearrange("p (a t) -> p a t", a=2)
    amask = nc.alloc_sbuf_tensor("s_amask", [128, 4, 2, 128], BF16)
    sink = nc.alloc_sbuf_tensor("s_sink", [128, 16], F32)
    esink = nc.alloc_sbuf_tensor("s_esink", [128, 16], F32)
    XH = nc.alloc_sbuf_tensor("s_XH", [128, KC, 256], F32)
    HH = nc.alloc_sbuf_tensor("s_HH", [128, KC, 256], BF16)
    CKV = L0B[:, 1536:2560].rearrange("p (a b n) -> p a b n", a=2, b=2)
    CKD = nc.alloc_sbuf_tensor("s_CKD", [128, 2, 4, 2, 64], BF16)
    ident_bf = nc.alloc_sbuf_tensor("s_ident_bf", [128, 128], BF16)
    KVO = L0B[:, 2560:3584].rearrange("p (a n) -> p a n", a=2)
    DEN = [nc.alloc_sbuf_tensor(f"s_DEN{i}", [128, 512], F32) for i in range(2)]
    RELU = [nc.alloc_sbuf_tensor(f"s_RELU{i}", [128, 512], BF16) for i in range(2)]
    PS = [nc.alloc_psum_tensor(f"p_PS{i}", [128, 512], F32) for i in range(8)]

    QT = BIG[:, 0:8192].rearrange("p (c t) -> p c t", c=8)
    KT = BIG[:, 8192:14336].rearrange("p (c t) -> p c t", c=4)
    VA = BIG[:, 14336:21504].rearrange("p (b k e) -> p b k e", b=14, k=4)
    ATT = BIG[:, 21504:29696].rearrange("p (c t) -> p c t", c=8)
    HID = BIG[:, 0:32768].rearrange("p (c t) -> p c t", c=32)

    ps_rr = [0]
    ps_mod = [8]

    def next_ps():
        i = ps_rr[0] % ps_mod[0]
        ps_rr[0] += 1
        return PS[i], ("ps", i)

    wb_rr = [0]

    def load_w(src_ap, shape_str=None, **kw):
        i = wb_rr[0] % 3
        wb_rr[0] += 1
        n = 1
        for s in src_ap.shape[1:]:
            n *= s
        dst = WB[i][:, 0:n]
        if shape_str is not None:
            dst = dst.rearrange(shape_str, **kw)
        P.op("pool", lambda e: e.dma_start(out=dst, in_=src_ap), writes=[("wb", i)], dma=True)
        return dst, ("wb", i)

    def dma_in(dst, src, key, eng="sp"):
        P.op(eng, lambda e: e.dma_start(out=dst, in_=src), writes=[key], dma=True)

    dma_in(X[:], x_own.ap(), "X_all")
    dma_in(XH[:], x_halo.ap(), "XH")
    dma_in(cs_f[:], cvec.ap(), "cs_f")
    dma_in(ng[:], ng_d.ap(), "ng")
    dma_in(bmod[:], bmod_d.ap(), "bmod")
    dma_in(rope, rope_d.ap(), "rope")
    dma_in(amask[:], amask_d.ap(), "amask")
    dma_in(ident[:], ident_d.ap(), "ident")
    dma_in(sink[:], sink_d.ap(), "sink")
    dma_in(CKV[:, :, 0, :], ck_d.ap().rearrange("(b p) n -> p b n", p=128), "CK")
    dma_in(CKV[:, :, 1, :], cv_d.ap().rearrange("(b p) n -> p b n", p=128), "CV")
    P.op("dve", lambda e: e.memset(ones_bf[:], 1.0), writes=["ones"])
    P.op("dve", lambda e: e.tensor_copy(ident_bf[:], ident[:]), reads=["ident"], writes=["ident_bf"])
    P.op("act", lambda e: e.activation(esink[:], sink[:], AF.Exp), reads=["sink"], writes=["esink"])
    P.op("act", lambda e: e.activation(cs_b[:], cs_f[:], AF.Silu), reads=["cs_f"], writes=["cs_b"])
    xkeys = [("X", kc, h) for kc in range(KC) for h in range(2)]
    for k in xkeys:
        P.last_write[k] = P.last_write["X_all"]

    def modulation_gen(i):
        psm = PS[7 - i]
        for piece in range(6):
            pk = ("psm", i, piece)
            for hb in range(2):
                cb = piece * 2 + hb
                wv, wk = load_w(w_mod.ap()[i].rearrange("(k p) n -> p k n", p=128)[:, :, cb * 512:(cb + 1) * 512],
                                "p (k n) -> p k n", k=KC)
                for cc in range(4):
                    j = cb * 4 + cc
                    for kc in range(KC):
                        P.op("pe", lambda e, j=j, kc=kc, cc=cc, wv=wv: e.matmul(
                            psm[:, j * 2:(j + 1) * 2], wv[:, kc, cc * 128:(cc + 1) * 128], cs_b[:, kc, :],
                            start=(kc == 0), stop=(kc == KC - 1)),
                            reads=[wk, "cs_b"], writes=[pk])
                if hb == 0:
                    yield
            bm = bass.AP(tensor=bmod, offset=i * 48 + piece * 8, ap=[[96, 128], [1, 8], [0, 2]])
            P.op("dve", lambda e, piece=piece, bm=bm: e.tensor_tensor(
                modT[:, i, piece * 8:(piece + 1) * 8, :], psm[:, piece * 16:(piece + 1) * 16].rearrange("p (j s) -> p j s", s=2), bm, ALU.add),
                reads=[pk, "bmod"], writes=[("modT", i, piece)])
            if piece in (1, 4):
                w = (piece - 1) // 3
                sc = modT[:, i, piece * 8:(piece + 1) * 8, :]
                gv = bass.AP(tensor=ng, offset=(2 * i + w) * KC, ap=[[5 * KC, 128], [1, KC], [0, 2]])
                P.op("dve", lambda e, w=w, sc=sc: e.tensor_scalar(gs[:, i, w], sc, 1.0, 32.0, ALU.add, ALU.mult),
                     reads=[("modT", i, piece)], writes=[("gs0", i, w)])
                P.op("dve", lambda e, w=w, gv=gv: e.tensor_tensor(gs[:, i, w], gs[:, i, w], gv, ALU.mult),
                     reads=[("gs0", i, w), "ng"], writes=[("gs", i, w)])
            yield

    def drain(g):
        for _ in g:
            pass

    def mod_piece(i, piece, kc, s):
        return modT[:, i, piece * 8 + kc, s:s + 1]

    def norm_mod(src, dst, t0, T, i, w, s, skey, dkey, final=False):
        ps, pk = next_ps()
        for kc in range(KC):
            P.op("act", lambda e, kc=kc: e.activation(SQ[:, kc, 0:T], src[:, kc, t0:t0 + T], AF.Square),
                 reads=[skey(kc)], writes=[("SQ", kc)])
        for kc in range(KC):
            P.op("pe", lambda e, kc=kc: e.matmul(ps[:, 0:T], ones_bf[:], SQ[:, kc, 0:T], start=(kc == 0), stop=(kc == KC - 1)),
                 reads=[("SQ", kc), "ones"], writes=[pk])
        NV = int(os.environ.get("NORMVAR", "9"))
        if NV < 2:
            return
        P.op("act", lambda e: e.activation(RSTD[:, 0:T], ps[:, 0:T], AF.Ln, bias=epsb[:, 0:1]), reads=[pk, "epsb"], writes=["RSTD"])
        P.op("act", lambda e: e.activation(RSTD[:, 0:T], RSTD[:, 0:T], AF.Exp, scale=-0.5), reads=["RSTD"], writes=["RSTD"])
        if NV < 3:
            return
        for kc in range(KC):
            sc = SCR[kc % 4]
            sk = ("SCR", kc % 4)
            P.op("dve", lambda e, kc=kc, sc=sc: e.tensor_tensor(sc[:, 0:T], src[:, kc, t0:t0 + T], RSTD[:, 0:T], ALU.mult),
                 reads=[skey(kc), "RSTD"], writes=[sk])
            if NV < 4:
                continue
            if final:
                P.op("dve", lambda e, kc=kc, sc=sc: e.tensor_scalar(dst[:, kc, 0:T], sc[:, 0:T], fng[:, kc:kc + 1], None, ALU.mult),
                     reads=[sk, "fng"], writes=[dkey(kc)])
            else:
                P.op("dve", lambda e, kc=kc, sc=sc: e.tensor_scalar(
                    dst[:, kc, t0:t0 + T], sc[:, 0:T], gs[:, i, w, kc, s:s + 1], mod_piece(i, 3 * w, kc, s), ALU.mult, ALU.add),
                    reads=[sk, ("gs", i, w), ("modT", i, 3 * w)], writes=[dkey(kc)])

    epsb = nc.alloc_sbuf_tensor("s_epsb", [128, 1], F32)
    P.op("dve", lambda e: e.memset(epsb[:], D * EPS), writes=["epsb"])
    fng = nc.alloc_sbuf_tensor("s_fng", [128, KC], F32)
    P.op("dve", lambda e: e.tensor_scalar(fng[:], ng[:, 4, :], 32.0, None, ALU.mult), reads=["ng"], writes=["fng"])

    def gated_residual(ps, pk, oc, h, i, piece):
        s = h
        P.op("dve", lambda e: e.scalar_tensor_tensor(
            X[:, oc, h * 512:(h + 1) * 512], ps[:, :], mod_piece(i, piece, oc, s), X[:, oc, h * 512:(h + 1) * 512],
            ALU.mult, ALU.add), reads=[pk, ("modT", i, piece), ("X", oc, h)], writes=[("X", oc, h)])

    def xkey_h(h):
        return lambda kc: ("X", kc, h)

    def hkey_h(h):
        return lambda kc: ("H", kc, h)

    def mlp(i, gen=None):
        for h in range(2):
            norm_mod(X, H, h * 512, 512, i, 1, h, xkey_h(h), hkey_h(h))
        w1v = w1_d.ap()[i].rearrange("(k p) n -> p k n", p=128)
        for hg in range(8):
            if gen is not None:
                next(gen, None)
            wv, wk = load_w(w1v[:, :, hg * 512:(hg + 1) * 512], "p (k n) -> p k n", k=KC)
            for cc in range(4):
                hc = hg * 4 + cc
                for h in range(2):
                    ps, pk = next_ps()
                    for kc in range(KC):
                        P.op("pe", lambda e, kc=kc, cc=cc, h=h, wv=wv, ps=ps: e.matmul(
                            ps[:, :], wv[:, kc, cc * 128:(cc + 1) * 128], H[:, kc, h * 512:(h + 1) * 512],
                            start=(kc == 0), stop=(kc == KC - 1)), reads=[wk, ("H", kc, h)], writes=[pk])
                    r = RELU[(hc * 2 + h) % 2]
                    rk = ("RELU", (hc * 2 + h) % 2)
                    P.op("act", lambda e, ps=ps, r=r: e.activation(r[:], ps[:, :], AF.Relu), reads=[pk], writes=[rk])
                    P.op("dve", lambda e, r=r, hc=hc, h=h: e.tensor_tensor(HID[:, hc, h * 512:(h + 1) * 512], r[:], r[:], ALU.mult),
                         reads=[rk], writes=[("HID", hc, h)])
        w2v = w2_d.ap()[i].rearrange("(k p) n -> p k n", p=128)
        for oc in range(KC):
            if gen is not None:
                next(gen, None)
            wv, wk = load_w(w2v[:, :, oc * 128:(oc + 1) * 128], "p (k n) -> p k n", k=32)
            for h in range(2):
                ps, pk = next_ps()
                for hc in range(32):
                    P.op("pe", lambda e, hc=hc, h=h, wv=wv, ps=ps: e.matmul(
                        ps[:, :], wv[:, hc, :], HID[:, hc, h * 512:(h + 1) * 512],
                        start=(hc == 0), stop=(hc == 31)), reads=[wk, ("HID", hc, h)], writes=[pk])
                gated_residual(ps, pk, oc, h, i, 5)

    def finish():
        for h in range(2):
            P.op("sp", lambda e, h=h: e.dma_start(out=y_d.ap()[:, :, h * 512:(h + 1) * 512], in_=X[:, :, h * 512:(h + 1) * 512]),
                 reads=[("X", kc, h) for kc in range(KC)], writes=[("y_out", h)], dma=True)
        return P.emit()

    ps_mod[0] = 6
    g0 = modulation_gen(0)
    for _ in range(4):
        next(g0)
    if stage == 1:
        return finish()
    for h in range(2):
        norm_mod(X, H, h * 512, 512, 0, 0, h, xkey_h(h), hkey_h(h))
    norm_mod(XH, HH, 0, 256, 0, 0, 1, lambda kc: "XH", lambda kc: ("HH", kc))

    if stage == 2:
        return finish()

    def proj_fm(wd, col0, jobs):
        wv, wk = load_w(wd.ap().rearrange("(k p) n -> p k n", p=128)[:, :, col0:col0 + 128], "p (k n) -> p k n", k=KC)
        for (ps, pk, off, n, rf, rkeys) in jobs:
            for kc in range(KC):
                P.op("pe", lambda e, kc=kc, off=off, n=n, rf=rf, ps=ps: e.matmul(
                    ps[:, off:off + n], wv[:, kc, :], rf(kc), start=(kc == 0), stop=(kc == KC - 1)),
                    reads=[wk, rkeys(kc)], writes=[pk])

    rhs_p = (0, 512, lambda kc: H[:, kc, 0:512], lambda kc: ("H", kc, 0))
    rhs_s = (0, 512, lambda kc: H[:, kc, 512:1024], lambda kc: ("H", kc, 1))
    rhs_hl = (0, 256, lambda kc: HH[:, kc, 0:256], lambda kc: ("HH", kc))

    def rope_evac(ps1, pk1, ps2, pk2, n, tab0, dst, dkey):
        a, ak = SCR[0], ("SCR", 0)
        b, bk = SCR[1], ("SCR", 1)
        P.op("dve", lambda e: e.tensor_tensor(a[:, 0:n], ps1[:, 0:n], rope[:, 0, tab0:tab0 + n], ALU.mult),
             reads=[pk1, "rope"], writes=[ak])
        P.op("dve", lambda e: e.tensor_tensor(b[:, 0:n], ps2[:, 0:n], rope[:, 1, tab0:tab0 + n], ALU.mult),
             reads=[pk2, "rope"], writes=[bk])
        P.op("dve", lambda e: e.tensor_tensor(dst, a[:, 0:n], b[:, 0:n], ALU.add), reads=[ak, bk], writes=[dkey])

    for c in range(8):
        ps, pk = next_ps()
        ps1, pk1 = next_ps()
        proj_fm(wq_d, c * 128, [(ps, pk) + rhs_p, (ps1, pk1) + rhs_s])
        P.op("act", lambda e, ps=ps, c=c: e.activation(QT[:, c, 0:512], ps[:, :], AF.Copy), reads=[pk], writes=[("QT", c, 0)])
        ps2, pk2 = next_ps()
        proj_fm(wqp_d, c * 128, [(ps2, pk2) + rhs_s])
        rope_evac(ps1, pk1, ps2, pk2, 512, 128, QT[:, c, 512:1024], ("QT", c, 1))
    for kv in range(4):
        ps, pk = next_ps()
        ps1, pk1 = next_ps()
        ps3, pk3 = next_ps()
        proj_fm(wkd_d, kv * 128, [(ps, pk) + rhs_p, (ps1, pk1) + rhs_s, (ps3, pk3) + rhs_hl])
        P.op("act", lambda e, ps=ps, kv=kv: e.activation(KT[:, kv, 0:512], ps[:, :], AF.Copy), reads=[pk], writes=[("KT", kv, 0)])
        ps2, pk2 = next_ps()
        ps4, pk4 = next_ps()
        proj_fm(wkdp_d, kv * 128, [(ps2, pk2) + rhs_s, (ps4, pk4) + rhs_hl])
        rope_evac(ps1, pk1, ps2, pk2, 512, 128, KT[:, kv, 640:1152], ("KT", kv, 1))
        ps1, pk1, ps2, pk2 = ps3, pk3, ps4, pk4
        rope_evac(ps1, pk1, ps2, pk2, 128, 0, KT[:, kv, 512:640], ("KT", kv, 2))
        a, ak = SCR[2], ("SCR", 2)
        b, bk = SCR[3], ("SCR", 3)
        P.op("dve", lambda e, ps1=ps1, a=a: e.tensor_tensor(a[:, 0:128], ps1[:, 128:256], rope[:, 0, 640:768], ALU.mult),
             reads=[pk1, "rope"], writes=[ak])
        P.op("dve", lambda e, ps2=ps2, b=b: e.tensor_tensor(b[:, 0:128], ps2[:, 128:256], rope[:, 1, 640:768], ALU.mult),
             reads=[pk2, "rope"], writes=[bk])
        P.op("dve", lambda e, kv=kv, a=a, b=b: e.tensor_tensor(KT[:, kv, 1152:1280], a[:, 0:128], b[:, 0:128], ALU.add),
             reads=[ak, bk], writes=[("KT", kv, 3)])
    if stage == 3:
        return finish()
    S4 = int(os.environ.get("S4VAR", "0"))
    for kb in range(0 if S4 == 1 else 2):
        for dup in range(2):
            P.op("dve", lambda e, kb=kb, dup=dup: e.tensor_copy(
                CKD[:, kb, :, dup, :], CKV[:, kb, 0, :].rearrange("p (k d) -> p k d", k=4)),
                reads=["CK"], writes=[("CKD", kb, dup)])
        for kv in range(4):
            ps, pk = next_ps()
            P.op("pe", lambda e, kb=kb, kv=kv, ps=ps: e.matmul(
                ps[:, 0:128], CKD[:, kb, kv].rearrange("p a d -> p (a d)"), ident_bf[:], start=True, stop=True),
                reads=[("CKD", kb, 0), ("CKD", kb, 1), "ident_bf"], writes=[pk])
            P.op("act", lambda e, kb=kb, kv=kv, ps=ps: e.activation(KT[:, kv, 1280 + kb * 128:1408 + kb * 128], ps[:, 0:128], AF.Copy),
                 reads=[pk], writes=[("KT", kv, 4 + kb)])
    wkv_v, wkv_k = load_w(wkv_d.ap().rearrange("(k p) n -> p k n", p=128), "p (k n) -> p k n", k=KC)
    vblocks = [(b, (lambda kc, b=b: H[:, kc, b * 128:(b + 1) * 128]), (lambda kc, b=b: ("H", kc, b // 4)), b if b < 4 else b + 1)
               for b in range(8)]
    vblocks += [(8, (lambda kc: HH[:, kc, 0:128]), (lambda kc: ("HH", kc)), 4),
                (9, (lambda kc: HH[:, kc, 128:256]), (lambda kc: ("HH", kc)), 9)]
    for (b, lf, lk, vb) in (vblocks if S4 != 2 else []):
        ps, pk = next_ps()
        for kc in range(KC):
            P.op("pe", lambda e, kc=kc, lf=lf, ps=ps: e.matmul(ps[:, :], lf(kc), wkv_v[:, kc, :], start=(kc == 0), stop=(kc == KC - 1)),
                 reads=[wkv_k, lk(kc)], writes=[pk])
        for dup in range(2):
            eng = "dve" if dup == 0 else "pool"
            if eng == "pool":
                continue
        for dup in range(2):
            P.op("act", lambda e, vb=vb, ps=ps, dup=dup: e.activation(
                VA[:, vb, :, dup * 64:(dup + 1) * 64], ps[:, 256:512].rearrange("p (k d) -> p k d", k=4), AF.Copy),
                reads=[pk], writes=[("VA", vb, dup)])
        if b < 4 and S4 != 3:
            P.op("act", lambda e, b=b, ps=ps: e.activation(KVO[:, b % 2, :], ps[:, :], AF.Copy), reads=[pk], writes=[("KVO", b % 2)])
            P.op("sp", lambda e, b=b: e.dma_start(out=nk_d.ap()[b * 128:(b + 1) * 128, :], in_=KVO[:, b % 2, 0:256]),
                 reads=[("KVO", b % 2)], writes=[("nk_out", b)], dma=True)
            P.op("sp", lambda e, b=b: e.dma_start(out=nv_d.ap()[b * 128:(b + 1) * 128, :], in_=KVO[:, b % 2, 256:512]),
                 reads=[("KVO", b % 2)], writes=[("nv_out", b)], dma=True)
    for kb in range(2):
        for dup in range(2):
            P.op("dve", lambda e, kb=kb, dup=dup: e.tensor_copy(
                VA[:, 10 + kb, :, dup * 64:(dup + 1) * 64], CKV[:, kb, 1, :].rearrange("p (k d) -> p k d", k=4)),
                reads=["CV"], writes=[("VA", 10 + kb, dup)])
    if stage == 4:
        return finish()
    def attention(qtok, half, keyblocks, pbuf):
        for kv in range(4):
            pt = PT[kv % 2]
            pbuf = kv % 2
            nkb = len(keyblocks)
            for j, (ktcol, ktk, vb, mside, qb) in enumerate(keyblocks):
                for hf in range(2):
                    ps, pk = next_ps()
                    for c2 in range(2):
                        P.op("pe", lambda e, hf=hf, c2=c2, kv=kv, ktcol=ktcol, ps=ps: e.matmul(
                            ps[:, c2 * 128:c2 * 128 + 128],
                            KT[hf * 64:(hf + 1) * 64, kv, ktcol:ktcol + 128],
                            QT[hf * 64:(hf + 1) * 64, 2 * kv + c2, qtok:qtok + 128], start=True, stop=True),
                            reads=[("KT", kv, ktk), ("QT", 2 * kv, half), ("QT", 2 * kv + 1, half)], writes=[pk])
                    P.op("act", lambda e, j=j, hf=hf, ps=ps, pt=pt: e.activation(pt[:, j, hf * 256:(hf + 1) * 256], ps[:, 0:256], AF.Exp, scale=0.125),
                         reads=[pk], writes=[("PT", pbuf, j, hf)])
                if mside is not None:
                    for a4 in range(4):
                        P.op("dve", lambda e, j=j, a4=a4, pt=pt, qb=qb, mside=mside: e.tensor_tensor(
                            pt[:, j, a4 * 128:(a4 + 1) * 128], pt[:, j, a4 * 128:(a4 + 1) * 128], amask[:, qb, mside, :], ALU.mult),
                            reads=[("PT", pbuf, j, a4 // 2), "amask"], writes=[("PT", pbuf, j, a4 // 2)])
            pso, pko = next_ps()
            psl, pkl = next_ps()
            for j, (ktcol, ktk, vb, mside, qb) in enumerate(keyblocks):
                P.op("pe", lambda e, j=j, vb=vb, kv=kv, pso=pso, pt=pt, nkb=nkb: e.matmul(pso[:, :], VA[:, vb, kv, :], pt[:, j, :], start=(j == 0), stop=(j == nkb - 1)),
                     reads=[("VA", vb, 0), ("VA", vb, 1), ("PT", pbuf, j, 0), ("PT", pbuf, j, 1)], writes=[pko])
            for j in range(nkb):
                P.op("pe", lambda e, j=j, psl=psl, pt=pt, nkb=nkb: e.matmul(psl[:, :], ones_bf[:], pt[:, j, :], start=(j == 0), stop=(j == nkb - 1)),
                     reads=["ones", ("PT", pbuf, j, 0), ("PT", pbuf, j, 1)], writes=[pkl])
            den = DEN[kv % 2]
            dk = ("DEN", kv % 2)
            for hf in range(2):
                for c2 in range(2):
                    hd = 4 * kv + 2 * c2 + hf
                    col = hf * 256 + c2 * 128
                    P.op("dve", lambda e, hd=hd, col=col, den=den, psl=psl: e.tensor_scalar(
                        den[:, col:col + 128], psl[:, col:col + 128], esink[:, hd:hd + 1], None, ALU.add),
                        reads=[pkl, "esink"], writes=[dk])
            P.op("dve", lambda e, den=den: e.reciprocal(den[:, :], den[:, :]), reads=[dk], writes=[dk])
            for hf in range(2):
                for c2 in range(2):
                    col = hf * 256 + c2 * 128
                    P.op("dve", lambda e, hf=hf, c2=c2, col=col, kv=kv, den=den, pso=pso: e.tensor_tensor(
                        ATT[hf * 64:(hf + 1) * 64, 2 * kv + c2, qtok:qtok + 128],
                        pso[hf * 64:(hf + 1) * 64, col:col + 128],
                        den[hf * 64:(hf + 1) * 64, col:col + 128], ALU.mult),
                        reads=[pko, dk], writes=[("ATT", 2 * kv, half), ("ATT", 2 * kv + 1, half)])

    pb = 0
    for s in range(2):
        for qb in range(2):
            kbs = [(s * 256 + j * 128, 0, 2 * s + j, None, 0) for j in range(2)]
            attention(s * 256 + qb * 128, 0, kbs, pb % 2)
            next(g0, None)
            pb += 1
    for qb in range(4):
        kbs = []
        for j in range(3):
            eb = qb + j
            ktk = 2 if eb == 0 else (3 if eb == 5 else 1)
            kbs.append((512 + eb * 128, ktk, 4 + eb, (0 if j == 0 else (1 if j == 2 else None)), qb))
        for kb in range(2):
            kbs.append((1280 + kb * 128, 4 + kb, 10 + kb, None, qb))
        attention(512 + qb * 128, 1, kbs, pb % 2)
        next(g0, None)
        pb += 1

    if stage == 5:
        return finish()
    drain(g0)
    for oc in range(KC):
        wv, wk = load_w(wo_d.ap().rearrange("(k p) n -> p k n", p=128)[:, :, oc * 128:(oc + 1) * 128], "p (k n) -> p k n", k=KC)
        for h in range(2):
            ps, pk = next_ps()
            for kc in range(KC):
                P.op("pe", lambda e, kc=kc, h=h, wv=wv, ps=ps: e.matmul(
                    ps[:, :], wv[:, kc, :], ATT[:, kc, h * 512:(h + 1) * 512], start=(kc == 0), stop=(kc == KC - 1)),
                    reads=[wk, ("ATT", kc, h)], writes=[pk])
            gated_residual(ps, pk, oc, h, 0, 2)
    if stage == 6:
        return finish()
    g1 = modulation_gen(1)
    mlp(0, g1)
    drain(g1)

    if stage == 7:
        return finish()
    P.fence()
    ssm_sc_d = din("ssm_sc", [128, 4, 64])
    s0_d = din("s0", [128, 2, 64])
    bpad_d = din("bpad", [8, 128, 2048])
    cpad_d = din("cpad", [8, 128, 2048])
    dskip_d = din("dskip", [128, KC])
    qsel_d = din("qsel", [128, 4])
    wa_d = din("glu_w_a", [D, D])
    wb_d = din("glu_w_b", [D, D])
    news_d = dout("new_s", [128, 2, 64, 2])
    g_in = nc.dram_tensor("g_in", [128, 128], F32)
    g_out = nc.dram_tensor("g_out", [512, 128], F32)
    tab_d = nc.dram_tensor("tab_scratch", [8, 128, 8192], F32)

    NTAB = 40
    TB = L0B[:, 0:2560].rearrange("p (a m) -> p a m", a=NTAB)
    CF = L0B[:, 2560:3584].bitcast(BF16)
    L0ROW = 3584

    def tb_ap(off, dims):
        return bass.AP(tensor=L0B, offset=off, ap=[[L0ROW, 128]] + dims)

    P0F = PT[0][:].rearrange("p a b -> p (a b)").bitcast(F32)
    P1F = PT[1][:].rearrange("p a b -> p (a b)").bitcast(F32)
    S0 = P0F[:, 0:128].rearrange("p (a m) -> p a m", a=2)
    FIN1 = P0F[:, 128:256].rearrange("p (a m) -> p a m", a=2)
    FIN2 = P0F[:, 256:512].rearrange("p (a m s) -> p a m s", a=2, s=2)
    NS = P0F[:, 512:768].rearrange("p (a m s) -> p a m s", a=2, s=2)
    TF = P0F[:, 768:896].rearrange("p (a m) -> p a m", a=2)
    ST = P0F[:, 896:1024].rearrange("p (a m) -> p a m", a=2)
    WI = P0F[:, 1024:1152].rearrange("p (a m) -> p a m", a=2)
    GG = P1F[:, 0:512].rearrange("p (q a m) -> p q a m", q=4, a=2)
    CH = P1F[:, 512:1024].rearrange("p (q a m) -> p q a m", q=4, a=2)
    NT1 = P1F[:, 1024:1152].rearrange("p (m s) -> p m s", s=2)
    NT2 = P1F[:, 1152:1280].rearrange("p (m s) -> p m s", s=2)
    hpi = nc.alloc_sbuf_tensor("s_hpi", [128, 1], F32)
    dskip = nc.alloc_sbuf_tensor("s_dskip", [128, KC], F32)
    qsel = nc.alloc_sbuf_tensor("s_qsel", [128, 4], F32)
    FTMP = nc.alloc_sbuf_tensor("s_FTMP", [128, 4], F32)
    fence_keys = [("X", kc, h) for kc in range(KC) for h in range(2)]
    (T_LR, T_TH, T_DT, T_R, T_R512, T_FRE, T_FIM, T_IFRE, T_IFIM, T_P1RE, T_P1IM, T_T1, T_T2, T_T3, T_T4) = range(15)
    T_CKC, T_CKS, T_LAMRE, T_LAMIM, T_LOGDT = 15, 25, 35, 36, 37

    def tb(i):
        return TB[:, i, :]

    tbk = lambda i: ("TB", i)

    def vop(eng, fn, reads, writes):
        P.op(eng, fn, reads=reads, writes=writes)

    def tt(eng, out, a, b, op, rk, wk):
        vop(eng, lambda e: e.tensor_tensor(out, a, b, op), rk, wk)

    P.op("sp", lambda e: e.dma_start(out=TB[:, T_LAMRE:T_LAMRE + 4, :], in_=ssm_sc_d.ap()), reads=fence_keys, writes=["ssm_sc"], dma=True)
    for i in range(T_LAMRE, T_LAMRE + 4):
        P.last_write[tbk(i)] = P.last_write["ssm_sc"]
    P.op("sp", lambda e: e.dma_start(out=S0, in_=s0_d.ap()), reads=fence_keys, writes=["S0"], dma=True)
    dma_in(dskip[:], dskip_d.ap(), "dskip")
    dma_in(qsel[:], qsel_d.ap(), "qsel")
    P.op("dve", lambda e: e.memset(hpi[:], math.pi / 2), writes=["hpi"])
    P.op("act", lambda e: e.activation(tb(T_DT), tb(T_LOGDT), AF.Exp), reads=[tbk(T_LOGDT)], writes=[tbk(T_DT)])
    tt("dve", tb(T_LR), tb(T_LAMRE), tb(T_DT), ALU.mult, [tbk(T_LAMRE), tbk(T_DT)], [tbk(T_LR)])
    tt("dve", tb(T_TH), tb(T_LAMIM), tb(T_DT), ALU.mult, [tbk(T_LAMIM), tbk(T_DT)], [tbk(T_TH)])
    P.op("act", lambda e: e.activation(tb(T_R), tb(T_LR), AF.Exp), reads=[tbk(T_LR)], writes=[tbk(T_R)])
    P.op("act", lambda e: e.activation(tb(T_R512), tb(T_LR), AF.Exp, scale=512.0), reads=[tbk(T_LR)], writes=[tbk(T_R512)])
    P.op("act", lambda e: e.activation(tb(T_CKS), tb(T_TH), AF.Sin, scale=1.0 / 64), reads=[tbk(T_TH)], writes=[tbk(T_CKS)])
    P.op("act", lambda e: e.activation(tb(T_CKC), tb(T_TH), AF.Sin, scale=1.0 / 64, bias=hpi[:, 0:1]), reads=[tbk(T_TH), "hpi"], writes=[tbk(T_CKC)])

    def square(ci, si, co, so):
        tt("dve", tb(T_T1), tb(ci), tb(ci), ALU.mult, [tbk(ci)], [tbk(T_T1)])
        tt("dve", tb(T_T2), tb(si), tb(si), ALU.mult, [tbk(si)], [tbk(T_T2)])
        vop("dve", lambda e: e.scalar_tensor_tensor(tb(so), tb(ci), 2.0, tb(si), ALU.mult, ALU.mult), [tbk(ci), tbk(si)], [tbk(so)])
        tt("dve", tb(co), tb(T_T1), tb(T_T2), ALU.subtract, [tbk(T_T1), tbk(T_T2)], [tbk(co)])

    for _ in range(6):
        square(T_CKC, T_CKS, T_CKC, T_CKS)
    for k in range(9):
        square(T_CKC + k, T_CKS + k, T_CKC + k + 1, T_CKS + k + 1)

    def cmul(eng, ore, oim, are, aim, bre, bim, rk, wk, t1, t2, tk1, tk2):
        tt(eng, t1, are, bre, ALU.mult, rk, [tk1])
        tt(eng, t2, aim, bim, ALU.mult, rk, [tk2])
        tt(eng, ore, t1, t2, ALU.subtract, [tk1, tk2], [wk[0]])
        tt(eng, t1, are, bim, ALU.mult, rk, [tk1])
        tt(eng, t2, aim, bre, ALU.mult, rk, [tk2])
        tt(eng, oim, t1, t2, ALU.add, [tk1, tk2], [wk[1]])

    tt("dve", tb(T_T3), tb(T_R), tb(T_CKC), ALU.mult, [tbk(T_R), tbk(T_CKC)], [tbk(T_T3)])
    tt("dve", tb(T_T4), tb(T_R), tb(T_CKS), ALU.mult, [tbk(T_R), tbk(T_CKS)], [tbk(T_T4)])
    vop("dve", lambda e: e.tensor_scalar(tb(T_T3), tb(T_T3), -1.0, None, ALU.add), [tbk(T_T3)], [tbk(T_T3)])
    tt("dve", tb(T_T1), tb(T_LAMRE), tb(T_LAMRE), ALU.mult, [tbk(T_LAMRE)], [tbk(T_T1)])
    tt("dve", tb(T_T2), tb(T_LAMIM), tb(T_LAMIM), ALU.mult, [tbk(T_LAMIM)], [tbk(T_T2)])
    tt("dve", tb(T_T1), tb(T_T1), tb(T_T2), ALU.add, [tbk(T_T1), tbk(T_T2)], [tbk(T_T1)])
    vop("dve", lambda e: e.reciprocal(tb(T_T1), tb(T_T1)), [tbk(T_T1)], [tbk(T_T1)])
    tt("dve", tb(T_FRE), tb(T_T3), tb(T_LAMRE), ALU.mult, [tbk(T_T3), tbk(T_LAMRE)], [tbk(T_FRE)])
    tt("dve", tb(T_T2), tb(T_T4), tb(T_LAMIM), ALU.mult, [tbk(T_T4), tbk(T_LAMIM)], [tbk(T_T2)])
    tt("dve", tb(T_FRE), tb(T_FRE), tb(T_T2), ALU.add, [tbk(T_FRE), tbk(T_T2)], [tbk(T_FRE)])
    tt("dve", tb(T_FRE), tb(T_FRE), tb(T_T1), ALU.mult, [tbk(T_FRE), tbk(T_T1)], [tbk(T_FRE)])
    tt("dve", tb(T_FIM), tb(T_T4), tb(T_LAMRE), ALU.mult, [tbk(T_T4), tbk(T_LAMRE)], [tbk(T_FIM)])
    tt("dve", tb(T_T2), tb(T_T3), tb(T_LAMIM), ALU.mult, [tbk(T_T3), tbk(T_LAMIM)], [tbk(T_T2)])
    tt("dve", tb(T_FIM), tb(T_FIM), tb(T_T2), ALU.subtract, [tbk(T_FIM), tbk(T_T2)], [tbk(T_FIM)])
    tt("dve", tb(T_FIM), tb(T_FIM), tb(T_T1), ALU.mult, [tbk(T_FIM), tbk(T_T1)], [tbk(T_FIM)])
    tt("dve", tb(T_T1), tb(T_FRE), tb(T_FRE), ALU.mult, [tbk(T_FRE)], [tbk(T_T1)])
    tt("dve", tb(T_T2), tb(T_FIM), tb(T_FIM), ALU.mult, [tbk(T_FIM)], [tbk(T_T2)])
    tt("dve", tb(T_T1), tb(T_T1), tb(T_T2), ALU.add, [tbk(T_T1), tbk(T_T2)], [tbk(T_T1)])
    vop("dve", lambda e: e.reciprocal(tb(T_T1), tb(T_T1)), [tbk(T_T1)], [tbk(T_T1)])
    tt("dve", tb(T_IFRE), tb(T_FRE), tb(T_T1), ALU.mult, [tbk(T_FRE), tbk(T_T1)], [tbk(T_IFRE)])
    vop("dve", lambda e: e.scalar_tensor_tensor(tb(T_IFIM), tb(T_FIM), -1.0, tb(T_T1), ALU.mult, ALU.mult), [tbk(T_FIM), tbk(T_T1)], [tbk(T_IFIM)])
    tt("dve", tb(T_P1RE), tb(T_R512), tb(T_CKC + 9), ALU.mult, [tbk(T_R512), tbk(T_CKC + 9)], [tbk(T_P1RE)])
    tt("dve", tb(T_P1IM), tb(T_R512), tb(T_CKS + 9), ALU.mult, [tbk(T_R512), tbk(T_CKS + 9)], [tbk(T_P1IM)])

    for h in range(2):
        norm_mod(X, H, h * 512, 512, 1, 0, h, xkey_h(h), hkey_h(h))

    BF = BIG[:].bitcast(F32)
    UC = BF[:, 0:4096].rearrange("p (a t) -> p a t", a=8)
    US = BF[:, 4096:8192].rearrange("p (a t) -> p a t", a=8)
    EW = [BF[:, 8192 + i * 512:8192 + (i + 1) * 512] for i in range(4)]
    UT = [BF[:, 8192:10240].rearrange("p (a t) -> p a t", a=8), BF[:, 10240:12288].rearrange("p (a t) -> p a t", a=8)]
    SBS = [BIG[:, 24576:25600].rearrange("p (a t) -> p a t", a=2), BIG[:, 29696:30720].rearrange("p (a t) -> p a t", a=2)]
    CFT1 = BF[:, 15360:15488]
    CFT2 = BF[:, 15488:15616]
    sb_i = [0]
    G0 = XH[:].bitcast(BF16)
    G1 = BIG[:, 25600:29696].rearrange("p (c t) -> p c t", c=8)

    def Gv(kc, h):
        return (G0 if h == 0 else G1)[:, kc, :]

    def build_tables(gh):
        P.op("dve", lambda e: e.memset(UC[:, :, 0:1], 1.0), reads=fence_keys, writes=["UC"])
        P.op("dve", lambda e: e.memset(US[:, :, 0:1], 0.0), reads=fence_keys, writes=["US"])
        for k in range(9):
            n = 1 << k
            ckc = tb_ap((T_CKC + k) * 64 + 4 * gh, [[32, 2], [1, 4], [0, n]])
            cks = tb_ap((T_CKS + k) * 64 + 4 * gh, [[32, 2], [1, 4], [0, n]])
            v4 = lambda ap: ap.rearrange("p (d m) t -> p d m t", d=2)
            sc_, ss_ = v4(UC[:, :, 0:n]), v4(US[:, :, 0:n])
            dc_, ds_ = v4(UC[:, :, n:2 * n]), v4(US[:, :, n:2 * n])
            t1, t2 = v4(UT[0][:, :, 0:n]), v4(UT[1][:, :, 0:n])
            rk = ["UC", "US", tbk(T_CKC + k), tbk(T_CKS + k)]
            tt("dve", t1, sc_, ckc, ALU.mult, rk, ["UT0"])
            tt("dve", t2, ss_, cks, ALU.mult, rk, ["UT1"])
            tt("dve", dc_, t1, t2, ALU.subtract, ["UT0", "UT1"], ["UC"])
            tt("dve", t1, sc_, cks, ALU.mult, rk, ["UT0"])
            tt("dve", t2, ss_, ckc, ALU.mult, rk, ["UT1"])
            tt("dve", ds_, t1, t2, ALU.add, ["UT0", "UT1"], ["US"])

    def seg_ap(ap512, lo, n, rev):
        v = ap512[:, lo:lo + n]
        return v[:, ::-1] if rev else v

    RS = [BF[:, 10240 + i * 512:10240 + (i + 1) * 512] for i in range(4)]

    def hv(ap512, half):
        return ap512.rearrange("p (s t) -> p s t", s=2) if half == 0 else ap512

    def tabv(T, a, half, d):
        n = 256 if half == 0 else 512
        v = T[:, a, 0:n]
        if d == 1:
            v = v[:, ::-1]
        return v.unsqueeze(1).broadcast_to([128, 2, 256]) if half == 0 else v

    def demod(a, md, half, ps_re, pk_re, ps_im, pk_im, d):
        uc, us = tabv(UC, a, half, d), tabv(US, a, half, d)
        xr, xi = hv(ps_re[:, :], half), hv(ps_im[:, :], half)
        t = [hv(SCR[i][:, :], half) for i in range(4)]
        rk = [pk_re, pk_im, "UC", "US"]
        tt("dve", t[0], xr, uc, ALU.mult, rk, [("SCR", 0)])
        tt("dve", t[1], xi, us, ALU.mult, rk, [("SCR", 1)])
        tt("dve", t[2], xi, uc, ALU.mult, rk, [("SCR", 2)])
        tt("dve", t[3], xr, us, ALU.mult, rk, [("SCR", 3)])
        tt("dve", hv(EW[0], half), t[0], t[1], ALU.add, [("SCR", 0), ("SCR", 1)], [("EW", 0)])
        tt("dve", hv(EW[1], half), t[2], t[3], ALU.subtract, [("SCR", 2), ("SCR", 3)], [("EW", 1)])

    def scans(md, half, d, use_init):
        segs = [(0, 256), (256, 256)] if half == 0 else [(0, 512)]
        for (lo, n) in segs:
            rbc = tb_ap(T_R * 64 + md, [[0, n]])
            for ri in range(2):
                init = WI[:, ri, md:md + 1] if use_init else 0.0
                src = seg_ap(EW[ri], lo, n, d == 1)
                dst = seg_ap(EW[2 + ri], lo, n, d == 1)
                vop("dve", lambda e, dst=dst, rbc=rbc, src=src, init=init: e.tensor_tensor_scan(dst, rbc, src, init, ALU.mult, ALU.add),
                    [("EW", ri), tbk(T_R)] + (["WI"] if use_init else []), [("EW", 2 + ri)])

    def finals(a, md, half, d, dst):
        L = 256 if half == 0 else 512
        ncol = 2 if half == 0 else 1
        c0 = (L - 1) if d == 0 else 0
        wre = EW[2][:, c0::256][:, 0:ncol] if half == 0 else EW[2][:, c0:c0 + 1]
        wim = EW[3][:, c0::256][:, 0:ncol] if half == 0 else EW[3][:, c0:c0 + 1]
        ucl, usl = UC[:, a, L - 1:L], US[:, a, L - 1:L]
        tA, tB = FTMP[:, 0:ncol], FTMP[:, 2:2 + ncol]
        rk = [("EW", 2), ("EW", 3), "UC", "US"]
        vop("dve", lambda e: e.tensor_scalar(tA, wim, usl, None, ALU.mult), rk, ["FTMPA"])
        vop("dve", lambda e: e.tensor_scalar(tB, wre, usl, None, ALU.mult), rk, ["FTMPB"])
        vop("dve", lambda e: e.scalar_tensor_tensor(dst[0], wre, ucl, tA, ALU.mult, ALU.subtract), rk + ["FTMPA"], [dst[2]])
        vop("dve", lambda e: e.scalar_tensor_tensor(dst[1], wim, ucl, tB, ALU.mult, ALU.add), rk + ["FTMPB"], [dst[3]])

    def remod(a, half, d, SB, sbi):
        uc, us = tabv(UC, a, half, d), tabv(US, a, half, d)
        wr, wi = hv(EW[2], half), hv(EW[3], half)
        t = [hv(RS[i], half) for i in range(4)]
        rk = [("EW", 2), ("EW", 3), "UC", "US"]
        tt("dve", t[0], wr, uc, ALU.mult, rk, [("RS", 0)])
        tt("dve", t[1], wi, us, ALU.mult, rk, [("RS", 1)])
        tt("dve", t[2], wi, uc, ALU.mult, rk, [("RS", 2)])
        tt("dve", t[3], wr, us, ALU.mult, rk, [("RS", 3)])
        tt("dve", hv(SB[:, 0, :], half), t[0], t[1], ALU.subtract, [("RS", 0), ("RS", 1)], [("SB", sbi, 0)])
        vop("dve", lambda e: e.scalar_tensor_tensor(hv(SB[:, 1, :], half), t[2], -1.0, t[3], ALU.mult, ALU.subtract),
            [("RS", 2), ("RS", 3)], [("SB", sbi, 1)])

    XSF = SQ[:].rearrange("p a b -> p (a b)").bitcast(F32)
    XS = [XSF[:, i * 512:(i + 1) * 512] for i in range(4)]
    xs_i = [0]

    def x_matmuls(bv, bk, gh, d, m4, half):
        outs = []
        buf = xs_i[0] % 2
        xs_i[0] += 1
        for ri in range(2):
            ps, pk = next_ps()
            col = ((d * 2 + ri) * 4 + m4) * 128
            P.op("pe", lambda e, ps=ps, col=col: e.matmul(ps[:, :], bv[:, col:col + 128], H[:, gh, half * 512:(half + 1) * 512], start=True, stop=True),
                 reads=[bk, ("H", gh, half)], writes=[pk])
            xs, xk = XS[buf * 2 + ri], ("XS", buf * 2 + ri)
            P.op("act", lambda e, ps=ps, xs=xs: e.activation(xs, ps[:, :], AF.Copy), reads=[pk], writes=[xk])
            outs += [xs, xk]
        return outs

    ps_mod[0] = 6
    bpv = lambda gh: bpad_d.ap()[gh]
    for gh in range(8):
        build_tables(gh)
        P.op("sp", lambda e, gh=gh: e.dma_start(out=tab_d.ap()[gh], in_=BF[:, 0:8192]), reads=["UC", "US"], writes=[("tab_d", gh)], dma=True)
        bv, bk = load_w(bpv(gh))
        for d in range(2):
            for m4 in range(4):
                a = d * 4 + m4
                md = d * 32 + 4 * gh + m4
                psr, pkr, psi, pki = x_matmuls(bv, bk, gh, d, m4, 1)
                demod(a, md, 1, psr, pkr, psi, pki, d)
                scans(md, 1, d, False)
                finals(a, md, 1, d, (FIN1[:, 0, md:md + 1], FIN1[:, 1, md:md + 1], ("FIN1", md), ("FIN1", md)))
    fin1k = [("FIN1", md) for md in range(64)]
    cmul("dve", TF[:, 0, :], TF[:, 1, :], FIN1[:, 0, :], FIN1[:, 1, :], tb(T_FRE), tb(T_FIM),
         fin1k + [tbk(T_FRE), tbk(T_FIM)], ["TF", "TF"], tb(T_T1), tb(T_T2), tbk(T_T1), tbk(T_T2))
    P.op("sp", lambda e: e.dma_start(out=g_in.ap(), in_=TF.rearrange("p a b -> p (a b)")), reads=["TF"], writes=["g_in"], dma=True)
    P.op("pool", lambda e: e.collective_compute("AllGather", ALU.bypass, replica_groups=[[0, 1, 2, 3], [4, 5, 6, 7]],
                                                  ins=[g_in.ap().opt()], outs=[g_out.ap().opt()]),
         reads=["g_in"], writes=["g_out"], cc=True)
    P.op("sp", lambda e: e.dma_start(out=GG.rearrange("p q a b -> p q (a b)"), in_=g_out.ap().rearrange("(q p) n -> p q n", p=128)),
         reads=["g_out"], writes=["GG"], dma=True)
    for d in range(2):
        sl = slice(d * 32, (d + 1) * 32)
        q0 = 0 if d == 0 else 3
        for ri in range(2):
            vop("dve", lambda e, ri=ri, sl=sl, q0=q0: e.tensor_copy(CH[:, q0, ri, sl], S0[:, ri, sl]), ["S0"], [("CH", q0, d)])
        order = [(0, 1, 0), (1, 2, 1), (2, 3, 2)] if d == 0 else [(3, 2, 3), (2, 1, 2), (1, 0, 1)]
        for (qs, qd, qg) in order:
            cmul("dve", CH[:, qd, 0, sl], CH[:, qd, 1, sl], CH[:, qs, 0, sl], CH[:, qs, 1, sl], TB[:, T_P1RE, sl], TB[:, T_P1IM, sl],
                 [("CH", qs, d), tbk(T_P1RE), tbk(T_P1IM)], [("CH", qd, d), ("CH", qd, d)], TB[:, T_T1, sl], TB[:, T_T2, sl], tbk(T_T1), tbk(T_T2))
            for ri in range(2):
                tt("dve", CH[:, qd, ri, sl], CH[:, qd, ri, sl], GG[:, qg, ri, sl], ALU.add, [("CH", qd, d), "GG"], [("CH", qd, d)])
    chk = [("CH", q, d) for q in range(4) for d in range(2)]
    for ri in range(2):
        vop("dve", lambda e, ri=ri: e.tensor_scalar(ST[:, ri, :], CH[:, 0, ri, :], qsel[:, 0:1], None, ALU.mult), chk + ["qsel"], [("ST", ri)])
        for q in range(1, 4):
            vop("dve", lambda e, ri=ri, q=q: e.scalar_tensor_tensor(ST[:, ri, :], CH[:, q, ri, :], qsel[:, q:q + 1], ST[:, ri, :], ALU.mult, ALU.add),
                chk + ["qsel", ("ST", ri)], [("ST", ri)])
    cmul("dve", TF[:, 0, :], TF[:, 1, :], ST[:, 0, :], ST[:, 1, :], tb(T_IFRE), tb(T_IFIM),
         [("ST", 0), ("ST", 1), tbk(T_IFRE), tbk(T_IFIM), "g_in"], ["TF2", "TF2"], tb(T_T1), tb(T_T2), tbk(T_T1), tbk(T_T2))
    cmul("dve", WI[:, 0, :], WI[:, 1, :], TF[:, 0, :], TF[:, 1, :], tb(T_CKC), tb(T_CKS),
         ["TF2", tbk(T_CKC), tbk(T_CKS)], ["WI", "WI"], tb(T_T1), tb(T_T2), tbk(T_T1), tbk(T_T2))

    cpv = lambda gh: cpad_d.ap()[gh]
    for gh in range(8):
        P.op("sp", lambda e, gh=gh: e.dma_start(out=BF[:, 0:8192], in_=tab_d.ap()[gh]), reads=[("tab_d", gh)], writes=["UC", "US"], dma=True)
        bv, bk = load_w(bpv(gh))
        cv_, ck_ = load_w(cpv(gh))
        for d in range(2):
            for m4 in range(4):
                md = d * 32 + 4 * gh + m4
                cre = cv_[:, ((d * 2 + 0) * 4 + m4) * 128:((d * 2 + 0) * 4 + m4 + 1) * 128]
                cim = cv_[:, ((d * 2 + 1) * 4 + m4) * 128:((d * 2 + 1) * 4 + m4 + 1) * 128]
                ore = CF[:, ((d * 2 + 0) * 4 + m4) * 128:((d * 2 + 0) * 4 + m4 + 1) * 128]
                oim = CF[:, ((d * 2 + 1) * 4 + m4) * 128:((d * 2 + 1) * 4 + m4 + 1) * 128]
                fre, fim = TB[:, T_FRE, md:md + 1], TB[:, T_FIM, md:md + 1]
                rk = [ck_, tbk(T_FRE), tbk(T_FIM)]
                vop("dve", lambda e, cim=cim, fim=fim: e.tensor_scalar(CFT1, cim, fim, None, ALU.mult), rk, ["CFT1"])
                vop("dve", lambda e, ore=ore, cre=cre, fre=fre: e.scalar_tensor_tensor(ore, cre, fre, CFT1, ALU.mult, ALU.subtract),
                    rk + ["CFT1"], [("CF", d, m4, 0)])
                vop("dve", lambda e, cim=cim, fre=fre: e.tensor_scalar(CFT2, cim, fre, None, ALU.mult), rk, ["CFT2"])
                vop("dve", lambda e, oim=oim, cre=cre, fim=fim: e.scalar_tensor_tensor(oim, cre, fim, CFT2, ALU.mult, ALU.add),
                    rk + ["CFT2"], [("CF", d, m4, 1)])
        ysl = [(PS[6], ("ps", 6)), (PS[7], ("ps", 7))]
        cnt = [0, 0]
        for d in range(2):
            for m4 in range(4):
                a = d * 4 + m4
                md = d * 32 + 4 * gh + m4
                for half in range(2):
                    psr, pkr, psi, pki = x_matmuls(bv, bk, gh, d, m4, half)
                    demod(a, md, half, psr, pkr, psi, pki, d)
                    scans(md, half, d, half == 1)
                    sbi = sb_i[0] % 2
                    sb_i[0] += 1
                    SB = SBS[sbi]
                    remod(a, half, d, SB, sbi)
                    if half == 0:
                        finals(a, md, 0, d, (FIN2[:, 0, md, :], FIN2[:, 1, md, :], ("FIN2", md), ("FIN2", md)))
                    psy, pky = ysl[half]
                    for ri in range(2):
                        col = ((d * 2 + ri) * 4 + m4) * 128
                        first = cnt[half] == 0
                        last = cnt[half] == 15
                        cnt[half] += 1
                        P.op("pe", lambda e, psy=psy, col=col, ri=ri, first=first, last=last, SB=SB: e.matmul(
                            psy[:, :], CF[:, col:col + 128], SB[:, ri, :], start=first, stop=last),
                            reads=[("CF", d, m4, ri), ("SB", sbi, ri)], writes=[pky])
        for half in range(2):
            psy, pky = ysl[half]
            yp, t1, t2 = RS[0], RS[1], RS[2]
            hs = slice(half * 512, (half + 1) * 512)
            vop("dve", lambda e, psy=psy, hs=hs, gh=gh: e.scalar_tensor_tensor(yp, H[:, gh, hs], dskip[:, gh:gh + 1], psy[:, :], ALU.mult, ALU.add),
                [pky, ("H", gh, half), "dskip"], [("RS", 0)])
            tt("pool", t1, yp, yp, ALU.mult, [("RS", 0)], [("RS", 1)])
            vop("pool", lambda e: e.tensor_scalar(t1, t1, 0.044715, 1.0, ALU.mult, ALU.add), [("RS", 1)], [("RS", 1)])
            tt("pool", t1, t1, yp, ALU.mult, [("RS", 1), ("RS", 0)], [("RS", 1)])
            P.op("act", lambda e: e.activation(t2, t1, AF.Sigmoid, scale=1.5957691216057308), reads=[("RS", 1)], writes=[("RS", 2)])
            tt("pool", Gv(gh, half), yp, t2, ALU.mult, [("RS", 0), ("RS", 2)], [("G", gh, half)])
    ps_mod[0] = 8
    fin2k = [("FIN2", md) for md in range(64)]
    fre_b = tb_ap(T_FRE * 64, [[1, 64], [0, 2]])
    fim_b = tb_ap(T_FIM * 64, [[1, 64], [0, 2]])
    cmul("dve", NS[:, 0], NS[:, 1], FIN2[:, 0], FIN2[:, 1], fre_b, fim_b, fin2k + [tbk(T_FRE), tbk(T_FIM)],
         ["NS", "NS"], NT1, NT2, "NT1", "NT2")
    P.op("sp", lambda e: e.dma_start(out=news_d.ap(), in_=NS), reads=["NS"], writes=["news_out"], dma=True)

    for oc in range(KC):
        wav, wak = load_w(wa_d.ap().rearrange("(k p) n -> p k n", p=128)[:, :, oc * 128:(oc + 1) * 128], "p (k n) -> p k n", k=KC)
        wbv, wbk = load_w(wb_d.ap().rearrange("(k p) n -> p k n", p=128)[:, :, oc * 128:(oc + 1) * 128], "p (k n) -> p k n", k=KC)
        for h in range(2):
            psa, pka = next_ps()
            psb, pkb = next_ps()
            for (wv_, wk_, ps_, pk_) in ((wav, wak, psa, pka), (wbv, wbk, psb, pkb)):
                for kc in range(KC):
                    P.op("pe", lambda e, kc=kc, h=h, wv_=wv_, ps_=ps_: e.matmul(
                        ps_[:, :], wv_[:, kc, :], Gv(kc, h), start=(kc == 0), stop=(kc == KC - 1)),
                        reads=[wk_, ("G", kc, h)], writes=[pk_])
            sg, pr = SCR[0], SCR[1]
            P.op("act", lambda e, psb=psb: e.activation(sg[:], psb[:, :], AF.Sigmoid), reads=[pkb], writes=[("SCR", 0)])
            tt("dve", pr[:], psa[:, :], sg[:], ALU.mult, [pka, ("SCR", 0)], [("SCR", 1)])
            P.op("dve", lambda e, oc=oc, h=h: e.scalar_tensor_tensor(
                X[:, oc, h * 512:(h + 1) * 512], pr[:], mod_piece(1, 2, oc, h), X[:, oc, h * 512:(h + 1) * 512], ALU.mult, ALU.add),
                reads=[("SCR", 1), ("modT", 1, 2), ("X", oc, h)], writes=[("X", oc, h)])
    mlp(1)
    YF = BF[:, 0:4096].rearrange("p (c t) -> p c t", c=8)
    for h in range(2):
        norm_mod(X, YF, h * 512, 512, 1, 0, h, xkey_h(h), lambda kc: ("YF", kc), final=True)
        P.op("sp", lambda e, h=h: e.dma_start(out=y_d.ap()[:, :, h * 512:(h + 1) * 512], in_=YF[:, :, 0:512]),
             reads=[("YF", kc) for kc in range(KC)], writes=[("y_out", h)], dma=True)
    return P.emit()


def _fm(x_tok):
    T = x_tok.shape[0]
    return np.ascontiguousarray(x_tok.reshape(T, KC, 128).transpose(2, 1, 0))


def _rope_tables(q):
    pos = 512 * q - 128 + np.arange(768)
    row = (pos // 64).astype(np.float32)
    col = (pos % 64).astype(np.float32)
    freqs = (10000.0 ** (-np.arange(16, dtype=np.float32) / 16)).astype(np.float32)
    tab = np.zeros((128, 2, 768), np.float32)
    for p in range(128):
        d = p % 64
        f = freqs[d % 16]
        ang = (row if d < 32 else col) * f
        tab[p, 0] = np.cos(ang)
        tab[p, 1] = np.sin(ang) * (-1.0 if (d % 32) < 16 else 1.0)
    return tab


def _amask(q):
    m = np.zeros((128, 4, 2, 128), np.float32)
    jj = np.arange(128)[:, None]
    rr = np.arange(128)[None, :]
    for qb in range(4):
        lpos0 = 512 * q + 128 * qb - 128
        rpos0 = 512 * q + 128 * qb + 128
        m[:, qb, 0, :] = (jj >= rr) * (1.0 if lpos0 >= 0 else 0.0)
        m[:, qb, 1, :] = (jj <= rr) * (1.0 if rpos0 < 2048 else 0.0)
    return m


def _prep_shared(inp):
    f = np.float32
    sh = {}
    w_qkv = inp["w_qkv"][0]
    wq = w_qkv[:, :1024]
    wk = w_qkv[:, 1024:1280]
    wv = w_qkv[:, 1280:1536]
    d = np.arange(64)
    partner = np.where((d % 32) < 16, d + 16, d - 16)
    qperm = (np.arange(16)[:, None] * 64 + partner[None, :]).reshape(-1)
    kperm = (np.arange(4)[:, None] * 64 + partner[None, :]).reshape(-1)
    dup = (np.arange(4)[:, None, None] * 64 + np.zeros((1, 2, 1), int) + d[None, None, :]).reshape(-1)
    sh["wq"] = np.ascontiguousarray(wq)
    sh["wqp"] = np.ascontiguousarray(wq[:, qperm])
    sh["wkd"] = np.ascontiguousarray(wk[:, dup])
    sh["wkdp"] = np.ascontiguousarray(wk[:, kperm][:, dup])
    sh["wkv"] = np.ascontiguousarray(np.concatenate([wk, wv], axis=1))
    sh["w_o"] = np.ascontiguousarray(inp["w_o"][0])
    sh["w_mod"] = np.ascontiguousarray(inp["w_mod"])
    sh["mlp_w1"] = np.ascontiguousarray(inp["mlp_w1"])
    sh["mlp_w2"] = np.ascontiguousarray(inp["mlp_w2"])
    ngs = np.stack([inp["norm1_g"][0], inp["norm2_g"][0], inp["norm1_g"][1], inp["norm2_g"][1], inp["final_norm_g"]], 0)
    sh["ng"] = np.ascontiguousarray(ngs.reshape(5, KC, 128).transpose(2, 0, 1)).astype(f)
    sh["bmodT"] = np.ascontiguousarray(inp["b_mod"].reshape(2, 48, 128).transpose(2, 0, 1)).astype(f)
    sh["ident"] = np.eye(128, dtype=f)
    sh["sink"] = np.ascontiguousarray(np.broadcast_to(inp["attn_sink"][0][None, :], (128, 16))).astype(f)
    def gp_md(a):
        return a.reshape(2, 32, 2, 64).transpose(2, 3, 0, 1).reshape(128, 64)
    ldt = np.broadcast_to(inp["ssm_log_dt"][0][:, :, None], (2, 64, 64))
    sc = np.zeros((128, 4, 64), f)
    sc[:, 0] = gp_md(inp["ssm_lam_re"][0])
    sc[:, 1] = gp_md(inp["ssm_lam_im"][0])
    sc[:, 2] = gp_md(ldt)
    sh["ssm_sc"] = sc
    bp = np.zeros((8, 8, 16, 2, 2, 4, 2, 64), f)
    cp = np.zeros((8, 2, 64, 2, 2, 4, 8, 16), f)
    Bs = (inp["ssm_b_re"][0], inp["ssm_b_im"][0])
    Cs = (inp["ssm_c_re"][0], inp["ssm_c_im"][0])
    for gh in range(8):
        for m4 in range(4):
            for g2 in range(2):
                g = 8 * gh + 2 * m4 + g2
                gl = 2 * m4 + g2
                for d in range(2):
                    for ri in range(2):
                        bp[gh, gl, :, d, ri, m4, g2, :] = Bs[ri][d, g].T
                        cp[gh, g2, :, d, ri, m4, gl, :] = Cs[ri][d, g].T
    sh["bpad"] = bp.reshape(8, 128, 2048)
    sh["cpad"] = cp.reshape(8, 128, 2048)
    sh["dskip"] = np.ascontiguousarray(inp["ssm_d"][0].reshape(KC, 128).T).astype(f)
    sh["glu_w_a"] = np.ascontiguousarray(inp["glu_w_a"][0])
    sh["glu_w_b"] = np.ascontiguousarray(inp["glu_w_b"][0])
    return sh


def _prep_core(inp, r, sh):
    b, q = r // 4, r % 4
    xs = inp["x_sample"][b]
    own = np.concatenate([inp["x_prompt"][2 * r], inp["x_prompt"][2 * r + 1], xs[512 * q:512 * q + 512]], 0)
    halo = np.zeros((256, D), np.float32)
    if q > 0:
        halo[:128] = xs[512 * q - 128:512 * q]
    if q < 3:
        halo[128:] = xs[512 * q + 512:512 * q + 640]
    cv = np.stack([inp["c_ctx"], inp["c"][b]], 1)
    m = dict(sh)
    m["x_own"] = _fm(own)
    m["x_halo"] = _fm(halo)
    m["cvec"] = np.ascontiguousarray(cv.reshape(KC, 128, 2).transpose(1, 0, 2)).astype(np.float32)
    m["rope"] = _rope_tables(q)
    import ml_dtypes
    m["amask"] = _amask(q).astype(ml_dtypes.bfloat16)
    m["ck"] = np.ascontiguousarray(inp["cache_k"][b, 0].reshape(256, 256))
    m["cv"] = np.ascontiguousarray(inp["cache_v"][b, 0].reshape(256, 256))
    st = inp["state_ssm"][b, 0]
    m["s0"] = np.ascontiguousarray(st.reshape(2, 2, 32, 2, 64).transpose(3, 4, 1, 0, 2).reshape(128, 2, 64)).astype(np.float32)
    qs = np.zeros((128, 4), np.float32)
    qs[:, q] = 1.0
    m["qsel"] = qs
    return m


_NC_CACHE = {}


STAGE = 99


def kernel(**inputs):
    inp = {k: np.asarray(v) for k, v in inputs.items()}
    sh = _prep_shared(inp)
    in_maps = [_prep_core(inp, r, sh) for r in range(NCORE)]
    nc = build_program(STAGE)
    res = run_bass_kernel_spmd(nc, in_maps, core_ids=list(range(NCORE)))
    y_prompt = np.zeros((16, 256, D), np.float32)
    y_sample = np.zeros((2, 2048, D), np.float32)
    new_k = np.zeros((16, 1, 256, 4, 64), np.float32)
    new_v = np.zeros((16, 1, 256, 4, 64), np.float32)
    new_s = np.zeros((16, 1, 2, 2, 64, 64), np.float32)
    for r in range(NCORE):
        o = res.results[r]
        b, q = r // 4, r % 4
        y = np.asarray(o["y"]).transpose(2, 1, 0).reshape(NT, D)
        y_prompt[2 * r] = y[0:256]
        y_prompt[2 * r + 1] = y[256:512]
        y_sample[b, 512 * q:512 * q + 512] = y[512:1024]
        nk = np.asarray(o["new_k"]).reshape(2, 256, 4, 64)
        nv = np.asarray(o["new_v"]).reshape(2, 256, 4, 64)
        new_k[2 * r:2 * r + 2, 0] = nk
        new_v[2 * r:2 * r + 2, 0] = nv
        ns = np.asarray(o["new_s"]).reshape(2, 64, 2, 2, 32, 2)
        new_s[2 * r:2 * r + 2, 0] = ns.transpose(5, 3, 2, 4, 0, 1).reshape(2, 2, 2, 64, 64)
    return (y_prompt, y_sample, new_k, new_v, new_s)
```

```python
import math
import os
import numpy as np
import concourse.bass as bass
import concourse.mybir as mybir
from concourse.bass_utils import run_bass_kernel_spmd

F32 = mybir.dt.float32
BF16 = mybir.dt.bfloat16
ALU = mybir.AluOpType
AF = mybir.ActivationFunctionType

NCORE = 8
D = 1024
KC = 8
NT = 1024
EPS = 1e-6
TWO_PI = 2.0 * math.pi


class Op:
    __slots__ = ("eng", "fn", "kind", "deps", "signal", "sig_val", "dsem", "dval", "prev_dma", "idx")

    def __init__(self, eng, fn, kind):
        self.eng = eng
        self.fn = fn
        self.kind = kind
        self.deps = []
        self.signal = False
        self.sig_val = None
        self.dsem = None
        self.dval = None
        self.prev_dma = None
        self.idx = None


class Prog:
    ENGS = ("pe", "act", "dve", "pool", "sp")
    NDMA = {"sp": 20, "pool": 20, "act": 6}

    def __init__(self):
        self.nc = bass.Bass("TRN2", target_bir_lowering=False)
        self.ops = {e: [] for e in self.ENGS}
        self.last_write = {}
        self.readers = {}
        self.dma_count = {e: 0 for e in self.ENGS}
        self.dma_hist = {e: [] for e in self.ENGS}
        self.cc_ops = []

    def op(self, eng, fn, reads=(), writes=(), dma=False, cc=False):
        kind = "cc" if cc else ("dma" if dma else None)
        o = Op(eng, fn, kind)
        deps = {}
        for k in reads:
            w = self.last_write.get(k)
            if w is not None:
                deps[id(w)] = (w, True)
        for k in writes:
            w = self.last_write.get(k)
            if w is not None and id(w) not in deps:
                deps[id(w)] = (w, False)
            for r in self.readers.get(k, ()):
                if id(r) not in deps:
                    deps[id(r)] = (r, False)
        for d, raw in deps.values():
            if d is o:
                continue
            if d.kind is None and kind is None and d.eng == eng:
                if eng == "pe" or (eng in ("dve", "act") and not raw):
                    continue
            o.deps.append(d)
            if d.kind is None:
                d.signal = True
        for k in reads:
            self.readers.setdefault(k, []).append(o)
        for k in writes:
            self.last_write[k] = o
            self.readers[k] = []
        if kind == "dma":
            n = self.NDMA[eng]
            i = self.dma_count[eng]
            self.dma_count[eng] += 1
            o.idx = i
            hist = self.dma_hist[eng]
            if i >= n:
                o.prev_dma = hist[i - n]
            hist.append(o)
        elif kind == "cc":
            o.idx = len(self.cc_ops)
            self.cc_ops.append(o)
        self.ops[eng].append(o)
        return o

    def fence(self, engs=("pe", "act", "dve", "pool", "sp")):
        deps = []
        for e in ("pe", "act", "dve", "pool"):
            comp = [o for o in self.ops[e] if o.kind is None]
            if comp:
                deps.append(comp[-1])
        for e, n in self.NDMA.items():
            deps += self.dma_hist[e][-n:]
        deps += self.cc_ops[-1:]
        for e in engs:
            o = Op(e, lambda eng: eng.nop(), None)
            o.deps = list(deps)
            for d in o.deps:
                if d.kind is None:
                    d.signal = True
            self.ops[e].append(o)

    def emit(self):
        nc = self.nc
        sems = {e: nc.alloc_semaphore("c_" + e) for e in ("pe", "act", "dve", "pool")}
        dsems = {e: [nc.alloc_semaphore(f"d_{e}{i}") for i in range(n)] for e, n in self.NDMA.items()}
        ccsem = nc.alloc_semaphore("ccsem")
        for e in self.ENGS:
            c = 0
            for o in self.ops[e]:
                if o.kind == "dma":
                    n = self.NDMA[e]
                    o.dsem = dsems[e][o.idx % n]
                    o.dval = 16 * (o.idx // n + 1)
                elif o.kind == "cc":
                    o.dsem = ccsem
                    o.dval = o.idx + 1
                elif o.signal:
                    c += 1
                    o.sig_val = c
        engobj = {"pe": "tensor", "act": "scalar", "dve": "vector", "pool": "gpsimd", "sp": "sync"}
        with nc.Block() as block:
            for e in self.ENGS:
                ops = self.ops[e]

                def body(eng, e=e, ops=ops):
                    waited = {}
                    for o in ops:
                        need = []
                        for d in o.deps:
                            if d.kind is not None:
                                need.append((d.dsem, d.dval))
                            else:
                                need.append((sems[d.eng], d.sig_val))
                        if o.prev_dma is not None:
                            need.append((o.prev_dma.dsem, o.prev_dma.dval))
                        for s, v in need:
                            k = id(s)
                            if waited.get(k, 0) >= v:
                                continue
                            waited[k] = v
                            eng.wait_ge(s, v)
                        ins = o.fn(eng)
                        if o.kind == "dma":
                            ins.then_inc(o.dsem, 16)
                        elif o.kind == "cc":
                            ins.then_inc(o.dsem)
                        elif o.signal:
                            ins.then_inc(sems[e], 1)
                    last = {}
                    for o in ops:
                        if o.kind is not None:
                            last[id(o.dsem)] = (o.dsem, o.dval)
                    for s, v in last.values():
                        if waited.get(id(s), 0) < v:
                            eng.wait_ge(s, v)

                getattr(block, engobj[e])(body)
        return nc


def build_program(stage=99):
    P = Prog()
    nc = P.nc
    dbg = {}

    def din(name, shape, dt=F32):
        return nc.dram_tensor(name, list(shape), dt, kind="ExternalInput")

    def dout(name, shape, dt=F32):
        return nc.dram_tensor(name, list(shape), dt, kind="ExternalOutput")

    x_own = din("x_own", [128, KC, NT])
    x_halo = din("x_halo", [128, KC, 256])
    cvec = din("cvec", [128, KC, 2])
    ng_d = din("ng", [128, 5, KC])
    bmod_d = din("bmodT", [128, 2, 48])
    rope_d = din("rope", [128, 2, 768])
    amask_d = din("amask", [128, 4, 2, 128], BF16)
    ident_d = din("ident", [128, 128])
    sink_d = din("sink", [128, 16])
    ck_d = din("ck", [256, 256])
    cv_d = din("cv", [256, 256])
    w_mod = din("w_mod", [2, D, 6 * D])
    wq_d = din("wq", [D, D])
    wqp_d = din("wqp", [D, D])
    wkd_d = din("wkd", [D, 512])
    wkdp_d = din("wkdp", [D, 512])
    wkv_d = din("wkv", [D, 512])
    wo_d = din("w_o", [D, D])
    w1_d = din("mlp_w1", [2, D, 4 * D])
    w2_d = din("mlp_w2", [2, 4 * D, D])
    y_d = dout("y", [128, KC, NT])
    nk_d = dout("new_k", [512, 256])
    nv_d = dout("new_v", [512, 256])

    X = nc.alloc_sbuf_tensor("s_X", [128, KC, NT], F32)
    H = nc.alloc_sbuf_tensor("s_H", [128, KC, NT], BF16)
    BIG = nc.alloc_sbuf_tensor("s_BIG", [128, 32768], BF16)
    WB = [nc.alloc_sbuf_tensor(f"s_WB{i}", [128, 4096], BF16) for i in range(3)]
    PT = [nc.alloc_sbuf_tensor(f"s_PT{i}", [128, 5, 512], BF16) for i in range(2)]
    SCR = [nc.alloc_sbuf_tensor(f"s_SCR{i}", [128, 512], F32) for i in range(4)]
    SQ = nc.alloc_sbuf_tensor("s_SQ", [128, KC, 512], BF16)
    RSTD = nc.alloc_sbuf_tensor("s_RSTD", [128, 512], F32)
    ones_bf = nc.alloc_sbuf_tensor("s_ones_bf", [128, 128], BF16)
    ident = nc.alloc_sbuf_tensor("s_ident", [128, 128], F32)
    cs_f = nc.alloc_sbuf_tensor("s_cs_f", [128, KC, 2], F32)
    cs_b = nc.alloc_sbuf_tensor("s_cs_b", [128, KC, 2], BF16)
    ng = nc.alloc_sbuf_tensor("s_ng", [128, 5, KC], F32)
    bmod = nc.alloc_sbuf_tensor("s_bmod", [128, 2, 48], F32)
    modT = nc.alloc_sbuf_tensor("s_modT", [128, 2, 48, 2], F32)
    gs = nc.alloc_sbuf_tensor("s_gs", [128, 2, 2, KC, 2], F32)
    L0B = nc.alloc_sbuf_tensor("s_L0B", [128, 3584], F32)
    rope = L0B[:, 0:1536].rearrange("p (a t) -> p a t", a=2)
    amask = nc.alloc_sbuf_tensor("s_amask", [128, 4, 2, 128], BF16)
    sink = nc.alloc_sbuf_tensor("s_sink", [128, 16], F32)
    esink = nc.alloc_sbuf_tensor("s_esink", [128, 16], F32)
    XH = nc.alloc_sbuf_tensor("s_XH", [128, KC, 256], F32)
    HH = nc.alloc_sbuf_tensor("s_HH", [128, KC, 256], BF16)
    CKV = L0B[:, 1536:2560].rearrange("p (a b n) -> p a b n", a=2, b=2)
    CKD = nc.alloc_sbuf_tensor("s_CKD", [128, 2, 4, 2, 64], BF16)
    ident_bf = nc.alloc_sbuf_tensor("s_ident_bf", [128, 128], BF16)
    KVO = L0B[:, 2560:3584].rearrange("p (a n) -> p a n", a=2)
    DEN = [nc.alloc_sbuf_tensor(f"s_DEN{i}", [128, 512], F32) for i in range(2)]
    RELU = [nc.alloc_sbuf_tensor(f"s_RELU{i}", [128, 512], BF16) for i in range(2)]
    PS = [nc.alloc_psum_tensor(f"p_PS{i}", [128, 512], F32) for i in range(8)]

    QT = BIG[:, 0:8192].rearrange("p (c t) -> p c t", c=8)
    KT = BIG[:, 8192:14336].rearrange("p (c t) -> p c t", c=4)
    VA = BIG[:, 14336:21504].rearrange("p (b k e) -> p b k e", b=14, k=4)
    ATT = BIG[:, 21504:29696].rearrange("p (c t) -> p c t", c=8)
    HID = BIG[:, 0:32768].rearrange("p (c t) -> p c t", c=32)

    ps_rr = [0]
    ps_mod = [8]

    def next_ps():
        i = ps_rr[0] % ps_mod[0]
        ps_rr[0] += 1
        return PS[i], ("ps", i)

    wb_rr = [0]

    def load_w(src_ap, shape_str=None, **kw):
        i = wb_rr[0] % 3
        wb_rr[0] += 1
        n = 1
        for s in src_ap.shape[1:]:
            n *= s
        dst = WB[i][:, 0:n]
        if shape_str is not None:
            dst = dst.rearrange(shape_str, **kw)
        P.op("pool", lambda e: e.dma_start(out=dst, in_=src_ap), writes=[("wb", i)], dma=True)
        return dst, ("wb", i)

    def dma_in(dst, src, key, eng="sp"):
        P.op(eng, lambda e: e.dma_start(out=dst, in_=src), writes=[key], dma=True)

    dma_in(X[:], x_own.ap(), "X_all")
    dma_in(XH[:], x_halo.ap(), "XH")
    dma_in(cs_f[:], cvec.ap(), "cs_f")
    dma_in(ng[:], ng_d.ap(), "ng")
    dma_in(bmod[:], bmod_d.ap(), "bmod")
    dma_in(rope, rope_d.ap(), "rope")
    dma_in(amask[:], amask_d.ap(), "amask")
    dma_in(ident[:], ident_d.ap(), "ident")
    dma_in(sink[:], sink_d.ap(), "sink")
    dma_in(CKV[:, :, 0, :], ck_d.ap().rearrange("(b p) n -> p b n", p=128), "CK")
    dma_in(CKV[:, :, 1, :], cv_d.ap().rearrange("(b p) n -> p b n", p=128), "CV")
    P.op("dve", lambda e: e.memset(ones_bf[:], 1.0), writes=["ones"])
    P.op("dve", lambda e: e.tensor_copy(ident_bf[:], ident[:]), reads=["ident"], writes=["ident_bf"])
    P.op("act", lambda e: e.activation(esink[:], sink[:], AF.Exp), reads=["sink"], writes=["esink"])
    P.op("act", lambda e: e.activation(cs_b[:], cs_f[:], AF.Silu), reads=["cs_f"], writes=["cs_b"])
    xkeys = [("X", kc, h) for kc in range(KC) for h in range(2)]
    for k in xkeys:
        P.last_write[k] = P.last_write["X_all"]

    def modulation_gen(i):
        psm = PS[7 - i]
        for piece in range(6):
            pk = ("psm", i, piece)
            for hb in range(2):
                cb = piece * 2 + hb
                wv, wk = load_w(w_mod.ap()[i].rearrange("(k p) n -> p k n", p=128)[:, :, cb * 512:(cb + 1) * 512],
                                "p (k n) -> p k n", k=KC)
                for cc in range(4):
                    j = cb * 4 + cc
                    for kc in range(KC):
                        P.op("pe", lambda e, j=j, kc=kc, cc=cc, wv=wv: e.matmul(
                            psm[:, j * 2:(j + 1) * 2], wv[:, kc, cc * 128:(cc + 1) * 128], cs_b[:, kc, :],
                            start=(kc == 0), stop=(kc == KC - 1)),
                            reads=[wk, "cs_b"], writes=[pk])
                if hb == 0:
                    yield
            bm = bass.AP(tensor=bmod, offset=i * 48 + piece * 8, ap=[[96, 128], [1, 8], [0, 2]])
            P.op("dve", lambda e, piece=piece, bm=bm: e.tensor_tensor(
                modT[:, i, piece * 8:(piece + 1) * 8, :], psm[:, piece * 16:(piece + 1) * 16].rearrange("p (j s) -> p j s", s=2), bm, ALU.add),
                reads=[pk, "bmod"], writes=[("modT", i, piece)])
            if piece in (1, 4):
                w = (piece - 1) // 3
                sc = modT[:, i, piece * 8:(piece + 1) * 8, :]
                gv = bass.AP(tensor=ng, offset=(2 * i + w) * KC, ap=[[5 * KC, 128], [1, KC], [0, 2]])
                P.op("dve", lambda e, w=w, sc=sc: e.tensor_scalar(gs[:, i, w], sc, 1.0, 32.0, ALU.add, ALU.mult),
                     reads=[("modT", i, piece)], writes=[("gs0", i, w)])
                P.op("dve", lambda e, w=w, gv=gv: e.tensor_tensor(gs[:, i, w], gs[:, i, w], gv, ALU.mult),
                     reads=[("gs0", i, w), "ng"], writes=[("gs", i, w)])
            yield

    def drain(g):
        for _ in g:
            pass

    def mod_piece(i, piece, kc, s):
        return modT[:, i, piece * 8 + kc, s:s + 1]

    def norm_mod(src, dst, t0, T, i, w, s, skey, dkey, final=False):
        ps, pk = next_ps()
        for kc in range(KC):
            P.op("act", lambda e, kc=kc: e.activation(SQ[:, kc, 0:T], src[:, kc, t0:t0 + T], AF.Square),
                 reads=[skey(kc)], writes=[("SQ", kc)])
        for kc in range(KC):
            P.op("pe", lambda e, kc=kc: e.matmul(ps[:, 0:T], ones_bf[:], SQ[:, kc, 0:T], start=(kc == 0), stop=(kc == KC - 1)),
                 reads=[("SQ", kc), "ones"], writes=[pk])
        NV = int(os.environ.get("NORMVAR", "9"))
        if NV < 2:
            return
        P.op("act", lambda e: e.activation(RSTD[:, 0:T], ps[:, 0:T], AF.Ln, bias=epsb[:, 0:1]), reads=[pk, "epsb"], writes=["RSTD"])
        P.op("act", lambda e: e.activation(RSTD[:, 0:T], RSTD[:, 0:T], AF.Exp, scale=-0.5), reads=["RSTD"], writes=["RSTD"])
        if NV < 3:
            return
        for kc in range(KC):
            sc = SCR[kc % 4]
            sk = ("SCR", kc % 4)
            P.op("dve", lambda e, kc=kc, sc=sc: e.tensor_tensor(sc[:, 0:T], src[:, kc, t0:t0 + T], RSTD[:, 0:T], ALU.mult),
                 reads=[skey(kc), "RSTD"], writes=[sk])
            if NV < 4:
                continue
            if final:
                P.op("dve", lambda e, kc=kc, sc=sc: e.tensor_scalar(dst[:, kc, 0:T], sc[:, 0:T], fng[:, kc:kc + 1], None, ALU.mult),
                     reads=[sk, "fng"], writes=[dkey(kc)])
            else:
                P.op("dve", lambda e, kc=kc, sc=sc: e.tensor_scalar(
                    dst[:, kc, t0:t0 + T], sc[:, 0:T], gs[:, i, w, kc, s:s + 1], mod_piece(i, 3 * w, kc, s), ALU.mult, ALU.add),
                    reads=[sk, ("gs", i, w), ("modT", i, 3 * w)], writes=[dkey(kc)])

    epsb = nc.alloc_sbuf_tensor("s_epsb", [128, 1], F32)
    P.op("dve", lambda e: e.memset(epsb[:], D * EPS), writes=["epsb"])
    fng = nc.alloc_sbuf_tensor("s_fng", [128, KC], F32)
    P.op("dve", lambda e: e.tensor_scalar(fng[:], ng[:, 4, :], 32.0, None, ALU.mult), reads=["ng"], writes=["fng"])

    def gated_residual(ps, pk, oc, h, i, piece):
        s = h
        P.op("dve", lambda e: e.scalar_tensor_tensor(
            X[:, oc, h * 512:(h + 1) * 512], ps[:, :], mod_piece(i, piece, oc, s), X[:, oc, h * 512:(h + 1) * 512],
            ALU.mult, ALU.add), reads=[pk, ("modT", i, piece), ("X", oc, h)], writes=[("X", oc, h)])

    def xkey_h(h):
        return lambda kc: ("X", kc, h)

    def hkey_h(h):
        return lambda kc: ("H", kc, h)

    def mlp(i, gen=None):
        for h in range(2):
            norm_mod(X, H, h * 512, 512, i, 1, h, xkey_h(h), hkey_h(h))
        w1v = w1_d.ap()[i].rearrange("(k p) n -> p k n", p=128)
        for hg in range(8):
            if gen is not None:
                next(gen, None)
            wv, wk = load_w(w1v[:, :, hg * 512:(hg + 1) * 512], "p (k n) -> p k n", k=KC)
            for cc in range(4):
                hc = hg * 4 + cc
                for h in range(2):
                    ps, pk = next_ps()
                    for kc in range(KC):
                        P.op("pe", lambda e, kc=kc, cc=cc, h=h, wv=wv, ps=ps: e.matmul(
                            ps[:, :], wv[:, kc, cc * 128:(cc + 1) * 128], H[:, kc, h * 512:(h + 1) * 512],
                            start=(kc == 0), stop=(kc == KC - 1)), reads=[wk, ("H", kc, h)], writes=[pk])
                    r = RELU[(hc * 2 + h) % 2]
                    rk = ("RELU", (hc * 2 + h) % 2)
                    P.op("act", lambda e, ps=ps, r=r: e.activation(r[:], ps[:, :], AF.Relu), reads=[pk], writes=[rk])
                    P.op("dve", lambda e, r=r, hc=hc, h=h: e.tensor_tensor(HID[:, hc, h * 512:(h + 1) * 512], r[:], r[:], ALU.mult),
                         reads=[rk], writes=[("HID", hc, h)])
        w2v = w2_d.ap()[i].rearrange("(k p) n -> p k n", p=128)
        for oc in range(KC):
            if gen is not None:
                next(gen, None)
            wv, wk = load_w(w2v[:, :, oc * 128:(oc + 1) * 128], "p (k n) -> p k n", k=32)
            for h in range(2):
                ps, pk = next_ps()
                for hc in range(32):
                    P.op("pe", lambda e, hc=hc, h=h, wv=wv, ps=ps: e.matmul(
                        ps[:, :], wv[:, hc, :], HID[:, hc, h * 512:(h + 1) * 512],
                        start=(hc == 0), stop=(hc == 31)), reads=[wk, ("HID", hc, h)], writes=[pk])
                gated_residual(ps, pk, oc, h, i, 5)

    def finish():
        for h in range(2):
            P.op("sp", lambda e, h=h: e.dma_start(out=y_d.ap()[:, :, h * 512:(h + 1) * 512], in_=X[:, :, h * 512:(h + 1) * 512]),
                 reads=[("X", kc, h) for kc in range(KC)], writes=[("y_out", h)], dma=True)
        return P.emit()

    ps_mod[0] = 6
    g0 = modulation_gen(0)
    for _ in range(4):
        next(g0)
    if stage == 1:
        return finish()
    for h in range(2):
        norm_mod(X, H, h * 512, 512, 0, 0, h, xkey_h(h), hkey_h(h))
    norm_mod(XH, HH, 0, 256, 0, 0, 1, lambda kc: "XH", lambda kc: ("HH", kc))

    if stage == 2:
        return finish()

    def proj_fm(wd, col0, jobs):
        wv, wk = load_w(wd.ap().rearrange("(k p) n -> p k n", p=128)[:, :, col0:col0 + 128], "p (k n) -> p k n", k=KC)
        for (ps, pk, off, n, rf, rkeys) in jobs:
            for kc in range(KC):
                P.op("pe", lambda e, kc=kc, off=off, n=n, rf=rf, ps=ps: e.matmul(
                    ps[:, off:off + n], wv[:, kc, :], rf(kc), start=(kc == 0), stop=(kc == KC - 1)),
                    reads=[wk, rkeys(kc)], writes=[pk])

    rhs_p = (0, 512, lambda kc: H[:, kc, 0:512], lambda kc: ("H", kc, 0))
    rhs_s = (0, 512, lambda kc: H[:, kc, 512:1024], lambda kc: ("H", kc, 1))
    rhs_hl = (0, 256, lambda kc: HH[:, kc, 0:256], lambda kc: ("HH", kc))

    def rope_evac(ps1, pk1, ps2, pk2, n, tab0, dst, dkey):
        a, ak = SCR[0], ("SCR", 0)
        b, bk = SCR[1], ("SCR", 1)
        P.op("dve", lambda e: e.tensor_tensor(a[:, 0:n], ps1[:, 0:n], rope[:, 0, tab0:tab0 + n], ALU.mult),
             reads=[pk1, "rope"], writes=[ak])
        P.op("dve", lambda e: e.tensor_tensor(b[:, 0:n], ps2[:, 0:n], rope[:, 1, tab0:tab0 + n], ALU.mult),
             reads=[pk2, "rope"], writes=[bk])
        P.op("dve", lambda e: e.tensor_tensor(dst, a[:, 0:n], b[:, 0:n], ALU.add), reads=[ak, bk], writes=[dkey])

    for c in range(8):
        ps, pk = next_ps()
        ps1, pk1 = next_ps()
        proj_fm(wq_d, c * 128, [(ps, pk) + rhs_p, (ps1, pk1) + rhs_s])
        P.op("act", lambda e, ps=ps, c=c: e.activation(QT[:, c, 0:512], ps[:, :], AF.Copy), reads=[pk], writes=[("QT", c, 0)])
        ps2, pk2 = next_ps()
        proj_fm(wqp_d, c * 128, [(ps2, pk2) + rhs_s])
        rope_evac(ps1, pk1, ps2, pk2, 512, 128, QT[:, c, 512:1024], ("QT", c, 1))
    for kv in range(4):
        ps, pk = next_ps()
        ps1, pk1 = next_ps()
        ps3, pk3 = next_ps()
        proj_fm(wkd_d, kv * 128, [(ps, pk) + rhs_p, (ps1, pk1) + rhs_s, (ps3, pk3) + rhs_hl])
        P.op("act", lambda e, ps=ps, kv=kv: e.activation(KT[:, kv, 0:512], ps[:, :], AF.Copy), reads=[pk], writes=[("KT", kv, 0)])
        ps2, pk2 = next_ps()
        ps4, pk4 = next_ps()
        proj_fm(wkdp_d, kv * 128, [(ps2, pk2) + rhs_s, (ps4, pk4) + rhs_hl])
        rope_evac(ps1, pk1, ps2, pk2, 512, 128, KT[:, kv, 640:1152], ("KT", kv, 1))
        ps1, pk1, ps2, pk2 = ps3, pk3, ps4, pk4
        rope_evac(ps1, pk1, ps2, pk2, 128, 0, KT[:, kv, 512:640], ("KT", kv, 2))
        a, ak = SCR[2], ("SCR", 2)
        b, bk = SCR[3], ("SCR", 3)
        P.op("dve", lambda e, ps1=ps1, a=a: e.tensor_tensor(a[:, 0:128], ps1[:, 128:256], rope[:, 0, 640:768], ALU.mult),
             reads=[pk1, "rope"], writes=[ak])
        P.op("dve", lambda e, ps2=ps2, b=b: e.tensor_tensor(b[:, 0:128], ps2[:, 128:256], rope[:, 1, 640:768], ALU.mult),
             reads=[pk2, "rope"], writes=[bk])
        P.op("dve", lambda e, kv=kv, a=a, b=b: e.tensor_tensor(KT[:, kv, 1152:1280], a[:, 0:128], b[:, 0:128], ALU.add),
             reads=[ak, bk], writes=[("KT", kv, 3)])
    if stage == 3:
        return finish()
    S4 = int(os.environ.get("S4VAR", "0"))
    for kb in range(0 if S4 == 1 else 2):
        for dup in range(2):
            P.op("dve", lambda e, kb=kb, dup=dup: e.tensor_copy(
                CKD[:, kb, :, dup, :], CKV[:, kb, 0, :].rearrange("p (k d) -> p k d", k=4)),
                reads=["CK"], writes=[("CKD", kb, dup)])
        for kv in range(4):
            ps, pk = next_ps()
            P.op("pe", lambda e, kb=kb, kv=kv, ps=ps: e.matmul(
                ps[:, 0:128], CKD[:, kb, kv].rearrange("p a d -> p (a d)"), ident_bf[:], start=True, stop=True),
                reads=[("CKD", kb, 0), ("CKD", kb, 1), "ident_bf"], writes=[pk])
            P.op("act", lambda e, kb=kb, kv=kv, ps=ps: e.activation(KT[:, kv, 1280 + kb * 128:1408 + kb * 128], ps[:, 0:128], AF.Copy),
                 reads=[pk], writes=[("KT", kv, 4 + kb)])
    wkv_v, wkv_k = load_w(wkv_d.ap().rearrange("(k p) n -> p k n", p=128), "p (k n) -> p k n", k=KC)
    vblocks = [(b, (lambda kc, b=b: H[:, kc, b * 128:(b + 1) * 128]), (lambda kc, b=b: ("H", kc, b // 4)), b if b < 4 else b + 1)
               for b in range(8)]
    vblocks += [(8, (lambda kc: HH[:, kc, 0:128]), (lambda kc: ("HH", kc)), 4),
                (9, (lambda kc: HH[:, kc, 128:256]), (lambda kc: ("HH", kc)), 9)]
    for (b, lf, lk, vb) in (vblocks if S4 != 2 else []):
        ps, pk = next_ps()
        for kc in range(KC):
            P.op("pe", lambda e, kc=kc, lf=lf, ps=ps: e.matmul(ps[:, :], lf(kc), wkv_v[:, kc, :], start=(kc == 0), stop=(kc == KC - 1)),
                 reads=[wkv_k, lk(kc)], writes=[pk])
        for dup in range(2):
            eng = "dve" if dup == 0 else "pool"
            if eng == "pool":
                continue
        for dup in range(2):
            P.op("act", lambda e, vb=vb, ps=ps, dup=dup: e.activation(
                VA[:, vb, :, dup * 64:(dup + 1) * 64], ps[:, 256:512].rearrange("p (k d) -> p k d", k=4), AF.Copy),
                reads=[pk], writes=[("VA", vb, dup)])
        if b < 4 and S4 != 3:
            P.op("act", lambda e, b=b, ps=ps: e.activation(KVO[:, b % 2, :], ps[:, :], AF.Copy), reads=[pk], writes=[("KVO", b % 2)])
            P.op("sp", lambda e, b=b: e.dma_start(out=nk_d.ap()[b * 128:(b + 1) * 128, :], in_=KVO[:, b % 2, 0:256]),
                 reads=[("KVO", b % 2)], writes=[("nk_out", b)], dma=True)
            P.op("sp", lambda e, b=b: e.dma_start(out=nv_d.ap()[b * 128:(b + 1) * 128, :], in_=KVO[:, b % 2, 256:512]),
                 reads=[("KVO", b % 2)], writes=[("nv_out", b)], dma=True)
    for kb in range(2):
        for dup in range(2):
            P.op("dve", lambda e, kb=kb, dup=dup: e.tensor_copy(
                VA[:, 10 + kb, :, dup * 64:(dup + 1) * 64], CKV[:, kb, 1, :].rearrange("p (k d) -> p k d", k=4)),
                reads=["CV"], writes=[("VA", 10 + kb, dup)])
    if stage == 4:
        return finish()
    def attention(qtok, half, keyblocks, pbuf):
        for kv in range(4):
            pt = PT[kv % 2]
            pbuf = kv % 2
            nkb = len(keyblocks)
            for j, (ktcol, ktk, vb, mside, qb) in enumerate(keyblocks):
                for hf in range(2):
                    ps, pk = next_ps()
                    for c2 in range(2):
                        P.op("pe", lambda e, hf=hf, c2=c2, kv=kv, ktcol=ktcol, ps=ps: e.matmul(
                            ps[:, c2 * 128:c2 * 128 + 128],
                            KT[hf * 64:(hf + 1) * 64, kv, ktcol:ktcol + 128],
                            QT[hf * 64:(hf + 1) * 64, 2 * kv + c2, qtok:qtok + 128], start=True, stop=True),
                            reads=[("KT", kv, ktk), ("QT", 2 * kv, half), ("QT", 2 * kv + 1, half)], writes=[pk])
                    P.op("act", lambda e, j=j, hf=hf, ps=ps, pt=pt: e.activation(pt[:, j, hf * 256:(hf + 1) * 256], ps[:, 0:256], AF.Exp, scale=0.125),
                         reads=[pk], writes=[("PT", pbuf, j, hf)])
                if mside is not None:
                    P.op("dve", lambda e, j=j, pt=pt, qb=qb, mside=mside: e.tensor_tensor(
                        pt[:, j, :].rearrange("p (a q) -> p a q", a=4), pt[:, j, :].rearrange("p (a q) -> p a q", a=4),
                        amask[:, qb, mside, :].unsqueeze(1).broadcast_to([128, 4, 128]), ALU.mult),
                        reads=[("PT", pbuf, j, 0), ("PT", pbuf, j, 1), "amask"], writes=[("PT", pbuf, j, 0), ("PT", pbuf, j, 1)])
            pso, pko = next_ps()
            psl, pkl = next_ps()
            for j, (ktcol, ktk, vb, mside, qb) in enumerate(keyblocks):
                P.op("pe", lambda e, j=j, vb=vb, kv=kv, pso=pso, pt=pt, nkb=nkb: e.matmul(pso[:, :], VA[:, vb, kv, :], pt[:, j, :], start=(j == 0), stop=(j == nkb - 1)),
                     reads=[("VA", vb, 0), ("VA", vb, 1), ("PT", pbuf, j, 0), ("PT", pbuf, j, 1)], writes=[pko])
            for j in range(nkb):
                P.op("pe", lambda e, j=j, psl=psl, pt=pt, nkb=nkb: e.matmul(psl[:, :], ones_bf[:], pt[:, j, :], start=(j == 0), stop=(j == nkb - 1)),
                     reads=["ones", ("PT", pbuf, j, 0), ("PT", pbuf, j, 1)], writes=[pkl])
            den = DEN[kv % 2]
            dk = ("DEN", kv % 2)
            es = bass.AP(tensor=esink, offset=4 * kv, ap=[[16, 128], [1, 2], [2, 2], [0, 128]])
            P.op("dve", lambda e, es=es, den=den, psl=psl: e.tensor_tensor(
                den[:, :].rearrange("p (h c q) -> p h c q", h=2, c=2), psl[:, :].rearrange("p (h c q) -> p h c q", h=2, c=2), es, ALU.add),
                reads=[pkl, "esink"], writes=[dk])
            P.op("dve", lambda e, den=den: e.reciprocal(den[:, :], den[:, :]), reads=[dk], writes=[dk])
            for hf in range(2):
                P.op("dve", lambda e, hf=hf, kv=kv, den=den, pso=pso: e.tensor_tensor(
                    ATT[hf * 64:(hf + 1) * 64, 2 * kv:2 * kv + 2, qtok:qtok + 128],
                    pso[hf * 64:(hf + 1) * 64, hf * 256:(hf + 1) * 256].rearrange("p (c q) -> p c q", c=2),
                    den[hf * 64:(hf + 1) * 64, hf * 256:(hf + 1) * 256].rearrange("p (c q) -> p c q", c=2), ALU.mult),
                    reads=[pko, dk], writes=[("ATT", 2 * kv, half), ("ATT", 2 * kv + 1, half)])

    pb = 0
    for s in range(2):
        for qb in range(2):
            kbs = [(s * 256 + j * 128, 0, 2 * s + j, None, 0) for j in range(2)]
            attention(s * 256 + qb * 128, 0, kbs, pb % 2)
            next(g0, None)
            pb += 1
    for qb in range(4):
        kbs = []
        for j in range(3):
            eb = qb + j
            ktk = 2 if eb == 0 else (3 if eb == 5 else 1)
            kbs.append((512 + eb * 128, ktk, 4 + eb, (0 if j == 0 else (1 if j == 2 else None)), qb))
        for kb in range(2):
            kbs.append((1280 + kb * 128, 4 + kb, 10 + kb, None, qb))
        attention(512 + qb * 128, 1, kbs, pb % 2)
        next(g0, None)
        pb += 1

    if stage == 5:
        return finish()
    drain(g0)
    for oc in range(KC):
        wv, wk = load_w(wo_d.ap().rearrange("(k p) n -> p k n", p=128)[:, :, oc * 128:(oc + 1) * 128], "p (k n) -> p k n", k=KC)
        for h in range(2):
            ps, pk = next_ps()
            for kc in range(KC):
                P.op("pe", lambda e, kc=kc, h=h, wv=wv, ps=ps: e.matmul(
                    ps[:, :], wv[:, kc, :], ATT[:, kc, h * 512:(h + 1) * 512], start=(kc == 0), stop=(kc == KC - 1)),
                    reads=[wk, ("ATT", kc, h)], writes=[pk])
            gated_residual(ps, pk, oc, h, 0, 2)
    if stage == 6:
        return finish()
    g1 = modulation_gen(1)
    mlp(0, g1)
    drain(g1)

    if stage == 7:
        return finish()
    P.fence()
    ssm_sc_d = din("ssm_sc", [128, 4, 64])
    s0_d = din("s0", [128, 2, 64])
    bpad_d = din("bpad", [8, 128, 2048])
    cpad_d = din("cpad", [8, 128, 2048])
    dskip_d = din("dskip", [128, KC])
    qsel_d = din("qsel", [128, 4])
    wa_d = din("glu_w_a", [D, D])
    wb_d = din("glu_w_b", [D, D])
    news_d = dout("new_s", [128, 2, 64, 2])
    g_in = nc.dram_tensor("g_in", [128, 128], F32)
    g_out = nc.dram_tensor("g_out", [512, 128], F32)
    tab_d = nc.dram_tensor("tab_scratch", [8, 128, 8192], F32)

    NTAB = 40
    TB = L0B[:, 0:2560].rearrange("p (a m) -> p a m", a=NTAB)
    CF = L0B[:, 2560:3584].bitcast(BF16)
    L0ROW = 3584

    def tb_ap(off, dims):
        return bass.AP(tensor=L0B, offset=off, ap=[[L0ROW, 128]] + dims)

    P0F = PT[0][:].rearrange("p a b -> p (a b)").bitcast(F32)
    P1F = PT[1][:].rearrange("p a b -> p (a b)").bitcast(F32)
    S0 = P0F[:, 0:128].rearrange("p (a m) -> p a m", a=2)
    FIN1 = P0F[:, 128:256].rearrange("p (a m) -> p a m", a=2)
    FIN2 = P0F[:, 256:512].rearrange("p (a m s) -> p a m s", a=2, s=2)
    NS = P0F[:, 512:768].rearrange("p (a m s) -> p a m s", a=2, s=2)
    TF = P0F[:, 768:896].rearrange("p (a m) -> p a m", a=2)
    ST = P0F[:, 896:1024].rearrange("p (a m) -> p a m", a=2)
    WI = P0F[:, 1024:1152].rearrange("p (a m) -> p a m", a=2)
    GG = P1F[:, 0:512].rearrange("p (q a m) -> p q a m", q=4, a=2)
    CH = P1F[:, 512:1024].rearrange("p (q a m) -> p q a m", q=4, a=2)
    NT1 = P1F[:, 1024:1152].rearrange("p (m s) -> p m s", s=2)
    NT2 = P1F[:, 1152:1280].rearrange("p (m s) -> p m s", s=2)
    hpi = nc.alloc_sbuf_tensor("s_hpi", [128, 1], F32)
    dskip = nc.alloc_sbuf_tensor("s_dskip", [128, KC], F32)
    qsel = nc.alloc_sbuf_tensor("s_qsel", [128, 4], F32)
    FTMP = nc.alloc_sbuf_tensor("s_FTMP", [128, 4], F32)
    fence_keys = [("X", kc, h) for kc in range(KC) for h in range(2)]
    (T_LR, T_TH, T_DT, T_R, T_R512, T_FRE, T_FIM, T_IFRE, T_IFIM, T_P1RE, T_P1IM, T_T1, T_T2, T_T3, T_T4) = range(15)
    T_CKC, T_CKS, T_LAMRE, T_LAMIM, T_LOGDT = 15, 25, 35, 36, 37

    def tb(i):
        return TB[:, i, :]

    tbk = lambda i: ("TB", i)

    def vop(eng, fn, reads, writes):
        P.op(eng, fn, reads=reads, writes=writes)

    def tt(eng, out, a, b, op, rk, wk):
        vop(eng, lambda e: e.tensor_tensor(out, a, b, op), rk, wk)

    P.op("sp", lambda e: e.dma_start(out=TB[:, T_LAMRE:T_LAMRE + 4, :], in_=ssm_sc_d.ap()), reads=fence_keys, writes=["ssm_sc"], dma=True)
    for i in range(T_LAMRE, T_LAMRE + 4):
        P.last_write[tbk(i)] = P.last_write["ssm_sc"]
    P.op("sp", lambda e: e.dma_start(out=S0, in_=s0_d.ap()), reads=fence_keys, writes=["S0"], dma=True)
    dma_in(dskip[:], dskip_d.ap(), "dskip")
    dma_in(qsel[:], qsel_d.ap(), "qsel")
    P.op("dve", lambda e: e.memset(hpi[:], math.pi / 2), writes=["hpi"])
    P.op("act", lambda e: e.activation(tb(T_DT), tb(T_LOGDT), AF.Exp), reads=[tbk(T_LOGDT)], writes=[tbk(T_DT)])
    tt("dve", tb(T_LR), tb(T_LAMRE), tb(T_DT), ALU.mult, [tbk(T_LAMRE), tbk(T_DT)], [tbk(T_LR)])
    tt("dve", tb(T_TH), tb(T_LAMIM), tb(T_DT), ALU.mult, [tbk(T_LAMIM), tbk(T_DT)], [tbk(T_TH)])
    P.op("act", lambda e: e.activation(tb(T_R), tb(T_LR), AF.Exp), reads=[tbk(T_LR)], writes=[tbk(T_R)])
    P.op("act", lambda e: e.activation(tb(T_R512), tb(T_LR), AF.Exp, scale=512.0), reads=[tbk(T_LR)], writes=[tbk(T_R512)])
    P.op("act", lambda e: e.activation(tb(T_CKS), tb(T_TH), AF.Sin, scale=1.0 / 64), reads=[tbk(T_TH)], writes=[tbk(T_CKS)])
    P.op("act", lambda e: e.activation(tb(T_CKC), tb(T_TH), AF.Sin, scale=1.0 / 64, bias=hpi[:, 0:1]), reads=[tbk(T_TH), "hpi"], writes=[tbk(T_CKC)])

    def square(ci, si, co, so):
        tt("dve", tb(T_T1), tb(ci), tb(ci), ALU.mult, [tbk(ci)], [tbk(T_T1)])
        tt("dve", tb(T_T2), tb(si), tb(si), ALU.mult, [tbk(si)], [tbk(T_T2)])
        vop("dve", lambda e: e.scalar_tensor_tensor(tb(so), tb(ci), 2.0, tb(si), ALU.mult, ALU.mult), [tbk(ci), tbk(si)], [tbk(so)])
        tt("dve", tb(co), tb(T_T1), tb(T_T2), ALU.subtract, [tbk(T_T1), tbk(T_T2)], [tbk(co)])

    for _ in range(6):
        square(T_CKC, T_CKS, T_CKC, T_CKS)
    for k in range(9):
        square(T_CKC + k, T_CKS + k, T_CKC + k + 1, T_CKS + k + 1)

    def cmul(eng, ore, oim, are, aim, bre, bim, rk, wk, t1, t2, tk1, tk2):
        tt(eng, t1, are, bre, ALU.mult, rk, [tk1])
        tt(eng, t2, aim, bim, ALU.mult, rk, [tk2])
        tt(eng, ore, t1, t2, ALU.subtract, [tk1, tk2], [wk[0]])
        tt(eng, t1, are, bim, ALU.mult, rk, [tk1])
        tt(eng, t2, aim, bre, ALU.mult, rk, [tk2])
        tt(eng, oim, t1, t2, ALU.add, [tk1, tk2], [wk[1]])

    tt("dve", tb(T_T3), tb(T_R), tb(T_CKC), ALU.mult, [tbk(T_R), tbk(T_CKC)], [tbk(T_T3)])
    tt("dve", tb(T_T4), tb(T_R), tb(T_CKS), ALU.mult, [tbk(T_R), tbk(T_CKS)], [tbk(T_T4)])
    vop("dve", lambda e: e.tensor_scalar(tb(T_T3), tb(T_T3), -1.0, None, ALU.add), [tbk(T_T3)], [tbk(T_T3)])
    tt("dve", tb(T_T1), tb(T_LAMRE), tb(T_LAMRE), ALU.mult, [tbk(T_LAMRE)], [tbk(T_T1)])
    tt("dve", tb(T_T2), tb(T_LAMIM), tb(T_LAMIM), ALU.mult, [tbk(T_LAMIM)], [tbk(T_T2)])
    tt("dve", tb(T_T1), tb(T_T1), tb(T_T2), ALU.add, [tbk(T_T1), tbk(T_T2)], [tbk(T_T1)])
    vop("dve", lambda e: e.reciprocal(tb(T_T1), tb(T_T1)), [tbk(T_T1)], [tbk(T_T1)])
    tt("dve", tb(T_FRE), tb(T_T3), tb(T_LAMRE), ALU.mult, [tbk(T_T3), tbk(T_LAMRE)], [tbk(T_FRE)])
    tt("dve", tb(T_T2), tb(T_T4), tb(T_LAMIM), ALU.mult, [tbk(T_T4), tbk(T_LAMIM)], [tbk(T_T2)])
    tt("dve", tb(T_FRE), tb(T_FRE), tb(T_T2), ALU.add, [tbk(T_FRE), tbk(T_T2)], [tbk(T_FRE)])
    tt("dve", tb(T_FRE), tb(T_FRE), tb(T_T1), ALU.mult, [tbk(T_FRE), tbk(T_T1)], [tbk(T_FRE)])
    tt("dve", tb(T_FIM), tb(T_T4), tb(T_LAMRE), ALU.mult, [tbk(T_T4), tbk(T_LAMRE)], [tbk(T_FIM)])
    tt("dve", tb(T_T2), tb(T_T3), tb(T_LAMIM), ALU.mult, [tbk(T_T3), tbk(T_LAMIM)], [tbk(T_T2)])
    tt("dve", tb(T_FIM), tb(T_FIM), tb(T_T2), ALU.subtract, [tbk(T_FIM), tbk(T_T2)], [tbk(T_FIM)])
    tt("dve", tb(T_FIM), tb(T_FIM), tb(T_T1), ALU.mult, [tbk(T_FIM), tbk(T_T1)], [tbk(T_FIM)])
    tt("dve", tb(T_T1), tb(T_FRE), tb(T_FRE), ALU.mult, [tbk(T_FRE)], [tbk(T_T1)])
    tt("dve", tb(T_T2), tb(T_FIM), tb(T_FIM), ALU.mult, [tbk(T_FIM)], [tbk(T_T2)])
    tt("dve", tb(T_T1), tb(T_T1), tb(T_T2), ALU.add, [tbk(T_T1), tbk(T_T2)], [tbk(T_T1)])
    vop("dve", lambda e: e.reciprocal(tb(T_T1), tb(T_T1)), [tbk(T_T1)], [tbk(T_T1)])
    tt("dve", tb(T_IFRE), tb(T_FRE), tb(T_T1), ALU.mult, [tbk(T_FRE), tbk(T_T1)], [tbk(T_IFRE)])
    vop("dve", lambda e: e.scalar_tensor_tensor(tb(T_IFIM), tb(T_FIM), -1.0, tb(T_T1), ALU.mult, ALU.mult), [tbk(T_FIM), tbk(T_T1)], [tbk(T_IFIM)])
    tt("dve", tb(T_P1RE), tb(T_R512), tb(T_CKC + 9), ALU.mult, [tbk(T_R512), tbk(T_CKC + 9)], [tbk(T_P1RE)])
    tt("dve", tb(T_P1IM), tb(T_R512), tb(T_CKS + 9), ALU.mult, [tbk(T_R512), tbk(T_CKS + 9)], [tbk(T_P1IM)])

    for h in range(2):
        norm_mod(X, H, h * 512, 512, 1, 0, h, xkey_h(h), hkey_h(h))

    BF = BIG[:].bitcast(F32)
    UC = BF[:, 0:4096].rearrange("p (a t) -> p a t", a=8)
    US = BF[:, 4096:8192].rearrange("p (a t) -> p a t", a=8)
    EW = [BF[:, 8192 + i * 512:8192 + (i + 1) * 512] for i in range(4)]
    UT = [BF[:, 8192:10240].rearrange("p (a t) -> p a t", a=8), BF[:, 10240:12288].rearrange("p (a t) -> p a t", a=8)]
    SBS = [BIG[:, 24576:25600].rearrange("p (a t) -> p a t", a=2), BIG[:, 29696:30720].rearrange("p (a t) -> p a t", a=2)]
    CFT1 = BF[:, 15360:15488]
    CFT2 = BF[:, 15488:15616]
    sb_i = [0]
    G0 = XH[:].bitcast(BF16)
    G1 = BIG[:, 25600:29696].rearrange("p (c t) -> p c t", c=8)

    def Gv(kc, h):
        return (G0 if h == 0 else G1)[:, kc, :]

    def build_tables(gh):
        P.op("dve", lambda e: e.memset(UC[:, :, 0:1], 1.0), reads=fence_keys, writes=["UC"])
        P.op("dve", lambda e: e.memset(US[:, :, 0:1], 0.0), reads=fence_keys, writes=["US"])
        for k in range(9):
            n = 1 << k
            ckc = tb_ap((T_CKC + k) * 64 + 4 * gh, [[32, 2], [1, 4], [0, n]])
            cks = tb_ap((T_CKS + k) * 64 + 4 * gh, [[32, 2], [1, 4], [0, n]])
            v4 = lambda ap: ap.rearrange("p (d m) t -> p d m t", d=2)
            sc_, ss_ = v4(UC[:, :, 0:n]), v4(US[:, :, 0:n])
            dc_, ds_ = v4(UC[:, :, n:2 * n]), v4(US[:, :, n:2 * n])
            t1, t2 = v4(UT[0][:, :, 0:n]), v4(UT[1][:, :, 0:n])
            rk = ["UC", "US", tbk(T_CKC + k), tbk(T_CKS + k)]
            tt("dve", t1, sc_, ckc, ALU.mult, rk, ["UT0"])
            tt("dve", t2, ss_, cks, ALU.mult, rk, ["UT1"])
            tt("dve", dc_, t1, t2, ALU.subtract, ["UT0", "UT1"], ["UC"])
            tt("dve", t1, sc_, cks, ALU.mult, rk, ["UT0"])
            tt("dve", t2, ss_, ckc, ALU.mult, rk, ["UT1"])
            tt("dve", ds_, t1, t2, ALU.add, ["UT0", "UT1"], ["US"])

    def seg_ap(ap512, lo, n, rev):
        v = ap512[:, lo:lo + n]
        return v[:, ::-1] if rev else v

    RS = [BF[:, 10240 + i * 512:10240 + (i + 1) * 512] for i in range(4)]

    def hv(ap512, half):
        return ap512.rearrange("p (s t) -> p s t", s=2) if half == 0 else ap512

    def tabv(T, a, half, d):
        n = 256 if half == 0 else 512
        v = T[:, a, 0:n]
        if d == 1:
            v = v[:, ::-1]
        return v.unsqueeze(1).broadcast_to([128, 2, 256]) if half == 0 else v

    def demod(a, md, half, ps_re, pk_re, ps_im, pk_im, d):
        uc, us = tabv(UC, a, half, d), tabv(US, a, half, d)
        xr, xi = hv(ps_re[:, :], half), hv(ps_im[:, :], half)
        t = [hv(SCR[i][:, :], half) for i in range(4)]
        rk = [pk_re, pk_im, "UC", "US"]
        tt("dve", t[0], xr, uc, ALU.mult, rk, [("SCR", 0)])
        tt("dve", t[1], xi, us, ALU.mult, rk, [("SCR", 1)])
        tt("dve", t[2], xi, uc, ALU.mult, rk, [("SCR", 2)])
        tt("dve", t[3], xr, us, ALU.mult, rk, [("SCR", 3)])
        tt("dve", hv(EW[0], half), t[0], t[1], ALU.add, [("SCR", 0), ("SCR", 1)], [("EW", 0)])
        tt("dve", hv(EW[1], half), t[2], t[3], ALU.subtract, [("SCR", 2), ("SCR", 3)], [("EW", 1)])

    def scans(md, half, d, use_init):
        segs = [(0, 256), (256, 256)] if half == 0 else [(0, 512)]
        for (lo, n) in segs:
            rbc = tb_ap(T_R * 64 + md, [[0, n]])
            for ri in range(2):
                init = WI[:, ri, md:md + 1] if use_init else 0.0
                src = seg_ap(EW[ri], lo, n, d == 1)
                dst = seg_ap(EW[2 + ri], lo, n, d == 1)
                vop("dve", lambda e, dst=dst, rbc=rbc, src=src, init=init: e.tensor_tensor_scan(dst, rbc, src, init, ALU.mult, ALU.add),
                    [("EW", ri), tbk(T_R)] + (["WI"] if use_init else []), [("EW", 2 + ri)])

    def finals(a, md, half, d, dst):
        L = 256 if half == 0 else 512
        ncol = 2 if half == 0 else 1
        c0 = (L - 1) if d == 0 else 0
        wre = EW[2][:, c0::256][:, 0:ncol] if half == 0 else EW[2][:, c0:c0 + 1]
        wim = EW[3][:, c0::256][:, 0:ncol] if half == 0 else EW[3][:, c0:c0 + 1]
        ucl, usl = UC[:, a, L - 1:L], US[:, a, L - 1:L]
        tA, tB = FTMP[:, 0:ncol], FTMP[:, 2:2 + ncol]
        rk = [("EW", 2), ("EW", 3), "UC", "US"]
        vop("dve", lambda e: e.tensor_scalar(tA, wim, usl, None, ALU.mult), rk, ["FTMPA"])
        vop("dve", lambda e: e.tensor_scalar(tB, wre, usl, None, ALU.mult), rk, ["FTMPB"])
        vop("dve", lambda e: e.scalar_tensor_tensor(dst[0], wre, ucl, tA, ALU.mult, ALU.subtract), rk + ["FTMPA"], [dst[2]])
        vop("dve", lambda e: e.scalar_tensor_tensor(dst[1], wim, ucl, tB, ALU.mult, ALU.add), rk + ["FTMPB"], [dst[3]])

    def remod(a, half, d, SB, sbi):
        uc, us = tabv(UC, a, half, d), tabv(US, a, half, d)
        wr, wi = hv(EW[2], half), hv(EW[3], half)
        t = [hv(RS[i], half) for i in range(4)]
        rk = [("EW", 2), ("EW", 3), "UC", "US"]
        tt("dve", t[0], wr, uc, ALU.mult, rk, [("RS", 0)])
        tt("dve", t[1], wi, us, ALU.mult, rk, [("RS", 1)])
        tt("dve", t[2], wi, uc, ALU.mult, rk, [("RS", 2)])
        tt("dve", t[3], wr, us, ALU.mult, rk, [("RS", 3)])
        tt("dve", hv(SB[:, 0, :], half), t[0], t[1], ALU.subtract, [("RS", 0), ("RS", 1)], [("SB", sbi, 0)])
        vop("dve", lambda e: e.scalar_tensor_tensor(hv(SB[:, 1, :], half), t[2], -1.0, t[3], ALU.mult, ALU.subtract),
            [("RS", 2), ("RS", 3)], [("SB", sbi, 1)])

    def x_matmuls(bv, bk, gh, d, m4, half):
        outs = []
        for ri in range(2):
            ps, pk = next_ps()
            col = ((d * 2 + ri) * 4 + m4) * 128
            P.op("pe", lambda e, ps=ps, col=col: e.matmul(ps[:, :], bv[:, col:col + 128], H[:, gh, half * 512:(half + 1) * 512], start=True, stop=True),
                 reads=[bk, ("H", gh, half)], writes=[pk])
            outs += [ps, pk]
        return outs

    ps_mod[0] = 6
    bpv = lambda gh: bpad_d.ap()[gh]
    for gh in range(8):
        build_tables(gh)
        P.op("sp", lambda e, gh=gh: e.dma_start(out=tab_d.ap()[gh], in_=BF[:, 0:8192]), reads=["UC", "US"], writes=[("tab_d", gh)], dma=True)
        bv, bk = load_w(bpv(gh))
        for d in range(2):
            for m4 in range(4):
                a = d * 4 + m4
                md = d * 32 + 4 * gh + m4
                psr, pkr, psi, pki = x_matmuls(bv, bk, gh, d, m4, 1)
                demod(a, md, 1, psr, pkr, psi, pki, d)
                scans(md, 1, d, False)
                finals(a, md, 1, d, (FIN1[:, 0, md:md + 1], FIN1[:, 1, md:md + 1], ("FIN1", md), ("FIN1", md)))
    fin1k = [("FIN1", md) for md in range(64)]
    cmul("dve", TF[:, 0, :], TF[:, 1, :], FIN1[:, 0, :], FIN1[:, 1, :], tb(T_FRE), tb(T_FIM),
         fin1k + [tbk(T_FRE), tbk(T_FIM)], ["TF", "TF"], tb(T_T1), tb(T_T2), tbk(T_T1), tbk(T_T2))
    P.op("sp", lambda e: e.dma_start(out=g_in.ap(), in_=TF.rearrange("p a b -> p (a b)")), reads=["TF"], writes=["g_in"], dma=True)
    P.op("pool", lambda e: e.collective_compute("AllGather", ALU.bypass, replica_groups=[[0, 1, 2, 3], [4, 5, 6, 7]],
                                                  ins=[g_in.ap().opt()], outs=[g_out.ap().opt()]),
         reads=["g_in"], writes=["g_out"], cc=True)
    P.op("sp", lambda e: e.dma_start(out=GG.rearrange("p q a b -> p q (a b)"), in_=g_out.ap().rearrange("(q p) n -> p q n", p=128)),
         reads=["g_out"], writes=["GG"], dma=True)
    for d in range(2):
        sl = slice(d * 32, (d + 1) * 32)
        q0 = 0 if d == 0 else 3
        for ri in range(2):
            vop("dve", lambda e, ri=ri, sl=sl, q0=q0: e.tensor_copy(CH[:, q0, ri, sl], S0[:, ri, sl]), ["S0"], [("CH", q0, d)])
        order = [(0, 1, 0), (1, 2, 1), (2, 3, 2)] if d == 0 else [(3, 2, 3), (2, 1, 2), (1, 0, 1)]
        for (qs, qd, qg) in order:
            cmul("dve", CH[:, qd, 0, sl], CH[:, qd, 1, sl], CH[:, qs, 0, sl], CH[:, qs, 1, sl], TB[:, T_P1RE, sl], TB[:, T_P1IM, sl],
                 [("CH", qs, d), tbk(T_P1RE), tbk(T_P1IM)], [("CH", qd, d), ("CH", qd, d)], TB[:, T_T1, sl], TB[:, T_T2, sl], tbk(T_T1), tbk(T_T2))
            for ri in range(2):
                tt("dve", CH[:, qd, ri, sl], CH[:, qd, ri, sl], GG[:, qg, ri, sl], ALU.add, [("CH", qd, d), "GG"], [("CH", qd, d)])
    chk = [("CH", q, d) for q in range(4) for d in range(2)]
    for ri in range(2):
        vop("dve", lambda e, ri=ri: e.tensor_scalar(ST[:, ri, :], CH[:, 0, ri, :], qsel[:, 0:1], None, ALU.mult), chk + ["qsel"], [("ST", ri)])
        for q in range(1, 4):
            vop("dve", lambda e, ri=ri, q=q: e.scalar_tensor_tensor(ST[:, ri, :], CH[:, q, ri, :], qsel[:, q:q + 1], ST[:, ri, :], ALU.mult, ALU.add),
                chk + ["qsel", ("ST", ri)], [("ST", ri)])
    cmul("dve", TF[:, 0, :], TF[:, 1, :], ST[:, 0, :], ST[:, 1, :], tb(T_IFRE), tb(T_IFIM),
         [("ST", 0), ("ST", 1), tbk(T_IFRE), tbk(T_IFIM), "g_in"], ["TF2", "TF2"], tb(T_T1), tb(T_T2), tbk(T_T1), tbk(T_T2))
    cmul("dve", WI[:, 0, :], WI[:, 1, :], TF[:, 0, :], TF[:, 1, :], tb(T_CKC), tb(T_CKS),
         ["TF2", tbk(T_CKC), tbk(T_CKS)], ["WI", "WI"], tb(T_T1), tb(T_T2), tbk(T_T1), tbk(T_T2))

    cpv = lambda gh: cpad_d.ap()[gh]
    for gh in range(8):
        P.op("sp", lambda e, gh=gh: e.dma_start(out=BF[:, 0:8192], in_=tab_d.ap()[gh]), reads=[("tab_d", gh)], writes=["UC", "US"], dma=True)
        bv, bk = load_w(bpv(gh))
        cv_, ck_ = load_w(cpv(gh))
        for d in range(2):
            for m4 in range(4):
                md = d * 32 + 4 * gh + m4
                cre = cv_[:, ((d * 2 + 0) * 4 + m4) * 128:((d * 2 + 0) * 4 + m4 + 1) * 128]
                cim = cv_[:, ((d * 2 + 1) * 4 + m4) * 128:((d * 2 + 1) * 4 + m4 + 1) * 128]
                ore = CF[:, ((d * 2 + 0) * 4 + m4) * 128:((d * 2 + 0) * 4 + m4 + 1) * 128]
                oim = CF[:, ((d * 2 + 1) * 4 + m4) * 128:((d * 2 + 1) * 4 + m4 + 1) * 128]
                fre, fim = TB[:, T_FRE, md:md + 1], TB[:, T_FIM, md:md + 1]
                rk = [ck_, tbk(T_FRE), tbk(T_FIM)]
                vop("dve", lambda e, cim=cim, fim=fim: e.tensor_scalar(CFT1, cim, fim, None, ALU.mult), rk, ["CFT1"])
                vop("dve", lambda e, ore=ore, cre=cre, fre=fre: e.scalar_tensor_tensor(ore, cre, fre, CFT1, ALU.mult, ALU.subtract),
                    rk + ["CFT1"], [("CF", d, m4, 0)])
                vop("dve", lambda e, cim=cim, fre=fre: e.tensor_scalar(CFT2, cim, fre, None, ALU.mult), rk, ["CFT2"])
                vop("dve", lambda e, oim=oim, cre=cre, fim=fim: e.scalar_tensor_tensor(oim, cre, fim, CFT2, ALU.mult, ALU.add),
                    rk + ["CFT2"], [("CF", d, m4, 1)])
        ysl = [(PS[6], ("ps", 6)), (PS[7], ("ps", 7))]
        cnt = [0, 0]
        for d in range(2):
            for m4 in range(4):
                a = d * 4 + m4
                md = d * 32 + 4 * gh + m4
                for half in range(2):
                    psr, pkr, psi, pki = x_matmuls(bv, bk, gh, d, m4, half)
                    demod(a, md, half, psr, pkr, psi, pki, d)
                    scans(md, half, d, half == 1)
                    sbi = sb_i[0] % 2
                    sb_i[0] += 1
                    SB = SBS[sbi]
                    remod(a, half, d, SB, sbi)
                    if half == 0:
                        finals(a, md, 0, d, (FIN2[:, 0, md, :], FIN2[:, 1, md, :], ("FIN2", md), ("FIN2", md)))
                    psy, pky = ysl[half]
                    for ri in range(2):
                        col = ((d * 2 + ri) * 4 + m4) * 128
                        first = cnt[half] == 0
                        last = cnt[half] == 15
                        cnt[half] += 1
                        P.op("pe", lambda e, psy=psy, col=col, ri=ri, first=first, last=last, SB=SB: e.matmul(
                            psy[:, :], CF[:, col:col + 128], SB[:, ri, :], start=first, stop=last),
                            reads=[("CF", d, m4, ri), ("SB", sbi, ri)], writes=[pky])
        for half in range(2):
            psy, pky = ysl[half]
            yp, t1, t2 = RS[0], RS[1], RS[2]
            hs = slice(half * 512, (half + 1) * 512)
            vop("dve", lambda e, psy=psy, hs=hs, gh=gh: e.scalar_tensor_tensor(yp, H[:, gh, hs], dskip[:, gh:gh + 1], psy[:, :], ALU.mult, ALU.add),
                [pky, ("H", gh, half), "dskip"], [("RS", 0)])
            tt("pool", t1, yp, yp, ALU.mult, [("RS", 0)], [("RS", 1)])
            vop("pool", lambda e: e.tensor_scalar(t1, t1, 0.044715, 1.0, ALU.mult, ALU.add), [("RS", 1)], [("RS", 1)])
            tt("pool", t1, t1, yp, ALU.mult, [("RS", 1), ("RS", 0)], [("RS", 1)])
            P.op("act", lambda e: e.activation(t2, t1, AF.Sigmoid, scale=1.5957691216057308), reads=[("RS", 1)], writes=[("RS", 2)])
            tt("pool", Gv(gh, half), yp, t2, ALU.mult, [("RS", 0), ("RS", 2)], [("G", gh, half)])
    ps_mod[0] = 8
    fin2k = [("FIN2", md) for md in range(64)]
    fre_b = tb_ap(T_FRE * 64, [[1, 64], [0, 2]])
    fim_b = tb_ap(T_FIM * 64, [[1, 64], [0, 2]])
    cmul("dve", NS[:, 0], NS[:, 1], FIN2[:, 0], FIN2[:, 1], fre_b, fim_b, fin2k + [tbk(T_FRE), tbk(T_FIM)],
         ["NS", "NS"], NT1, NT2, "NT1", "NT2")
    P.op("sp", lambda e: e.dma_start(out=news_d.ap(), in_=NS), reads=["NS"], writes=["news_out"], dma=True)

    for oc in range(KC):
        wav, wak = load_w(wa_d.ap().rearrange("(k p) n -> p k n", p=128)[:, :, oc * 128:(oc + 1) * 128], "p (k n) -> p k n", k=KC)
        wbv, wbk = load_w(wb_d.ap().rearrange("(k p) n -> p k n", p=128)[:, :, oc * 128:(oc + 1) * 128], "p (k n) -> p k n", k=KC)
        for h in range(2):
            psa, pka = next_ps()
            psb, pkb = next_ps()
            for (wv_, wk_, ps_, pk_) in ((wav, wak, psa, pka), (wbv, wbk, psb, pkb)):
                for kc in range(KC):
                    P.op("pe", lambda e, kc=kc, h=h, wv_=wv_, ps_=ps_: e.matmul(
                        ps_[:, :], wv_[:, kc, :], Gv(kc, h), start=(kc == 0), stop=(kc == KC - 1)),
                        reads=[wk_, ("G", kc, h)], writes=[pk_])
            sg, pr = SCR[0], SCR[1]
            P.op("act", lambda e, psb=psb: e.activation(sg[:], psb[:, :], AF.Sigmoid), reads=[pkb], writes=[("SCR", 0)])
            tt("dve", pr[:], psa[:, :], sg[:], ALU.mult, [pka, ("SCR", 0)], [("SCR", 1)])
            P.op("dve", lambda e, oc=oc, h=h: e.scalar_tensor_tensor(
                X[:, oc, h * 512:(h + 1) * 512], pr[:], mod_piece(1, 2, oc, h), X[:, oc, h * 512:(h + 1) * 512], ALU.mult, ALU.add),
                reads=[("SCR", 1), ("modT", 1, 2), ("X", oc, h)], writes=[("X", oc, h)])
    mlp(1)
    YF = BF[:, 0:4096].rearrange("p (c t) -> p c t", c=8)
    for h in range(2):
        norm_mod(X, YF, h * 512, 512, 1, 0, h, xkey_h(h), lambda kc: ("YF", kc), final=True)
        P.op("sp", lambda e, h=h: e.dma_start(out=y_d.ap()[:, :, h * 512:(h + 1) * 512], in_=YF[:, :, 0:512]),
             reads=[("YF", kc) for kc in range(KC)], writes=[("y_out", h)], dma=True)
    return P.emit()


def _fm(x_tok):
    T = x_tok.shape[0]
    return np.ascontiguousarray(x_tok.reshape(T, KC, 128).transpose(2, 1, 0))


def _rope_tables(q):
    pos = 512 * q - 128 + np.arange(768)
    row = (pos // 64).astype(np.float32)
    col = (pos % 64).astype(np.float32)
    freqs = (10000.0 ** (-np.arange(16, dtype=np.float32) / 16)).astype(np.float32)
    tab = np.zeros((128, 2, 768), np.float32)
    for p in range(128):
        d = p % 64
        f = freqs[d % 16]
        ang = (row if d < 32 else col) * f
        tab[p, 0] = np.cos(ang)
        tab[p, 1] = np.sin(ang) * (-1.0 if (d % 32) < 16 else 1.0)
    return tab


def _amask(q):
    m = np.zeros((128, 4, 2, 128), np.float32)
    jj = np.arange(128)[:, None]
    rr = np.arange(128)[None, :]
    for qb in range(4):
        lpos0 = 512 * q + 128 * qb - 128
        rpos0 = 512 * q + 128 * qb + 128
        m[:, qb, 0, :] = (jj >= rr) * (1.0 if lpos0 >= 0 else 0.0)
        m[:, qb, 1, :] = (jj <= rr) * (1.0 if rpos0 < 2048 else 0.0)
    return m


def _prep_shared(inp):
    f = np.float32
    sh = {}
    w_qkv = inp["w_qkv"][0]
    wq = w_qkv[:, :1024]
    wk = w_qkv[:, 1024:1280]
    wv = w_qkv[:, 1280:1536]
    d = np.arange(64)
    partner = np.where((d % 32) < 16, d + 16, d - 16)
    qperm = (np.arange(16)[:, None] * 64 + partner[None, :]).reshape(-1)
    kperm = (np.arange(4)[:, None] * 64 + partner[None, :]).reshape(-1)
    dup = (np.arange(4)[:, None, None] * 64 + np.zeros((1, 2, 1), int) + d[None, None, :]).reshape(-1)
    sh["wq"] = np.ascontiguousarray(wq)
    sh["wqp"] = np.ascontiguousarray(wq[:, qperm])
    sh["wkd"] = np.ascontiguousarray(wk[:, dup])
    sh["wkdp"] = np.ascontiguousarray(wk[:, kperm][:, dup])
    sh["wkv"] = np.ascontiguousarray(np.concatenate([wk, wv], axis=1))
    sh["w_o"] = np.ascontiguousarray(inp["w_o"][0])
    sh["w_mod"] = np.ascontiguousarray(inp["w_mod"])
    sh["mlp_w1"] = np.ascontiguousarray(inp["mlp_w1"])
    sh["mlp_w2"] = np.ascontiguousarray(inp["mlp_w2"])
    ngs = np.stack([inp["norm1_g"][0], inp["norm2_g"][0], inp["norm1_g"][1], inp["norm2_g"][1], inp["final_norm_g"]], 0)
    sh["ng"] = np.ascontiguousarray(ngs.reshape(5, KC, 128).transpose(2, 0, 1)).astype(f)
    sh["bmodT"] = np.ascontiguousarray(inp["b_mod"].reshape(2, 48, 128).transpose(2, 0, 1)).astype(f)
    sh["ident"] = np.eye(128, dtype=f)
    sh["sink"] = np.ascontiguousarray(np.broadcast_to(inp["attn_sink"][0][None, :], (128, 16))).astype(f)
    def gp_md(a):
        return a.reshape(2, 32, 2, 64).transpose(2, 3, 0, 1).reshape(128, 64)
    ldt = np.broadcast_to(inp["ssm_log_dt"][0][:, :, None], (2, 64, 64))
    sc = np.zeros((128, 4, 64), f)
    sc[:, 0] = gp_md(inp["ssm_lam_re"][0])
    sc[:, 1] = gp_md(inp["ssm_lam_im"][0])
    sc[:, 2] = gp_md(ldt)
    sh["ssm_sc"] = sc
    bp = np.zeros((8, 8, 16, 2, 2, 4, 2, 64), f)
    cp = np.zeros((8, 2, 64, 2, 2, 4, 8, 16), f)
    Bs = (inp["ssm_b_re"][0], inp["ssm_b_im"][0])
    Cs = (inp["ssm_c_re"][0], inp["ssm_c_im"][0])
    for gh in range(8):
        for m4 in range(4):
            for g2 in range(2):
                g = 8 * gh + 2 * m4 + g2
                gl = 2 * m4 + g2
                for d in range(2):
                    for ri in range(2):
                        bp[gh, gl, :, d, ri, m4, g2, :] = Bs[ri][d, g].T
                        cp[gh, g2, :, d, ri, m4, gl, :] = Cs[ri][d, g].T
    sh["bpad"] = bp.reshape(8, 128, 2048)
    sh["cpad"] = cp.reshape(8, 128, 2048)
    sh["dskip"] = np.ascontiguousarray(inp["ssm_d"][0].reshape(KC, 128).T).astype(f)
    sh["glu_w_a"] = np.ascontiguousarray(inp["glu_w_a"][0])
    sh["glu_w_b"] = np.ascontiguousarray(inp["glu_w_b"][0])
    return sh


def _prep_core(inp, r, sh):
    b, q = r // 4, r % 4
    xs = inp["x_sample"][b]
    own = np.concatenate([inp["x_prompt"][2 * r], inp["x_prompt"][2 * r + 1], xs[512 * q:512 * q + 512]], 0)
    halo = np.zeros((256, D), np.float32)
    if q > 0:
        halo[:128] = xs[512 * q - 128:512 * q]
    if q < 3:
        halo[128:] = xs[512 * q + 512:512 * q + 640]
    cv = np.stack([inp["c_ctx"], inp["c"][b]], 1)
    m = dict(sh)
    m["x_own"] = _fm(own)
    m["x_halo"] = _fm(halo)
    m["cvec"] = np.ascontiguousarray(cv.reshape(KC, 128, 2).transpose(1, 0, 2)).astype(np.float32)
    m["rope"] = _rope_tables(q)
    import ml_dtypes
    m["amask"] = _amask(q).astype(ml_dtypes.bfloat16)
    m["ck"] = np.ascontiguousarray(inp["cache_k"][b, 0].reshape(256, 256))
    m["cv"] = np.ascontiguousarray(inp["cache_v"][b, 0].reshape(256, 256))
    st = inp["state_ssm"][b, 0]
    m["s0"] = np.ascontiguousarray(st.reshape(2, 2, 32, 2, 64).transpose(3, 4, 1, 0, 2).reshape(128, 2, 64)).astype(np.float32)
    qs = np.zeros((128, 4), np.float32)
    qs[:, q] = 1.0
    m["qsel"] = qs
    return m


_NC_CACHE = {}


STAGE = 99


def kernel(**inputs):
    inp = {k: np.asarray(v) for k, v in inputs.items()}
    sh = _prep_shared(inp)
    in_maps = [_prep_core(inp, r, sh) for r in range(NCORE)]
    nc = build_program(STAGE)
    res = run_bass_kernel_spmd(nc, in_maps, core_ids=list(range(NCORE)))
    y_prompt = np.zeros((16, 256, D), np.float32)
    y_sample = np.zeros((2, 2048, D), np.float32)
    new_k = np.zeros((16, 1, 256, 4, 64), np.float32)
    new_v = np.zeros((16, 1, 256, 4, 64), np.float32)
    new_s = np.zeros((16, 1, 2, 2, 64, 64), np.float32)
    for r in range(NCORE):
        o = res.results[r]
        b, q = r // 4, r % 4
        y = np.asarray(o["y"]).transpose(2, 1, 0).reshape(NT, D)
        y_prompt[2 * r] = y[0:256]
        y_prompt[2 * r + 1] = y[256:512]
        y_sample[b, 512 * q:512 * q + 512] = y[512:1024]
        nk = np.asarray(o["new_k"]).reshape(2, 256, 4, 64)
        nv = np.asarray(o["new_v"]).reshape(2, 256, 4, 64)
        new_k[2 * r:2 * r + 2, 0] = nk
        new_v[2 * r:2 * r + 2, 0] = nv
        ns = np.asarray(o["new_s"]).reshape(2, 64, 2, 2, 32, 2)
        new_s[2 * r:2 * r + 2, 0] = ns.transpose(5, 3, 2, 4, 0, 1).reshape(2, 2, 2, 64, 64)
    return (y_prompt, y_sample, new_k, new_v, new_s)
```

```python
import math
import os
import numpy as np
import concourse.bass as bass
import concourse.mybir as mybir
from concourse.bass_utils import run_bass_kernel_spmd

F32 = mybir.dt.float32
BF16 = mybir.dt.bfloat16
ALU = mybir.AluOpType
AF = mybir.ActivationFunctionType

NCORE = 8
D = 1024
KC = 8
NT = 1024
EPS = 1e-6
TWO_PI = 2.0 * math.pi


class Op:
    __slots__ = ("eng", "fn", "kind", "deps", "signal", "sig_val", "dsem", "dval", "prev_dma", "idx")

    def __init__(self, eng, fn, kind):
        self.eng = eng
        self.fn = fn
        self.kind = kind
        self.deps = []
        self.signal = False
        self.sig_val = None
        self.dsem = None
        self.dval = None
        self.prev_dma = None
        self.idx = None


class Prog:
    ENGS = ("pe", "act", "dve", "pool", "sp")
    NDMA = {"sp": 20, "pool": 20, "act": 6}

    def __init__(self):
        self.nc = bass.Bass("TRN2", target_bir_lowering=False)
        self.ops = {e: [] for e in self.ENGS}
        self.last_write = {}
        self.readers = {}
        self.dma_count = {e: 0 for e in self.ENGS}
        self.dma_hist = {e: [] for e in self.ENGS}
        self.cc_ops = []

    def op(self, eng, fn, reads=(), writes=(), dma=False, cc=False):
        kind = "cc" if cc else ("dma" if dma else None)
        o = Op(eng, fn, kind)
        deps = {}
        for k in reads:
            w = self.last_write.get(k)
            if w is not None:
                deps[id(w)] = (w, True)
        for k in writes:
            w = self.last_write.get(k)
            if w is not None and id(w) not in deps:
                deps[id(w)] = (w, False)
            for r in self.readers.get(k, ()):
                if id(r) not in deps:
                    deps[id(r)] = (r, False)
        for d, raw in deps.values():
            if d is o:
                continue
            if d.kind is None and kind is None and d.eng == eng:
                if eng == "pe" or (eng in ("dve", "act") and not raw):
                    continue
            o.deps.append(d)
            if d.kind is None:
                d.signal = True
        for k in reads:
            self.readers.setdefault(k, []).append(o)
        for k in writes:
            self.last_write[k] = o
            self.readers[k] = []
        if kind == "dma":
            n = self.NDMA[eng]
            i = self.dma_count[eng]
            self.dma_count[eng] += 1
            o.idx = i
            hist = self.dma_hist[eng]
            if i >= n:
                o.prev_dma = hist[i - n]
            hist.append(o)
        elif kind == "cc":
            o.idx = len(self.cc_ops)
            self.cc_ops.append(o)
        self.ops[eng].append(o)
        return o

    def fence(self, engs=("pe", "act", "dve", "pool", "sp")):
        deps = []
        for e in ("pe", "act", "dve", "pool"):
            comp = [o for o in self.ops[e] if o.kind is None]
            if comp:
                deps.append(comp[-1])
        for e, n in self.NDMA.items():
            deps += self.dma_hist[e][-n:]
        deps += self.cc_ops[-1:]
        for e in engs:
            o = Op(e, lambda eng: eng.nop(), None)
            o.deps = list(deps)
            for d in o.deps:
                if d.kind is None:
                    d.signal = True
            self.ops[e].append(o)

    def emit(self):
        nc = self.nc
        sems = {e: nc.alloc_semaphore("c_" + e) for e in ("pe", "act", "dve", "pool")}
        dsems = {e: [nc.alloc_semaphore(f"d_{e}{i}") for i in range(n)] for e, n in self.NDMA.items()}
        ccsem = nc.alloc_semaphore("ccsem")
        for e in self.ENGS:
            c = 0
            for o in self.ops[e]:
                if o.kind == "dma":
                    n = self.NDMA[e]
                    o.dsem = dsems[e][o.idx % n]
                    o.dval = 16 * (o.idx // n + 1)
                elif o.kind == "cc":
                    o.dsem = ccsem
                    o.dval = o.idx + 1
                elif o.signal:
                    c += 1
                    o.sig_val = c
        engobj = {"pe": "tensor", "act": "scalar", "dve": "vector", "pool": "gpsimd", "sp": "sync"}
        with nc.Block() as block:
            for e in self.ENGS:
                ops = self.ops[e]

                def body(eng, e=e, ops=ops):
                    waited = {}
                    for o in ops:
                        need = []
                        for d in o.deps:
                            if d.kind is not None:
                                need.append((d.dsem, d.dval))
                            else:
                                need.append((sems[d.eng], d.sig_val))
                        if o.prev_dma is not None:
                            need.append((o.prev_dma.dsem, o.prev_dma.dval))
                        for s, v in need:
                            k = id(s)
                            if waited.get(k, 0) >= v:
                                continue
                            waited[k] = v
                            eng.wait_ge(s, v)
                        ins = o.fn(eng)
                        if o.kind == "dma":
                            ins.then_inc(o.dsem, 16)
                        elif o.kind == "cc":
                            ins.then_inc(o.dsem)
                        elif o.signal:
                            ins.then_inc(sems[e], 1)
                    last = {}
                    for o in ops:
                        if o.kind is not None:
                            last[id(o.dsem)] = (o.dsem, o.dval)
                    for s, v in last.values():
                        if waited.get(id(s), 0) < v:
                            eng.wait_ge(s, v)

                getattr(block, engobj[e])(body)
        return nc


def build_program(stage=99):
    P = Prog()
    nc = P.nc
    dbg = {}

    def din(name, shape, dt=F32):
        return nc.dram_tensor(name, list(shape), dt, kind="ExternalInput")

    def dout(name, shape, dt=F32):
        return nc.dram_tensor(name, list(shape), dt, kind="ExternalOutput")

    x_own = din("x_own", [128, KC, NT])
    x_halo = din("x_halo", [128, KC, 256])
    cvec = din("cvec", [128, KC, 2])
    ng_d = din("ng", [128, 5, KC])
    bmod_d = din("bmodT", [128, 2, 48])
    rope_d = din("rope", [128, 2, 768])
    amask_d = din("amask", [128, 4, 2, 128], BF16)
    ident_d = din("ident", [128, 128])
    sink_d = din("sink", [128, 16])
    ck_d = din("ck", [256, 256])
    cv_d = din("cv", [256, 256])
    w_mod = din("w_mod", [2, D, 6 * D])
    wq_d = din("wq", [D, D])
    wqp_d = din("wqp", [D, D])
    wkd_d = din("wkd", [D, 512])
    wkdp_d = din("wkdp", [D, 512])
    wkv_d = din("wkv", [D, 512])
    wo_d = din("w_o", [D, D])
    w1_d = din("mlp_w1", [2, D, 4 * D])
    w2_d = din("mlp_w2", [2, 4 * D, D])
    y_d = dout("y", [128, KC, NT])
    nk_d = dout("new_k", [512, 256])
    nv_d = dout("new_v", [512, 256])

    X = nc.alloc_sbuf_tensor("s_X", [128, KC, NT], F32)
    H = nc.alloc_sbuf_tensor("s_H", [128, KC, NT], BF16)
    BIG = nc.alloc_sbuf_tensor("s_BIG", [128, 32768], BF16)
    WB = [nc.alloc_sbuf_tensor(f"s_WB{i}", [128, 4096], BF16) for i in range(3)]
    PT = [nc.alloc_sbuf_tensor(f"s_PT{i}", [128, 5, 512], BF16) for i in range(2)]
    SCR = [nc.alloc_sbuf_tensor(f"s_SCR{i}", [128, 512], F32) for i in range(4)]
    SQ = nc.alloc_sbuf_tensor("s_SQ", [128, KC, 512], BF16)
    RSTD = nc.alloc_sbuf_tensor("s_RSTD", [128, 512], F32)
    ones_bf = nc.alloc_sbuf_tensor("s_ones_bf", [128, 128], BF16)
    ident = nc.alloc_sbuf_tensor("s_ident", [128, 128], F32)
    cs_f = nc.alloc_sbuf_tensor("s_cs_f", [128, KC, 2], F32)
    cs_b = nc.alloc_sbuf_tensor("s_cs_b", [128, KC, 2], BF16)
    ng = nc.alloc_sbuf_tensor("s_ng", [128, 5, KC], F32)
    bmod = nc.alloc_sbuf_tensor("s_bmod", [128, 2, 48], F32)
    modT = nc.alloc_sbuf_tensor("s_modT", [128, 2, 48, 2], F32)
    gs = nc.alloc_sbuf_tensor("s_gs", [128, 2, 2, KC, 2], F32)
    L0B = nc.alloc_sbuf_tensor("s_L0B", [128, 3584], F32)
    rope = L0B[:, 0:1536].rearrange("p (a t) -> p a t", a=2)
    amask = nc.alloc_sbuf_tensor("s_amask", [128, 4, 2, 128], BF16)
    sink = nc.alloc_sbuf_tensor("s_sink", [128, 16], F32)
    esink = nc.alloc_sbuf_tensor("s_esink", [128, 16], F32)
    XH = nc.alloc_sbuf_tensor("s_XH", [128, KC, 256], F32)
    HH = nc.alloc_sbuf_tensor("s_HH", [128, KC, 256], BF16)
    CKV = L0B[:, 1536:2560].rearrange("p (a b n) -> p a b n", a=2, b=2)
    CKD = nc.alloc_sbuf_tensor("s_CKD", [128, 2, 4, 2, 64], BF16)
    ident_bf = nc.alloc_sbuf_tensor("s_ident_bf", [128, 128], BF16)
    KVO = L0B[:, 2560:3584].rearrange("p (a n) -> p a n", a=2)
    DEN = [nc.alloc_sbuf_tensor(f"s_DEN{i}", [128, 512], F32) for i in range(2)]
    RELU = [nc.alloc_sbuf_tensor(f"s_RELU{i}", [128, 512], BF16) for i in range(2)]
    PS = [nc.alloc_psum_tensor(f"p_PS{i}", [128, 512], F32) for i in range(8)]

    QT = BIG[:, 0:8192].rearrange("p (c t) -> p c t", c=8)
    KT = BIG[:, 8192:14336].rearrange("p (c t) -> p c t", c=4)
    VA = BIG[:, 14336:21504].rearrange("p (b k e) -> p b k e", b=14, k=4)
    ATT = BIG[:, 21504:29696].rearrange("p (c t) -> p c t", c=8)
    HID = BIG[:, 0:32768].rearrange("p (c t) -> p c t", c=32)

    ps_rr = [0]
    ps_mod = [8]

    def next_ps():
        i = ps_rr[0] % ps_mod[0]
        ps_rr[0] += 1
        return PS[i], ("ps", i)

    wb_rr = [0]

    def load_w(src_ap, shape_str=None, **kw):
        i = wb_rr[0] % 3
        wb_rr[0] += 1
        n = 1
        for s in src_ap.shape[1:]:
            n *= s
        dst = WB[i][:, 0:n]
        if shape_str is not None:
            dst = dst.rearrange(shape_str, **kw)
        P.op("pool", lambda e: e.dma_start(out=dst, in_=src_ap), writes=[("wb", i)], dma=True)
        return dst, ("wb", i)

    def dma_in(dst, src, key, eng="sp"):
        P.op(eng, lambda e: e.dma_start(out=dst, in_=src), writes=[key], dma=True)

    dma_in(X[:], x_own.ap(), "X_all")
    dma_in(XH[:], x_halo.ap(), "XH")
    dma_in(cs_f[:], cvec.ap(), "cs_f")
    dma_in(ng[:], ng_d.ap(), "ng")
    dma_in(bmod[:], bmod_d.ap(), "bmod")
    dma_in(rope, rope_d.ap(), "rope")
    dma_in(amask[:], amask_d.ap(), "amask")
    dma_in(ident[:], ident_d.ap(), "ident")
    dma_in(sink[:], sink_d.ap(), "sink")
    dma_in(CKV[:, :, 0, :], ck_d.ap().rearrange("(b p) n -> p b n", p=128), "CK")
    dma_in(CKV[:, :, 1, :], cv_d.ap().rearrange("(b p) n -> p b n", p=128), "CV")
    P.op("dve", lambda e: e.memset(ones_bf[:], 1.0), writes=["ones"])
    P.op("dve", lambda e: e.tensor_copy(ident_bf[:], ident[:]), reads=["ident"], writes=["ident_bf"])
    P.op("act", lambda e: e.activation(esink[:], sink[:], AF.Exp), reads=["sink"], writes=["esink"])
    P.op("act", lambda e: e.activation(cs_b[:], cs_f[:], AF.Silu), reads=["cs_f"], writes=["cs_b"])
    xkeys = [("X", kc, h) for kc in range(KC) for h in range(2)]
    for k in xkeys:
        P.last_write[k] = P.last_write["X_all"]

    def modulation_gen(i):
        psm = PS[7 - i]
        for piece in range(6):
            pk = ("psm", i, piece)
            for hb in range(2):
                cb = piece * 2 + hb
                wv, wk = load_w(w_mod.ap()[i].rearrange("(k p) n -> p k n", p=128)[:, :, cb * 512:(cb + 1) * 512],
                                "p (k n) -> p k n", k=KC)
                for cc in range(4):
                    j = cb * 4 + cc
                    for kc in range(KC):
                        P.op("pe", lambda e, j=j, kc=kc, cc=cc, wv=wv: e.matmul(
                            psm[:, j * 2:(j + 1) * 2], wv[:, kc, cc * 128:(cc + 1) * 128], cs_b[:, kc, :],
                            start=(kc == 0), stop=(kc == KC - 1)),
                            reads=[wk, "cs_b"], writes=[pk])
                if hb == 0:
                    yield
            bm = bass.AP(tensor=bmod, offset=i * 48 + piece * 8, ap=[[96, 128], [1, 8], [0, 2]])
            P.op("dve", lambda e, piece=piece, bm=bm: e.tensor_tensor(
                modT[:, i, piece * 8:(piece + 1) * 8, :], psm[:, piece * 16:(piece + 1) * 16].rearrange("p (j s) -> p j s", s=2), bm, ALU.add),
                reads=[pk, "bmod"], writes=[("modT", i, piece)])
            if piece in (1, 4):
                w = (piece - 1) // 3
                sc = modT[:, i, piece * 8:(piece + 1) * 8, :]
                gv = bass.AP(tensor=ng, offset=(2 * i + w) * KC, ap=[[5 * KC, 128], [1, KC], [0, 2]])
                P.op("dve", lambda e, w=w, sc=sc: e.tensor_scalar(gs[:, i, w], sc, 1.0, 32.0, ALU.add, ALU.mult),
                     reads=[("modT", i, piece)], writes=[("gs0", i, w)])
                P.op("dve", lambda e, w=w, gv=gv: e.tensor_tensor(gs[:, i, w], gs[:, i, w], gv, ALU.mult),
                     reads=[("gs0", i, w), "ng"], writes=[("gs", i, w)])
            yield

    def drain(g):
        for _ in g:
            pass

    def mod_piece(i, piece, kc, s):
        return modT[:, i, piece * 8 + kc, s:s + 1]

    def norm_mod(src, dst, t0, T, i, w, s, skey, dkey, final=False):
        ps, pk = next_ps()
        for kc in range(KC):
            P.op("act", lambda e, kc=kc: e.activation(SQ[:, kc, 0:T], src[:, kc, t0:t0 + T], AF.Square),
                 reads=[skey(kc)], writes=[("SQ", kc)])
        for kc in range(KC):
            P.op("pe", lambda e, kc=kc: e.matmul(ps[:, 0:T], ones_bf[:], SQ[:, kc, 0:T], start=(kc == 0), stop=(kc == KC - 1)),
                 reads=[("SQ", kc), "ones"], writes=[pk])
        NV = int(os.environ.get("NORMVAR", "9"))
        if NV < 2:
            return
        P.op("act", lambda e: e.activation(RSTD[:, 0:T], ps[:, 0:T], AF.Ln, bias=epsb[:, 0:1]), reads=[pk, "epsb"], writes=["RSTD"])
        P.op("act", lambda e: e.activation(RSTD[:, 0:T], RSTD[:, 0:T], AF.Exp, scale=-0.5), reads=["RSTD"], writes=["RSTD"])
        if NV < 3:
            return
        for kc in range(KC):
            sc = SCR[kc % 4]
            sk = ("SCR", kc % 4)
            P.op("dve", lambda e, kc=kc, sc=sc: e.tensor_tensor(sc[:, 0:T], src[:, kc, t0:t0 + T], RSTD[:, 0:T], ALU.mult),
                 reads=[skey(kc), "RSTD"], writes=[sk])
            if NV < 4:
                continue
            if final:
                P.op("dve", lambda e, kc=kc, sc=sc: e.tensor_scalar(dst[:, kc, 0:T], sc[:, 0:T], fng[:, kc:kc + 1], None, ALU.mult),
                     reads=[sk, "fng"], writes=[dkey(kc)])
            else:
                P.op("dve", lambda e, kc=kc, sc=sc: e.tensor_scalar(
                    dst[:, kc, t0:t0 + T], sc[:, 0:T], gs[:, i, w, kc, s:s + 1], mod_piece(i, 3 * w, kc, s), ALU.mult, ALU.add),
                    reads=[sk, ("gs", i, w), ("modT", i, 3 * w)], writes=[dkey(kc)])

    epsb = nc.alloc_sbuf_tensor("s_epsb", [128, 1], F32)
    P.op("dve", lambda e: e.memset(epsb[:], D * EPS), writes=["epsb"])
    fng = nc.alloc_sbuf_tensor("s_fng", [128, KC], F32)
    P.op("dve", lambda e: e.tensor_scalar(fng[:], ng[:, 4, :], 32.0, None, ALU.mult), reads=["ng"], writes=["fng"])

    def gated_residual(ps, pk, oc, h, i, piece):
        s = h
        P.op("dve", lambda e: e.scalar_tensor_tensor(
            X[:, oc, h * 512:(h + 1) * 512], ps[:, :], mod_piece(i, piece, oc, s), X[:, oc, h * 512:(h + 1) * 512],
            ALU.mult, ALU.add), reads=[pk, ("modT", i, piece), ("X", oc, h)], writes=[("X", oc, h)])

    def xkey_h(h):
        return lambda kc: ("X", kc, h)

    def hkey_h(h):
        return lambda kc: ("H", kc, h)

    def mlp(i, gen=None):
        for h in range(2):
            norm_mod(X, H, h * 512, 512, i, 1, h, xkey_h(h), hkey_h(h))
        w1v = w1_d.ap()[i].rearrange("(k p) n -> p k n", p=128)
        for hg in range(8):
            if gen is not None:
                next(gen, None)
            wv, wk = load_w(w1v[:, :, hg * 512:(hg + 1) * 512], "p (k n) -> p k n", k=KC)
            for cc in range(4):
                hc = hg * 4 + cc
                for h in range(2):
                    ps, pk = next_ps()
                    for kc in range(KC):
                        P.op("pe", lambda e, kc=kc, cc=cc, h=h, wv=wv, ps=ps: e.matmul(
                            ps[:, :], wv[:, kc, cc * 128:(cc + 1) * 128], H[:, kc, h * 512:(h + 1) * 512],
                            start=(kc == 0), stop=(kc == KC - 1)), reads=[wk, ("H", kc, h)], writes=[pk])
                    r = RELU[(hc * 2 + h) % 2]
                    rk = ("RELU", (hc * 2 + h) % 2)
                    P.op("act", lambda e, ps=ps, r=r: e.activation(r[:], ps[:, :], AF.Relu), reads=[pk], writes=[rk])
                    P.op("dve", lambda e, r=r, hc=hc, h=h: e.tensor_tensor(HID[:, hc, h * 512:(h + 1) * 512], r[:], r[:], ALU.mult),
                         reads=[rk], writes=[("HID", hc, h)])
        w2v = w2_d.ap()[i].rearrange("(k p) n -> p k n", p=128)
        for oc in range(KC):
            if gen is not None:
                next(gen, None)
            wv, wk = load_w(w2v[:, :, oc * 128:(oc + 1) * 128], "p (k n) -> p k n", k=32)
            for h in range(2):
                ps, pk = next_ps()
                for hc in range(32):
                    P.op("pe", lambda e, hc=hc, h=h, wv=wv, ps=ps: e.matmul(
                        ps[:, :], wv[:, hc, :], HID[:, hc, h * 512:(h + 1) * 512],
                        start=(hc == 0), stop=(hc == 31)), reads=[wk, ("HID", hc, h)], writes=[pk])
                gated_residual(ps, pk, oc, h, i, 5)

    def finish():
        for h in range(2):
            P.op("sp", lambda e, h=h: e.dma_start(out=y_d.ap()[:, :, h * 512:(h + 1) * 512], in_=X[:, :, h * 512:(h + 1) * 512]),
                 reads=[("X", kc, h) for kc in range(KC)], writes=[("y_out", h)], dma=True)
        return P.emit()

    ps_mod[0] = 6
    g0 = modulation_gen(0)
    for _ in range(4):
        next(g0)
    if stage == 1:
        return finish()
    for h in range(2):
        norm_mod(X, H, h * 512, 512, 0, 0, h, xkey_h(h), hkey_h(h))
    norm_mod(XH, HH, 0, 256, 0, 0, 1, lambda kc: "XH", lambda kc: ("HH", kc))

    if stage == 2:
        return finish()

    def proj_fm(wd, col0, jobs):
        wv, wk = load_w(wd.ap().rearrange("(k p) n -> p k n", p=128)[:, :, col0:col0 + 128], "p (k n) -> p k n", k=KC)
        for (ps, pk, off, n, rf, rkeys) in jobs:
            for kc in range(KC):
                P.op("pe", lambda e, kc=kc, off=off, n=n, rf=rf, ps=ps: e.matmul(
                    ps[:, off:off + n], wv[:, kc, :], rf(kc), start=(kc == 0), stop=(kc == KC - 1)),
                    reads=[wk, rkeys(kc)], writes=[pk])

    rhs_p = (0, 512, lambda kc: H[:, kc, 0:512], lambda kc: ("H", kc, 0))
    rhs_s = (0, 512, lambda kc: H[:, kc, 512:1024], lambda kc: ("H", kc, 1))
    rhs_hl = (0, 256, lambda kc: HH[:, kc, 0:256], lambda kc: ("HH", kc))

    def rope_evac(ps1, pk1, ps2, pk2, n, tab0, dst, dkey):
        a, ak = SCR[0], ("SCR", 0)
        b, bk = SCR[1], ("SCR", 1)
        P.op("dve", lambda e: e.tensor_tensor(a[:, 0:n], ps1[:, 0:n], rope[:, 0, tab0:tab0 + n], ALU.mult),
             reads=[pk1, "rope"], writes=[ak])
        P.op("dve", lambda e: e.tensor_tensor(b[:, 0:n], ps2[:, 0:n], rope[:, 1, tab0:tab0 + n], ALU.mult),
             reads=[pk2, "rope"], writes=[bk])
        P.op("dve", lambda e: e.tensor_tensor(dst, a[:, 0:n], b[:, 0:n], ALU.add), reads=[ak, bk], writes=[dkey])

    for c in range(8):
        ps, pk = next_ps()
        ps1, pk1 = next_ps()
        proj_fm(wq_d, c * 128, [(ps, pk) + rhs_p, (ps1, pk1) + rhs_s])
        P.op("act", lambda e, ps=ps, c=c: e.activation(QT[:, c, 0:512], ps[:, :], AF.Copy), reads=[pk], writes=[("QT", c, 0)])
        ps2, pk2 = next_ps()
        proj_fm(wqp_d, c * 128, [(ps2, pk2) + rhs_s])
        rope_evac(ps1, pk1, ps2, pk2, 512, 128, QT[:, c, 512:1024], ("QT", c, 1))
    for kv in range(4):
        ps, pk = next_ps()
        ps1, pk1 = next_ps()
        ps3, pk3 = next_ps()
        proj_fm(wkd_d, kv * 128, [(ps, pk) + rhs_p, (ps1, pk1) + rhs_s, (ps3, pk3) + rhs_hl])
        P.op("act", lambda e, ps=ps, kv=kv: e.activation(KT[:, kv, 0:512], ps[:, :], AF.Copy), reads=[pk], writes=[("KT", kv, 0)])
        ps2, pk2 = next_ps()
        ps4, pk4 = next_ps()
        proj_fm(wkdp_d, kv * 128, [(ps2, pk2) + rhs_s, (ps4, pk4) + rhs_hl])
        rope_evac(ps1, pk1, ps2, pk2, 512, 128, KT[:, kv, 640:1152], ("KT", kv, 1))
        ps1, pk1, ps2, pk2 = ps3, pk3, ps4, pk4
        rope_evac(ps1, pk1, ps2, pk2, 128, 0, KT[:, kv, 512:640], ("KT", kv, 2))
        a, ak = SCR[2], ("SCR", 2)
        b, bk = SCR[3], ("SCR", 3)
        P.op("dve", lambda e, ps1=ps1, a=a: e.tensor_tensor(a[:, 0:128], ps1[:, 128:256], rope[:, 0, 640:768], ALU.mult),
             reads=[pk1, "rope"], writes=[ak])
        P.op("dve", lambda e, ps2=ps2, b=b: e.tensor_tensor(b[:, 0:128], ps2[:, 128:256], rope[:, 1, 640:768], ALU.mult),
             reads=[pk2, "rope"], writes=[bk])
        P.op("dve", lambda e, kv=kv, a=a, b=b: e.tensor_tensor(KT[:, kv, 1152:1280], a[:, 0:128], b[:, 0:128], ALU.add),
             reads=[ak, bk], writes=[("KT", kv, 3)])
    if stage == 3:
        return finish()
    S4 = int(os.environ.get("S4VAR", "0"))
    for kb in range(0 if S4 == 1 else 2):
        for dup in range(2):
            P.op("dve", lambda e, kb=kb, dup=dup: e.tensor_copy(
                CKD[:, kb, :, dup, :], CKV[:, kb, 0, :].rearrange("p (k d) -> p k d", k=4)),
                reads=["CK"], writes=[("CKD", kb, dup)])
        for kv in range(4):
            ps, pk = next_ps()
            P.op("pe", lambda e, kb=kb, kv=kv, ps=ps: e.matmul(
                ps[:, 0:128], CKD[:, kb, kv].rearrange("p a d -> p (a d)"), ident_bf[:], start=True, stop=True),
                reads=[("CKD", kb, 0), ("CKD", kb, 1), "ident_bf"], writes=[pk])
            P.op("act", lambda e, kb=kb, kv=kv, ps=ps: e.activation(KT[:, kv, 1280 + kb * 128:1408 + kb * 128], ps[:, 0:128], AF.Copy),
                 reads=[pk], writes=[("KT", kv, 4 + kb)])
    wkv_v, wkv_k = load_w(wkv_d.ap().rearrange("(k p) n -> p k n", p=128), "p (k n) -> p k n", k=KC)
    vblocks = [(b, (lambda kc, b=b: H[:, kc, b * 128:(b + 1) * 128]), (lambda kc, b=b: ("H", kc, b // 4)), b if b < 4 else b + 1)
               for b in range(8)]
    vblocks += [(8, (lambda kc: HH[:, kc, 0:128]), (lambda kc: ("HH", kc)), 4),
                (9, (lambda kc: HH[:, kc, 128:256]), (lambda kc: ("HH", kc)), 9)]
    for (b, lf, lk, vb) in (vblocks if S4 != 2 else []):
        ps, pk = next_ps()
        for kc in range(KC):
            P.op("pe", lambda e, kc=kc, lf=lf, ps=ps: e.matmul(ps[:, :], lf(kc), wkv_v[:, kc, :], start=(kc == 0), stop=(kc == KC - 1)),
                 reads=[wkv_k, lk(kc)], writes=[pk])
        for dup in range(2):
            eng = "dve" if dup == 0 else "pool"
            if eng == "pool":
                continue
        for dup in range(2):
            P.op("act", lambda e, vb=vb, ps=ps, dup=dup: e.activation(
                VA[:, vb, :, dup * 64:(dup + 1) * 64], ps[:, 256:512].rearrange("p (k d) -> p k d", k=4), AF.Copy),
                reads=[pk], writes=[("VA", vb, dup)])
        if b < 4 and S4 != 3:
            P.op("act", lambda e, b=b, ps=ps: e.activation(KVO[:, b % 2, :], ps[:, :], AF.Copy), reads=[pk], writes=[("KVO", b % 2)])
            P.op("sp", lambda e, b=b: e.dma_start(out=nk_d.ap()[b * 128:(b + 1) * 128, :], in_=KVO[:, b % 2, 0:256]),
                 reads=[("KVO", b % 2)], writes=[("nk_out", b)], dma=True)
            P.op("sp", lambda e, b=b: e.dma_start(out=nv_d.ap()[b * 128:(b + 1) * 128, :], in_=KVO[:, b % 2, 256:512]),
                 reads=[("KVO", b % 2)], writes=[("nv_out", b)], dma=True)
    for kb in range(2):
        for dup in range(2):
            P.op("dve", lambda e, kb=kb, dup=dup: e.tensor_copy(
                VA[:, 10 + kb, :, dup * 64:(dup + 1) * 64], CKV[:, kb, 1, :].rearrange("p (k d) -> p k d", k=4)),
                reads=["CV"], writes=[("VA", 10 + kb, dup)])
    if stage == 4:
        return finish()
    def attention(qtok, half, keyblocks, pbuf):
        for kv in range(4):
            pt = PT[kv % 2]
            pbuf = kv % 2
            nkb = len(keyblocks)
            for j, (ktcol, ktk, vb, mside, qb) in enumerate(keyblocks):
                for hf in range(2):
                    ps, pk = next_ps()
                    P.op("pe", lambda e, hf=hf, kv=kv, ktcol=ktcol, ps=ps: e.matmul(
                        ps[:, 0:256].rearrange("p (c q) -> p c q", c=2),
                        KT[hf * 64:(hf + 1) * 64, kv, ktcol:ktcol + 128],
                        QT[hf * 64:(hf + 1) * 64, 2 * kv:2 * kv + 2, qtok:qtok + 128], start=True, stop=True),
                        reads=[("KT", kv, ktk), ("QT", 2 * kv, half), ("QT", 2 * kv + 1, half)], writes=[pk])
                    P.op("act", lambda e, j=j, hf=hf, ps=ps, pt=pt: e.activation(pt[:, j, hf * 256:(hf + 1) * 256], ps[:, 0:256], AF.Exp, scale=0.125),
                         reads=[pk], writes=[("PT", pbuf, j, hf)])
                if mside is not None:
                    P.op("dve", lambda e, j=j, pt=pt, qb=qb, mside=mside: e.tensor_tensor(
                        pt[:, j, :].rearrange("p (a q) -> p a q", a=4), pt[:, j, :].rearrange("p (a q) -> p a q", a=4),
                        amask[:, qb, mside, :].unsqueeze(1).broadcast_to([128, 4, 128]), ALU.mult),
                        reads=[("PT", pbuf, j, 0), ("PT", pbuf, j, 1), "amask"], writes=[("PT", pbuf, j, 0), ("PT", pbuf, j, 1)])
            pso, pko = next_ps()
            psl, pkl = next_ps()
            for j, (ktcol, ktk, vb, mside, qb) in enumerate(keyblocks):
                P.op("pe", lambda e, j=j, vb=vb, kv=kv, pso=pso, pt=pt, nkb=nkb: e.matmul(pso[:, :], VA[:, vb, kv, :], pt[:, j, :], start=(j == 0), stop=(j == nkb - 1)),
                     reads=[("VA", vb, 0), ("VA", vb, 1), ("PT", pbuf, j, 0), ("PT", pbuf, j, 1)], writes=[pko])
            for j in range(nkb):
                P.op("pe", lambda e, j=j, psl=psl, pt=pt, nkb=nkb: e.matmul(psl[:, :], ones_bf[:], pt[:, j, :], start=(j == 0), stop=(j == nkb - 1)),
                     reads=["ones", ("PT", pbuf, j, 0), ("PT", pbuf, j, 1)], writes=[pkl])
            den = DEN[kv % 2]
            dk = ("DEN", kv % 2)
            es = bass.AP(tensor=esink, offset=4 * kv, ap=[[16, 128], [1, 2], [2, 2], [0, 128]])
            P.op("dve", lambda e, es=es, den=den, psl=psl: e.tensor_tensor(
                den[:, :].rearrange("p (h c q) -> p h c q", h=2, c=2), psl[:, :].rearrange("p (h c q) -> p h c q", h=2, c=2), es, ALU.add),
                reads=[pkl, "esink"], writes=[dk])
            P.op("dve", lambda e, den=den: e.reciprocal(den[:, :], den[:, :]), reads=[dk], writes=[dk])
            for hf in range(2):
                P.op("dve", lambda e, hf=hf, kv=kv, den=den, pso=pso: e.tensor_tensor(
                    ATT[hf * 64:(hf + 1) * 64, 2 * kv:2 * kv + 2, qtok:qtok + 128],
                    pso[hf * 64:(hf + 1) * 64, hf * 256:(hf + 1) * 256].rearrange("p (c q) -> p c q", c=2),
                    den[hf * 64:(hf + 1) * 64, hf * 256:(hf + 1) * 256].rearrange("p (c q) -> p c q", c=2), ALU.mult),
                    reads=[pko, dk], writes=[("ATT", 2 * kv, half), ("ATT", 2 * kv + 1, half)])

    pb = 0
    for s in range(2):
        for qb in range(2):
            kbs = [(s * 256 + j * 128, 0, 2 * s + j, None, 0) for j in range(2)]
            attention(s * 256 + qb * 128, 0, kbs, pb % 2)
            next(g0, None)
            pb += 1
    for qb in range(4):
        kbs = []
        for j in range(3):
            eb = qb + j
            ktk = 2 if eb == 0 else (3 if eb == 5 else 1)
            kbs.append((512 + eb * 128, ktk, 4 + eb, (0 if j == 0 else (1 if j == 2 else None)), qb))
        for kb in range(2):
            kbs.append((1280 + kb * 128, 4 + kb, 10 + kb, None, qb))
        attention(512 + qb * 128, 1, kbs, pb % 2)
        next(g0, None)
        pb += 1

    if stage == 5:
        return finish()
    drain(g0)
    for oc in range(KC):
        wv, wk = load_w(wo_d.ap().rearrange("(k p) n -> p k n", p=128)[:, :, oc * 128:(oc + 1) * 128], "p (k n) -> p k n", k=KC)
        for h in range(2):
            ps, pk = next_ps()
            for kc in range(KC):
                P.op("pe", lambda e, kc=kc, h=h, wv=wv, ps=ps: e.matmul(
                    ps[:, :], wv[:, kc, :], ATT[:, kc, h * 512:(h + 1) * 512], start=(kc == 0), stop=(kc == KC - 1)),
                    reads=[wk, ("ATT", kc, h)], writes=[pk])
            gated_residual(ps, pk, oc, h, 0, 2)
    if stage == 6:
        return finish()
    g1 = modulation_gen(1)
    mlp(0, g1)
    drain(g1)

    if stage == 7:
        return finish()
    P.fence()
    ssm_sc_d = din("ssm_sc", [128, 4, 64])
    s0_d = din("s0", [128, 2, 64])
    bpad_d = din("bpad", [8, 128, 2048])
    cpad_d = din("cpad", [8, 128, 2048])
    dskip_d = din("dskip", [128, KC])
    qsel_d = din("qsel", [128, 4])
    wa_d = din("glu_w_a", [D, D])
    wb_d = din("glu_w_b", [D, D])
    news_d = dout("new_s", [128, 2, 64, 2])
    g_in = nc.dram_tensor("g_in", [128, 128], F32)
    g_out = nc.dram_tensor("g_out", [512, 128], F32)
    tab_d = nc.dram_tensor("tab_scratch", [8, 128, 8192], F32)

    NTAB = 40
    TB = L0B[:, 0:2560].rearrange("p (a m) -> p a m", a=NTAB)
    CF = L0B[:, 2560:3584].bitcast(BF16)
    L0ROW = 3584

    def tb_ap(off, dims):
        return bass.AP(tensor=L0B, offset=off, ap=[[L0ROW, 128]] + dims)

    P0F = PT[0][:].rearrange("p a b -> p (a b)").bitcast(F32)
    P1F = PT[1][:].rearrange("p a b -> p (a b)").bitcast(F32)
    S0 = P0F[:, 0:128].rearrange("p (a m) -> p a m", a=2)
    FIN1 = P0F[:, 128:256].rearrange("p (a m) -> p a m", a=2)
    FIN2 = P0F[:, 256:512].rearrange("p (a m s) -> p a m s", a=2, s=2)
    NS = P0F[:, 512:768].rearrange("p (a m s) -> p a m s", a=2, s=2)
    TF = P0F[:, 768:896].rearrange("p (a m) -> p a m", a=2)
    ST = P0F[:, 896:1024].rearrange("p (a m) -> p a m", a=2)
    WI = P0F[:, 1024:1152].rearrange("p (a m) -> p a m", a=2)
    GG = P1F[:, 0:512].rearrange("p (q a m) -> p q a m", q=4, a=2)
    CH = P1F[:, 512:1024].rearrange("p (q a m) -> p q a m", q=4, a=2)
    NT1 = P1F[:, 1024:1152].rearrange("p (m s) -> p m s", s=2)
    NT2 = P1F[:, 1152:1280].rearrange("p (m s) -> p m s", s=2)
    hpi = nc.alloc_sbuf_tensor("s_hpi", [128, 1], F32)
    dskip = nc.alloc_sbuf_tensor("s_dskip", [128, KC], F32)
    qsel = nc.alloc_sbuf_tensor("s_qsel", [128, 4], F32)
    FTMP = nc.alloc_sbuf_tensor("s_FTMP", [128, 4], F32)
    fence_keys = [("X", kc, h) for kc in range(KC) for h in range(2)]
    (T_LR, T_TH, T_DT, T_R, T_R512, T_FRE, T_FIM, T_IFRE, T_IFIM, T_P1RE, T_P1IM, T_T1, T_T2, T_T3, T_T4) = range(15)
    T_CKC, T_CKS, T_LAMRE, T_LAMIM, T_LOGDT = 15, 25, 35, 36, 37

    def tb(i):
        return TB[:, i, :]

    tbk = lambda i: ("TB", i)

    def vop(eng, fn, reads, writes):
        P.op(eng, fn, reads=reads, writes=writes)

    def tt(eng, out, a, b, op, rk, wk):
        vop(eng, lambda e: e.tensor_tensor(out, a, b, op), rk, wk)

    P.op("sp", lambda e: e.dma_start(out=TB[:, T_LAMRE:T_LAMRE + 4, :], in_=ssm_sc_d.ap()), reads=fence_keys, writes=["ssm_sc"], dma=True)
    for i in range(T_LAMRE, T_LAMRE + 4):
        P.last_write[tbk(i)] = P.last_write["ssm_sc"]
    P.op("sp", lambda e: e.dma_start(out=S0, in_=s0_d.ap()), reads=fence_keys, writes=["S0"], dma=True)
    dma_in(dskip[:], dskip_d.ap(), "dskip")
    dma_in(qsel[:], qsel_d.ap(), "qsel")
    P.op("dve", lambda e: e.memset(hpi[:], math.pi / 2), writes=["hpi"])
    P.op("act", lambda e: e.activation(tb(T_DT), tb(T_LOGDT), AF.Exp), reads=[tbk(T_LOGDT)], writes=[tbk(T_DT)])
    tt("dve", tb(T_LR), tb(T_LAMRE), tb(T_DT), ALU.mult, [tbk(T_LAMRE), tbk(T_DT)], [tbk(T_LR)])
    tt("dve", tb(T_TH), tb(T_LAMIM), tb(T_DT), ALU.mult, [tbk(T_LAMIM), tbk(T_DT)], [tbk(T_TH)])
    P.op("act", lambda e: e.activation(tb(T_R), tb(T_LR), AF.Exp), reads=[tbk(T_LR)], writes=[tbk(T_R)])
    P.op("act", lambda e: e.activation(tb(T_R512), tb(T_LR), AF.Exp, scale=512.0), reads=[tbk(T_LR)], writes=[tbk(T_R512)])
    P.op("act", lambda e: e.activation(tb(T_CKS), tb(T_TH), AF.Sin, scale=1.0 / 64), reads=[tbk(T_TH)], writes=[tbk(T_CKS)])
    P.op("act", lambda e: e.activation(tb(T_CKC), tb(T_TH), AF.Sin, scale=1.0 / 64, bias=hpi[:, 0:1]), reads=[tbk(T_TH), "hpi"], writes=[tbk(T_CKC)])

    def square(ci, si, co, so):
        tt("dve", tb(T_T1), tb(ci), tb(ci), ALU.mult, [tbk(ci)], [tbk(T_T1)])
        tt("dve", tb(T_T2), tb(si), tb(si), ALU.mult, [tbk(si)], [tbk(T_T2)])
        vop("dve", lambda e: e.scalar_tensor_tensor(tb(so), tb(ci), 2.0, tb(si), ALU.mult, ALU.mult), [tbk(ci), tbk(si)], [tbk(so)])
        tt("dve", tb(co), tb(T_T1), tb(T_T2), ALU.subtract, [tbk(T_T1), tbk(T_T2)], [tbk(co)])

    for _ in range(6):
        square(T_CKC, T_CKS, T_CKC, T_CKS)
    for k in range(9):
        square(T_CKC + k, T_CKS + k, T_CKC + k + 1, T_CKS + k + 1)

    def cmul(eng, ore, oim, are, aim, bre, bim, rk, wk, t1, t2, tk1, tk2):
        tt(eng, t1, are, bre, ALU.mult, rk, [tk1])
        tt(eng, t2, aim, bim, ALU.mult, rk, [tk2])
        tt(eng, ore, t1, t2, ALU.subtract, [tk1, tk2], [wk[0]])
        tt(eng, t1, are, bim, ALU.mult, rk, [tk1])
        tt(eng, t2, aim, bre, ALU.mult, rk, [tk2])
        tt(eng, oim, t1, t2, ALU.add, [tk1, tk2], [wk[1]])

    tt("dve", tb(T_T3), tb(T_R), tb(T_CKC), ALU.mult, [tbk(T_R), tbk(T_CKC)], [tbk(T_T3)])
    tt("dve", tb(T_T4), tb(T_R), tb(T_CKS), ALU.mult, [tbk(T_R), tbk(T_CKS)], [tbk(T_T4)])
    vop("dve", lambda e: e.tensor_scalar(tb(T_T3), tb(T_T3), -1.0, None, ALU.add), [tbk(T_T3)], [tbk(T_T3)])
    tt("dve", tb(T_T1), tb(T_LAMRE), tb(T_LAMRE), ALU.mult, [tbk(T_LAMRE)], [tbk(T_T1)])
    tt("dve", tb(T_T2), tb(T_LAMIM), tb(T_LAMIM), ALU.mult, [tbk(T_LAMIM)], [tbk(T_T2)])
    tt("dve", tb(T_T1), tb(T_T1), tb(T_T2), ALU.add, [tbk(T_T1), tbk(T_T2)], [tbk(T_T1)])
    vop("dve", lambda e: e.reciprocal(tb(T_T1), tb(T_T1)), [tbk(T_T1)], [tbk(T_T1)])
    tt("dve", tb(T_FRE), tb(T_T3), tb(T_LAMRE), ALU.mult, [tbk(T_T3), tbk(T_LAMRE)], [tbk(T_FRE)])
    tt("dve", tb(T_T2), tb(T_T4), tb(T_LAMIM), ALU.mult, [tbk(T_T4), tbk(T_LAMIM)], [tbk(T_T2)])
    tt("dve", tb(T_FRE), tb(T_FRE), tb(T_T2), ALU.add, [tbk(T_FRE), tbk(T_T2)], [tbk(T_FRE)])
    tt("dve", tb(T_FRE), tb(T_FRE), tb(T_T1), ALU.mult, [tbk(T_FRE), tbk(T_T1)], [tbk(T_FRE)])
    tt("dve", tb(T_FIM), tb(T_T4), tb(T_LAMRE), ALU.mult, [tbk(T_T4), tbk(T_LAMRE)], [tbk(T_FIM)])
    tt("dve", tb(T_T2), tb(T_T3), tb(T_LAMIM), ALU.mult, [tbk(T_T3), tbk(T_LAMIM)], [tbk(T_T2)])
    tt("dve", tb(T_FIM), tb(T_FIM), tb(T_T2), ALU.subtract, [tbk(T_FIM), tbk(T_T2)], [tbk(T_FIM)])
    tt("dve", tb(T_FIM), tb(T_FIM), tb(T_T1), ALU.mult, [tbk(T_FIM), tbk(T_T1)], [tbk(T_FIM)])
    tt("dve", tb(T_T1), tb(T_FRE), tb(T_FRE), ALU.mult, [tbk(T_FRE)], [tbk(T_T1)])
    tt("dve", tb(T_T2), tb(T_FIM), tb(T_FIM), ALU.mult, [tbk(T_FIM)], [tbk(T_T2)])
    tt("dve", tb(T_T1), tb(T_T1), tb(T_T2), ALU.add, [tbk(T_T1), tbk(T_T2)], [tbk(T_T1)])
    vop("dve", lambda e: e.reciprocal(tb(T_T1), tb(T_T1)), [tbk(T_T1)], [tbk(T_T1)])
    tt("dve", tb(T_IFRE), tb(T_FRE), tb(T_T1), ALU.mult, [tbk(T_FRE), tbk(T_T1)], [tbk(T_IFRE)])
    vop("dve", lambda e: e.scalar_tensor_tensor(tb(T_IFIM), tb(T_FIM), -1.0, tb(T_T1), ALU.mult, ALU.mult), [tbk(T_FIM), tbk(T_T1)], [tbk(T_IFIM)])
    tt("dve", tb(T_P1RE), tb(T_R512), tb(T_CKC + 9), ALU.mult, [tbk(T_R512), tbk(T_CKC + 9)], [tbk(T_P1RE)])
    tt("dve", tb(T_P1IM), tb(T_R512), tb(T_CKS + 9), ALU.mult, [tbk(T_R512), tbk(T_CKS + 9)], [tbk(T_P1IM)])

    for h in range(2):
        norm_mod(X, H, h * 512, 512, 1, 0, h, xkey_h(h), hkey_h(h))

    BF = BIG[:].bitcast(F32)
    UC = BF[:, 0:4096].rearrange("p (a t) -> p a t", a=8)
    US = BF[:, 4096:8192].rearrange("p (a t) -> p a t", a=8)
    EW = [BF[:, 8192 + i * 512:8192 + (i + 1) * 512] for i in range(4)]
    UT = [BF[:, 8192:10240].rearrange("p (a t) -> p a t", a=8), BF[:, 10240:12288].rearrange("p (a t) -> p a t", a=8)]
    SBS = [BIG[:, 24576:25600].rearrange("p (a t) -> p a t", a=2), BIG[:, 29696:30720].rearrange("p (a t) -> p a t", a=2)]
    CFT1 = BF[:, 15360:15488]
    CFT2 = BF[:, 15488:15616]
    sb_i = [0]
    G0 = XH[:].bitcast(BF16)
    G1 = BIG[:, 25600:29696].rearrange("p (c t) -> p c t", c=8)

    def Gv(kc, h):
        return (G0 if h == 0 else G1)[:, kc, :]

    def build_tables(gh):
        P.op("dve", lambda e: e.memset(UC[:, :, 0:1], 1.0), reads=fence_keys, writes=["UC"])
        P.op("dve", lambda e: e.memset(US[:, :, 0:1], 0.0), reads=fence_keys, writes=["US"])
        for k in range(9):
            n = 1 << k
            ckc = tb_ap((T_CKC + k) * 64 + 4 * gh, [[32, 2], [1, 4], [0, n]])
            cks = tb_ap((T_CKS + k) * 64 + 4 * gh, [[32, 2], [1, 4], [0, n]])
            v4 = lambda ap: ap.rearrange("p (d m) t -> p d m t", d=2)
            sc_, ss_ = v4(UC[:, :, 0:n]), v4(US[:, :, 0:n])
            dc_, ds_ = v4(UC[:, :, n:2 * n]), v4(US[:, :, n:2 * n])
            rk = ["UC", "US", tbk(T_CKC + k), tbk(T_CKS + k)]
            if n <= 128:
                t1, t3 = v4(UT[0][:, :, 0:n]), v4(UT[0][:, :, 128:128 + n])
                t2, t4 = v4(UT[1][:, :, 0:n]), v4(UT[1][:, :, 128:128 + n])
                tt("dve", t1, sc_, ckc, ALU.mult, rk, ["UT0a"])
                tt("dve", t2, ss_, cks, ALU.mult, rk, ["UT1a"])
                tt("dve", t3, sc_, cks, ALU.mult, rk, ["UT0b"])
                tt("dve", t4, ss_, ckc, ALU.mult, rk, ["UT1b"])
                tt("dve", dc_, t1, t2, ALU.subtract, ["UT0a", "UT1a"], ["UC"])
                tt("dve", ds_, t3, t4, ALU.add, ["UT0b", "UT1b"], ["US"])
            else:
                t1, t2 = v4(UT[0][:, :, 0:n]), v4(UT[1][:, :, 0:n])
                k0, k1 = ["UT0a", "UT0b"], ["UT1a", "UT1b"]
                tt("dve", t1, sc_, ckc, ALU.mult, rk, k0)
                tt("dve", t2, ss_, cks, ALU.mult, rk, k1)
                tt("dve", dc_, t1, t2, ALU.subtract, k0 + k1, ["UC"])
                tt("dve", t1, sc_, cks, ALU.mult, rk, k0)
                tt("dve", t2, ss_, ckc, ALU.mult, rk, k1)
                tt("dve", ds_, t1, t2, ALU.add, k0 + k1, ["US"])

    def seg_ap(ap512, lo, n, rev):
        v = ap512[:, lo:lo + n]
        return v[:, ::-1] if rev else v

    RS = [BF[:, 10240 + i * 512:10240 + (i + 1) * 512] for i in range(4)]

    def hv(ap512, half):
        return ap512.rearrange("p (s t) -> p s t", s=2) if half == 0 else ap512

    def tabv(T, a, half, d):
        n = 256 if half == 0 else 512
        v = T[:, a, 0:n]
        if d == 1:
            v = v[:, ::-1]
        return v.unsqueeze(1).broadcast_to([128, 2, 256]) if half == 0 else v

    def demod(a, md, half, ps_re, pk_re, ps_im, pk_im, d):
        uc, us = tabv(UC, a, half, d), tabv(US, a, half, d)
        xr, xi = hv(ps_re[:, :], half), hv(ps_im[:, :], half)
        t = [hv(SCR[i][:, :], half) for i in range(4)]
        rk = [pk_re, pk_im, "UC", "US"]
        tt("dve", t[0], xr, uc, ALU.mult, rk, [("SCR", 0)])
        tt("dve", t[1], xi, us, ALU.mult, rk, [("SCR", 1)])
        tt("dve", t[2], xi, uc, ALU.mult, rk, [("SCR", 2)])
        tt("dve", t[3], xr, us, ALU.mult, rk, [("SCR", 3)])
        tt("dve", hv(EW[0], half), t[0], t[1], ALU.add, [("SCR", 0), ("SCR", 1)], [("EW", 0)])
        tt("dve", hv(EW[1], half), t[2], t[3], ALU.subtract, [("SCR", 2), ("SCR", 3)], [("EW", 1)])

    def scans(md, half, d, use_init):
        segs = [(0, 256), (256, 256)] if half == 0 else [(0, 512)]
        for (lo, n) in segs:
            rbc = tb_ap(T_R * 64 + md, [[0, n]])
            for ri in range(2):
                init = WI[:, ri, md:md + 1] if use_init else 0.0
                src = seg_ap(EW[ri], lo, n, d == 1)
                dst = seg_ap(EW[2 + ri], lo, n, d == 1)
                vop("dve", lambda e, dst=dst, rbc=rbc, src=src, init=init: e.tensor_tensor_scan(dst, rbc, src, init, ALU.mult, ALU.add),
                    [("EW", ri), tbk(T_R)] + (["WI"] if use_init else []), [("EW", 2 + ri)])

    def finals(a, md, half, d, dst):
        L = 256 if half == 0 else 512
        ncol = 2 if half == 0 else 1
        c0 = (L - 1) if d == 0 else 0
        wre = EW[2][:, c0::256][:, 0:ncol] if half == 0 else EW[2][:, c0:c0 + 1]
        wim = EW[3][:, c0::256][:, 0:ncol] if half == 0 else EW[3][:, c0:c0 + 1]
        ucl, usl = UC[:, a, L - 1:L], US[:, a, L - 1:L]
        tA, tB = FTMP[:, 0:ncol], FTMP[:, 2:2 + ncol]
        rk = [("EW", 2), ("EW", 3), "UC", "US"]
        vop("dve", lambda e: e.tensor_scalar(tA, wim, usl, None, ALU.mult), rk, ["FTMPA"])
        vop("dve", lambda e: e.tensor_scalar(tB, wre, usl, None, ALU.mult), rk, ["FTMPB"])
        vop("dve", lambda e: e.scalar_tensor_tensor(dst[0], wre, ucl, tA, ALU.mult, ALU.subtract), rk + ["FTMPA"], [dst[2]])
        vop("dve", lambda e: e.scalar_tensor_tensor(dst[1], wim, ucl, tB, ALU.mult, ALU.add), rk + ["FTMPB"], [dst[3]])

    def remod(a, half, d, SB, sbi):
        uc, us = tabv(UC, a, half, d), tabv(US, a, half, d)
        wr, wi = hv(EW[2], half), hv(EW[3], half)
        t = [hv(RS[i], half) for i in range(4)]
        rk = [("EW", 2), ("EW", 3), "UC", "US"]
        tt("dve", t[0], wr, uc, ALU.mult, rk, [("RS", 0)])
        tt("dve", t[1], wi, us, ALU.mult, rk, [("RS", 1)])
        tt("dve", t[2], wi, uc, ALU.mult, rk, [("RS", 2)])
        tt("dve", t[3], wr, us, ALU.mult, rk, [("RS", 3)])
        tt("dve", hv(SB[:, 0, :], half), t[0], t[1], ALU.subtract, [("RS", 0), ("RS", 1)], [("SB", sbi, 0)])
        vop("dve", lambda e: e.scalar_tensor_tensor(hv(SB[:, 1, :], half), t[2], -1.0, t[3], ALU.mult, ALU.subtract),
            [("RS", 2), ("RS", 3)], [("SB", sbi, 1)])

    def x_matmuls(bv, bk, gh, d, m4, half):
        outs = []
        for ri in range(2):
            ps, pk = next_ps()
            col = ((d * 2 + ri) * 4 + m4) * 128
            P.op("pe", lambda e, ps=ps, col=col: e.matmul(ps[:, :], bv[:, col:col + 128], H[:, gh, half * 512:(half + 1) * 512], start=True, stop=True),
                 reads=[bk, ("H", gh, half)], writes=[pk])
            outs += [ps, pk]
        return outs

    ps_mod[0] = 6
    bpv = lambda gh: bpad_d.ap()[gh]
    for gh in range(8):
        build_tables(gh)
        P.op("sp", lambda e, gh=gh: e.dma_start(out=tab_d.ap()[gh], in_=BF[:, 0:8192]), reads=["UC", "US"], writes=[("tab_d", gh)], dma=True)
        bv, bk = load_w(bpv(gh))
        for d in range(2):
            for m4 in range(4):
                a = d * 4 + m4
                md = d * 32 + 4 * gh + m4
                psr, pkr, psi, pki = x_matmuls(bv, bk, gh, d, m4, 1)
                demod(a, md, 1, psr, pkr, psi, pki, d)
                scans(md, 1, d, False)
                finals(a, md, 1, d, (FIN1[:, 0, md:md + 1], FIN1[:, 1, md:md + 1], ("FIN1", md), ("FIN1", md)))
    fin1k = [("FIN1", md) for md in range(64)]
    cmul("dve", TF[:, 0, :], TF[:, 1, :], FIN1[:, 0, :], FIN1[:, 1, :], tb(T_FRE), tb(T_FIM),
         fin1k + [tbk(T_FRE), tbk(T_FIM)], ["TF", "TF"], tb(T_T1), tb(T_T2), tbk(T_T1), tbk(T_T2))
    P.op("sp", lambda e: e.dma_start(out=g_in.ap(), in_=TF.rearrange("p a b -> p (a b)")), reads=["TF"], writes=["g_in"], dma=True)
    P.op("pool", lambda e: e.collective_compute("AllGather", ALU.bypass, replica_groups=[[0, 1, 2, 3], [4, 5, 6, 7]],
                                                  ins=[g_in.ap().opt()], outs=[g_out.ap().opt()]),
         reads=["g_in"], writes=["g_out"], cc=True)
    P.op("sp", lambda e: e.dma_start(out=GG.rearrange("p q a b -> p q (a b)"), in_=g_out.ap().rearrange("(q p) n -> p q n", p=128)),
         reads=["g_out"], writes=["GG"], dma=True)
    for d in range(2):
        sl = slice(d * 32, (d + 1) * 32)
        q0 = 0 if d == 0 else 3
        for ri in range(2):
            vop("dve", lambda e, ri=ri, sl=sl, q0=q0: e.tensor_copy(CH[:, q0, ri, sl], S0[:, ri, sl]), ["S0"], [("CH", q0, d)])
        order = [(0, 1, 0), (1, 2, 1), (2, 3, 2)] if d == 0 else [(3, 2, 3), (2, 1, 2), (1, 0, 1)]
        for (qs, qd, qg) in order:
            cmul("dve", CH[:, qd, 0, sl], CH[:, qd, 1, sl], CH[:, qs, 0, sl], CH[:, qs, 1, sl], TB[:, T_P1RE, sl], TB[:, T_P1IM, sl],
                 [("CH", qs, d), tbk(T_P1RE), tbk(T_P1IM)], [("CH", qd, d), ("CH", qd, d)], TB[:, T_T1, sl], TB[:, T_T2, sl], tbk(T_T1), tbk(T_T2))
            for ri in range(2):
                tt("dve", CH[:, qd, ri, sl], CH[:, qd, ri, sl], GG[:, qg, ri, sl], ALU.add, [("CH", qd, d), "GG"], [("CH", qd, d)])
    chk = [("CH", q, d) for q in range(4) for d in range(2)]
    for ri in range(2):
        vop("dve", lambda e, ri=ri: e.tensor_scalar(ST[:, ri, :], CH[:, 0, ri, :], qsel[:, 0:1], None, ALU.mult), chk + ["qsel"], [("ST", ri)])
        for q in range(1, 4):
            vop("dve", lambda e, ri=ri, q=q: e.scalar_tensor_tensor(ST[:, ri, :], CH[:, q, ri, :], qsel[:, q:q + 1], ST[:, ri, :], ALU.mult, ALU.add),
                chk + ["qsel", ("ST", ri)], [("ST", ri)])
    cmul("dve", TF[:, 0, :], TF[:, 1, :], ST[:, 0, :], ST[:, 1, :], tb(T_IFRE), tb(T_IFIM),
         [("ST", 0), ("ST", 1), tbk(T_IFRE), tbk(T_IFIM), "g_in"], ["TF2", "TF2"], tb(T_T1), tb(T_T2), tbk(T_T1), tbk(T_T2))
    cmul("dve", WI[:, 0, :], WI[:, 1, :], TF[:, 0, :], TF[:, 1, :], tb(T_CKC), tb(T_CKS),
         ["TF2", tbk(T_CKC), tbk(T_CKS)], ["WI", "WI"], tb(T_T1), tb(T_T2), tbk(T_T1), tbk(T_T2))

    cpv = lambda gh: cpad_d.ap()[gh]
    for gh in range(8):
        P.op("sp", lambda e, gh=gh: e.dma_start(out=BF[:, 0:8192], in_=tab_d.ap()[gh]), reads=[("tab_d", gh)], writes=["UC", "US"], dma=True)
        bv, bk = load_w(bpv(gh))
        cv_, ck_ = load_w(cpv(gh))
        for d in range(2):
            for m4 in range(4):
                md = d * 32 + 4 * gh + m4
                cre = cv_[:, ((d * 2 + 0) * 4 + m4) * 128:((d * 2 + 0) * 4 + m4 + 1) * 128]
                cim = cv_[:, ((d * 2 + 1) * 4 + m4) * 128:((d * 2 + 1) * 4 + m4 + 1) * 128]
                ore = CF[:, ((d * 2 + 0) * 4 + m4) * 128:((d * 2 + 0) * 4 + m4 + 1) * 128]
                oim = CF[:, ((d * 2 + 1) * 4 + m4) * 128:((d * 2 + 1) * 4 + m4 + 1) * 128]
                fre, fim = TB[:, T_FRE, md:md + 1], TB[:, T_FIM, md:md + 1]
                rk = [ck_, tbk(T_FRE), tbk(T_FIM)]
                vop("dve", lambda e, cim=cim, fim=fim: e.tensor_scalar(CFT1, cim, fim, None, ALU.mult), rk, ["CFT1"])
                vop("dve", lambda e, ore=ore, cre=cre, fre=fre: e.scalar_tensor_tensor(ore, cre, fre, CFT1, ALU.mult, ALU.subtract),
                    rk + ["CFT1"], [("CF", d, m4, 0)])
                vop("dve", lambda e, cim=cim, fre=fre: e.tensor_scalar(CFT2, cim, fre, None, ALU.mult), rk, ["CFT2"])
                vop("dve", lambda e, oim=oim, cre=cre, fim=fim: e.scalar_tensor_tensor(oim, cre, fim, CFT2, ALU.mult, ALU.add),
                    rk + ["CFT2"], [("CF", d, m4, 1)])
        ysl = [(PS[6], ("ps", 6)), (PS[7], ("ps", 7))]
        cnt = [0, 0]
        for d in range(2):
            for m4 in range(4):
                a = d * 4 + m4
                md = d * 32 + 4 * gh + m4
                for half in range(2):
                    psr, pkr, psi, pki = x_matmuls(bv, bk, gh, d, m4, half)
                    demod(a, md, half, psr, pkr, psi, pki, d)
                    scans(md, half, d, half == 1)
                    sbi = sb_i[0] % 2
                    sb_i[0] += 1
                    SB = SBS[sbi]
                    remod(a, half, d, SB, sbi)
                    if half == 0:
                        finals(a, md, 0, d, (FIN2[:, 0, md, :], FIN2[:, 1, md, :], ("FIN2", md), ("FIN2", md)))
                    psy, pky = ysl[half]
                    for ri in range(2):
                        col = ((d * 2 + ri) * 4 + m4) * 128
                        first = cnt[half] == 0
                        last = cnt[half] == 15
                        cnt[half] += 1
                        P.op("pe", lambda e, psy=psy, col=col, ri=ri, first=first, last=last, SB=SB: e.matmul(
                            psy[:, :], CF[:, col:col + 128], SB[:, ri, :], start=first, stop=last),
                            reads=[("CF", d, m4, ri), ("SB", sbi, ri)], writes=[pky])
        for half in range(2):
            psy, pky = ysl[half]
            yp, t1, t2 = RS[0], RS[1], RS[2]
            hs = slice(half * 512, (half + 1) * 512)
            vop("dve", lambda e, psy=psy, hs=hs, gh=gh: e.scalar_tensor_tensor(yp, H[:, gh, hs], dskip[:, gh:gh + 1], psy[:, :], ALU.mult, ALU.add),
                [pky, ("H", gh, half), "dskip"], [("RS", 0)])
            tt("pool", t1, yp, yp, ALU.mult, [("RS", 0)], [("RS", 1)])
            vop("pool", lambda e: e.tensor_scalar(t1, t1, 0.044715, 1.0, ALU.mult, ALU.add), [("RS", 1)], [("RS", 1)])
            tt("pool", t1, t1, yp, ALU.mult, [("RS", 1), ("RS", 0)], [("RS", 1)])
            P.op("act", lambda e: e.activation(t2, t1, AF.Sigmoid, scale=1.5957691216057308), reads=[("RS", 1)], writes=[("RS", 2)])
            tt("pool", Gv(gh, half), yp, t2, ALU.mult, [("RS", 0), ("RS", 2)], [("G", gh, half)])
    ps_mod[0] = 8
    fin2k = [("FIN2", md) for md in range(64)]
    fre_b = tb_ap(T_FRE * 64, [[1, 64], [0, 2]])
    fim_b = tb_ap(T_FIM * 64, [[1, 64], [0, 2]])
    cmul("dve", NS[:, 0], NS[:, 1], FIN2[:, 0], FIN2[:, 1], fre_b, fim_b, fin2k + [tbk(T_FRE), tbk(T_FIM)],
         ["NS", "NS"], NT1, NT2, "NT1", "NT2")
    P.op("sp", lambda e: e.dma_start(out=news_d.ap(), in_=NS), reads=["NS"], writes=["news_out"], dma=True)

    for oc in range(KC):
        wav, wak = load_w(wa_d.ap().rearrange("(k p) n -> p k n", p=128)[:, :, oc * 128:(oc + 1) * 128], "p (k n) -> p k n", k=KC)
        wbv, wbk = load_w(wb_d.ap().rearrange("(k p) n -> p k n", p=128)[:, :, oc * 128:(oc + 1) * 128], "p (k n) -> p k n", k=KC)
        for h in range(2):
            psa, pka = next_ps()
            psb, pkb = next_ps()
            for (wv_, wk_, ps_, pk_) in ((wav, wak, psa, pka), (wbv, wbk, psb, pkb)):
                for kc in range(KC):
                    P.op("pe", lambda e, kc=kc, h=h, wv_=wv_, ps_=ps_: e.matmul(
                        ps_[:, :], wv_[:, kc, :], Gv(kc, h), start=(kc == 0), stop=(kc == KC - 1)),
                        reads=[wk_, ("G", kc, h)], writes=[pk_])
            sg, pr = SCR[0], SCR[1]
            P.op("act", lambda e, psb=psb: e.activation(sg[:], psb[:, :], AF.Sigmoid), reads=[pkb], writes=[("SCR", 0)])
            tt("dve", pr[:], psa[:, :], sg[:], ALU.mult, [pka, ("SCR", 0)], [("SCR", 1)])
            P.op("dve", lambda e, oc=oc, h=h: e.scalar_tensor_tensor(
                X[:, oc, h * 512:(h + 1) * 512], pr[:], mod_piece(1, 2, oc, h), X[:, oc, h * 512:(h + 1) * 512], ALU.mult, ALU.add),
                reads=[("SCR", 1), ("modT", 1, 2), ("X", oc, h)], writes=[("X", oc, h)])
    mlp(1)
    YF = BF[:, 0:4096].rearrange("p (c t) -> p c t", c=8)
    for h in range(2):
        norm_mod(X, YF, h * 512, 512, 1, 0, h, xkey_h(h), lambda kc: ("YF", kc), final=True)
        P.op("sp", lambda e, h=h: e.dma_start(out=y_d.ap()[:, :, h * 512:(h + 1) * 512], in_=YF[:, :, 0:512]),
             reads=[("YF", kc) for kc in range(KC)], writes=[("y_out", h)], dma=True)
    return P.emit()


def _fm(x_tok):
    T = x_tok.shape[0]
    return np.ascontiguousarray(x_tok.reshape(T, KC, 128).transpose(2, 1, 0))


def _rope_tables(q):
    pos = 512 * q - 128 + np.arange(768)
    row = (pos // 64).astype(np.float32)
    col = (pos % 64).astype(np.float32)
    freqs = (10000.0 ** (-np.arange(16, dtype=np.float32) / 16)).astype(np.float32)
    tab = np.zeros((128, 2, 768), np.float32)
    for p in range(128):
        d = p % 64
        f = freqs[d % 16]
        ang = (row if d < 32 else col) * f
        tab[p, 0] = np.cos(ang)
        tab[p, 1] = np.sin(ang) * (-1.0 if (d % 32) < 16 else 1.0)
    return tab


def _amask(q):
    m = np.zeros((128, 4, 2, 128), np.float32)
    jj = np.arange(128)[:, None]
    rr = np.arange(128)[None, :]
    for qb in range(4):
        lpos0 = 512 * q + 128 * qb - 128
        rpos0 = 512 * q + 128 * qb + 128
        m[:, qb, 0, :] = (jj >= rr) * (1.0 if lpos0 >= 0 else 0.0)
        m[:, qb, 1, :] = (jj <= rr) * (1.0 if rpos0 < 2048 else 0.0)
    return m


def _prep_shared(inp):
    f = np.float32
    sh = {}
    w_qkv = inp["w_qkv"][0]
    wq = w_qkv[:, :1024]
    wk = w_qkv[:, 1024:1280]
    wv = w_qkv[:, 1280:1536]
    d = np.arange(64)
    partner = np.where((d % 32) < 16, d + 16, d - 16)
    qperm = (np.arange(16)[:, None] * 64 + partner[None, :]).reshape(-1)
    kperm = (np.arange(4)[:, None] * 64 + partner[None, :]).reshape(-1)
    dup = (np.arange(4)[:, None, None] * 64 + np.zeros((1, 2, 1), int) + d[None, None, :]).reshape(-1)
    sh["wq"] = np.ascontiguousarray(wq)
    sh["wqp"] = np.ascontiguousarray(wq[:, qperm])
    sh["wkd"] = np.ascontiguousarray(wk[:, dup])
    sh["wkdp"] = np.ascontiguousarray(wk[:, kperm][:, dup])
    sh["wkv"] = np.ascontiguousarray(np.concatenate([wk, wv], axis=1))
    sh["w_o"] = np.ascontiguousarray(inp["w_o"][0])
    sh["w_mod"] = np.ascontiguousarray(inp["w_mod"])
    sh["mlp_w1"] = np.ascontiguousarray(inp["mlp_w1"])
    sh["mlp_w2"] = np.ascontiguousarray(inp["mlp_w2"])
    ngs = np.stack([inp["norm1_g"][0], inp["norm2_g"][0], inp["norm1_g"][1], inp["norm2_g"][1], inp["final_norm_g"]], 0)
    sh["ng"] = np.ascontiguousarray(ngs.reshape(5, KC, 128).transpose(2, 0, 1)).astype(f)
    sh["bmodT"] = np.ascontiguousarray(inp["b_mod"].reshape(2, 48, 128).transpose(2, 0, 1)).astype(f)
    sh["ident"] = np.eye(128, dtype=f)
    sh["sink"] = np.ascontiguousarray(np.broadcast_to(inp["attn_sink"][0][None, :], (128, 16))).astype(f)
    def gp_md(a):
        return a.reshape(2, 32, 2, 64).transpose(2, 3, 0, 1).reshape(128, 64)
    ldt = np.broadcast_to(inp["ssm_log_dt"][0][:, :, None], (2, 64, 64))
    sc = np.zeros((128, 4, 64), f)
    sc[:, 0] = gp_md(inp["ssm_lam_re"][0])
    sc[:, 1] = gp_md(inp["ssm_lam_im"][0])
    sc[:, 2] = gp_md(ldt)
    sh["ssm_sc"] = sc
    bp = np.zeros((8, 8, 16, 2, 2, 4, 2, 64), f)
    cp = np.zeros((8, 2, 64, 2, 2, 4, 8, 16), f)
    Bs = (inp["ssm_b_re"][0], inp["ssm_b_im"][0])
    Cs = (inp["ssm_c_re"][0], inp["ssm_c_im"][0])
    for gh in range(8):
        for m4 in range(4):
            for g2 in range(2):
                g = 8 * gh + 2 * m4 + g2
                gl = 2 * m4 + g2
                for d in range(2):
                    for ri in range(2):
                        bp[gh, gl, :, d, ri, m4, g2, :] = Bs[ri][d, g].T
                        cp[gh, g2, :, d, ri, m4, gl, :] = Cs[ri][d, g].T
    sh["bpad"] = bp.reshape(8, 128, 2048)
    sh["cpad"] = cp.reshape(8, 128, 2048)
    sh["dskip"] = np.ascontiguousarray(inp["ssm_d"][0].reshape(KC, 128).T).astype(f)
    sh["glu_w_a"] = np.ascontiguousarray(inp["glu_w_a"][0])
    sh["glu_w_b"] = np.ascontiguousarray(inp["glu_w_b"][0])
    return sh


def _prep_core(inp, r, sh):
    b, q = r // 4, r % 4
    xs = inp["x_sample"][b]
    own = np.concatenate([inp["x_prompt"][2 * r], inp["x_prompt"][2 * r + 1], xs[512 * q:512 * q + 512]], 0)
    halo = np.zeros((256, D), np.float32)
    if q > 0:
        halo[:128] = xs[512 * q - 128:512 * q]
    if q < 3:
        halo[128:] = xs[512 * q + 512:512 * q + 640]
    cv = np.stack([inp["c_ctx"], inp["c"][b]], 1)
    m = dict(sh)
    m["x_own"] = _fm(own)
    m["x_halo"] = _fm(halo)
    m["cvec"] = np.ascontiguousarray(cv.reshape(KC, 128, 2).transpose(1, 0, 2)).astype(np.float32)
    m["rope"] = _rope_tables(q)
    import ml_dtypes
    m["amask"] = _amask(q).astype(ml_dtypes.bfloat16)
    m["ck"] = np.ascontiguousarray(inp["cache_k"][b, 0].reshape(256, 256))
    m["cv"] = np.ascontiguousarray(inp["cache_v"][b, 0].reshape(256, 256))
    st = inp["state_ssm"][b, 0]
    m["s0"] = np.ascontiguousarray(st.reshape(2, 2, 32, 2, 64).transpose(3, 4, 1, 0, 2).reshape(128, 2, 64)).astype(np.float32)
    qs = np.zeros((128, 4), np.float32)
    qs[:, q] = 1.0
    m["qsel"] = qs
    return m


_NC_CACHE = {}


STAGE = 99


def kernel(**inputs):
    inp = {k: np.asarray(v) for k, v in inputs.items()}
    sh = _prep_shared(inp)
    in_maps = [_prep_core(inp, r, sh) for r in range(NCORE)]
    nc = build_program(STAGE)
    res = run_bass_kernel_spmd(nc, in_maps, core_ids=list(range(NCORE)))
    y_prompt = np.zeros((16, 256, D), np.float32)
    y_sample = np.zeros((2, 2048, D), np.float32)
    new_k = np.zeros((16, 1, 256, 4, 64), np.float32)
    new_v = np.zeros((16, 1, 256, 4, 64), np.float32)
    new_s = np.zeros((16, 1, 2, 2, 64, 64), np.float32)
    for r in range(NCORE):
        o = res.results[r]
        b, q = r // 4, r % 4
        y = np.asarray(o["y"]).transpose(2, 1, 0).reshape(NT, D)
        y_prompt[2 * r] = y[0:256]
        y_prompt[2 * r + 1] = y[256:512]
        y_sample[b, 512 * q:512 * q + 512] = y[512:1024]
        nk = np.asarray(o["new_k"]).reshape(2, 256, 4, 64)
        nv = np.asarray(o["new_v"]).reshape(2, 256, 4, 64)
        new_k[2 * r:2 * r + 2, 0] = nk
        new_v[2 * r:2 * r + 2, 0] = nv
        ns = np.asarray(o["new_s"]).reshape(2, 64, 2, 2, 32, 2)
        new_s[2 * r:2 * r + 2, 0] = ns.transpose(5, 3, 2, 4, 0, 1).reshape(2, 2, 2, 64, 64)
    return (y_prompt, y_sample, new_k, new_v, new_s)
```

```python
import math
import os
import numpy as np
import concourse.bass as bass
import concourse.mybir as mybir
from concourse.bass_utils import run_bass_kernel_spmd

F32 = mybir.dt.float32
BF16 = mybir.dt.bfloat16
ALU = mybir.AluOpType
AF = mybir.ActivationFunctionType

NCORE = 8
D = 1024
KC = 8
NT = 1024
EPS = 1e-6
TWO_PI = 2.0 * math.pi


class Op:
    __slots__ = ("eng", "fn", "kind", "deps", "signal", "sig_val", "dsem", "dval", "prev_dma", "idx")

    def __init__(self, eng, fn, kind):
        self.eng = eng
        self.fn = fn
        self.kind = kind
        self.deps = []
        self.signal = False
        self.sig_val = None
        self.dsem = None
        self.dval = None
        self.prev_dma = None
        self.idx = None


class Prog:
    ENGS = ("pe", "act", "dve", "pool", "sp")
    NDMA = {"sp": 20, "pool": 20, "act": 6}

    def __init__(self):
        self.nc = bass.Bass("TRN2", target_bir_lowering=False)
        self.ops = {e: [] for e in self.ENGS}
        self.last_write = {}
        self.readers = {}
        self.dma_count = {e: 0 for e in self.ENGS}
        self.dma_hist = {e: [] for e in self.ENGS}
        self.cc_ops = []

    def op(self, eng, fn, reads=(), writes=(), dma=False, cc=False):
        kind = "cc" if cc else ("dma" if dma else None)
        o = Op(eng, fn, kind)
        deps = {}
        for k in reads:
            w = self.last_write.get(k)
            if w is not None:
                deps[id(w)] = (w, True)
        for k in writes:
            w = self.last_write.get(k)
            if w is not None and id(w) not in deps:
                deps[id(w)] = (w, False)
            for r in self.readers.get(k, ()):
                if id(r) not in deps:
                    deps[id(r)] = (r, False)
        for d, raw in deps.values():
            if d is o:
                continue
            if d.kind is None and kind is None and d.eng == eng:
                if eng == "pe" or (eng in ("dve", "act") and not raw):
                    continue
            o.deps.append(d)
            if d.kind is None:
                d.signal = True
        for k in reads:
            self.readers.setdefault(k, []).append(o)
        for k in writes:
            self.last_write[k] = o
            self.readers[k] = []
        if kind == "dma":
            n = self.NDMA[eng]
            i = self.dma_count[eng]
            self.dma_count[eng] += 1
            o.idx = i
            hist = self.dma_hist[eng]
            if i >= n:
                o.prev_dma = hist[i - n]
            hist.append(o)
        elif kind == "cc":
            o.idx = len(self.cc_ops)
            self.cc_ops.append(o)
        self.ops[eng].append(o)
        return o

    def fence(self, engs=("pe", "act", "dve", "pool", "sp")):
        deps = []
        for e in ("pe", "act", "dve", "pool"):
            comp = [o for o in self.ops[e] if o.kind is None]
            if comp:
                deps.append(comp[-1])
        for e, n in self.NDMA.items():
            deps += self.dma_hist[e][-n:]
        deps += self.cc_ops[-1:]
        for e in engs:
            o = Op(e, lambda eng: eng.nop(), None)
            o.deps = list(deps)
            for d in o.deps:
                if d.kind is None:
                    d.signal = True
            self.ops[e].append(o)

    def emit(self):
        nc = self.nc
        sems = {e: nc.alloc_semaphore("c_" + e) for e in ("pe", "act", "dve", "pool")}
        dsems = {e: [nc.alloc_semaphore(f"d_{e}{i}") for i in range(n)] for e, n in self.NDMA.items()}
        ccsem = nc.alloc_semaphore("ccsem")
        for e in self.ENGS:
            c = 0
            for o in self.ops[e]:
                if o.kind == "dma":
                    n = self.NDMA[e]
                    o.dsem = dsems[e][o.idx % n]
                    o.dval = 16 * (o.idx // n + 1)
                elif o.kind == "cc":
                    o.dsem = ccsem
                    o.dval = o.idx + 1
                elif o.signal:
                    c += 1
                    o.sig_val = c
        engobj = {"pe": "tensor", "act": "scalar", "dve": "vector", "pool": "gpsimd", "sp": "sync"}
        with nc.Block() as block:
            for e in self.ENGS:
                ops = self.ops[e]

                def body(eng, e=e, ops=ops):
                    waited = {}
                    for o in ops:
                        need = []
                        for d in o.deps:
                            if d.kind is not None:
                                need.append((d.dsem, d.dval))
                            else:
                                need.append((sems[d.eng], d.sig_val))
                        if o.prev_dma is not None:
                            need.append((o.prev_dma.dsem, o.prev_dma.dval))
                        for s, v in need:
                            k = id(s)
                            if waited.get(k, 0) >= v:
                                continue
                            waited[k] = v
                            eng.wait_ge(s, v)
                        ins = o.fn(eng)
                        if o.kind == "dma":
                            ins.then_inc(o.dsem, 16)
                        elif o.kind == "cc":
                            ins.then_inc(o.dsem)
                        elif o.signal:
                            ins.then_inc(sems[e], 1)
                    last = {}
                    for o in ops:
                        if o.kind is not None:
                            last[id(o.dsem)] = (o.dsem, o.dval)
                    for s, v in last.values():
                        if waited.get(id(s), 0) < v:
                            eng.wait_ge(s, v)

                getattr(block, engobj[e])(body)
        return nc


def build_program(stage=99):
    P = Prog()
    nc = P.nc
    dbg = {}

    def din(name, shape, dt=F32):
        return nc.dram_tensor(name, list(shape), dt, kind="ExternalInput")

    def dout(name, shape, dt=F32):
        return nc.dram_tensor(name, list(shape), dt, kind="ExternalOutput")

    x_own = din("x_own", [128, KC, NT])
    x_halo = din("x_halo", [128, KC, 256])
    cvec = din("cvec", [128, KC, 2])
    ng_d = din("ng", [128, 5, KC])
    bmod_d = din("bmodT", [128, 2, 48])
    rope_d = din("rope", [128, 2, 768])
    amask_d = din("amask", [128, 4, 2, 128], BF16)
    ident_d = din("ident", [128, 128])
    sink_d = din("sink", [128, 16])
    ck_d = din("ck", [256, 256])
    cv_d = din("cv", [256, 256])
    w_mod = din("w_mod", [2, D, 6 * D])
    wq_d = din("wq", [D, D])
    wqp_d = din("wqp", [D, D])
    wkd_d = din("wkd", [D, 512])
    wkdp_d = din("wkdp", [D, 512])
    wkv_d = din("wkv", [D, 512])
    wo_d = din("w_o", [D, D])
    w1_d = din("mlp_w1", [2, D, 4 * D])
    w2_d = din("mlp_w2", [2, 4 * D, D])
    y_d = dout("y", [128, KC, NT])
    nk_d = dout("new_k", [512, 256])
    nv_d = dout("new_v", [512, 256])

    X = nc.alloc_sbuf_tensor("s_X", [128, KC, NT], F32)
    H = nc.alloc_sbuf_tensor("s_H", [128, KC, NT], BF16)
    BIG = nc.alloc_sbuf_tensor("s_BIG", [128, 32768], BF16)
    WB = [nc.alloc_sbuf_tensor(f"s_WB{i}", [128, 4096], BF16) for i in range(3)]
    PT = [nc.alloc_sbuf_tensor(f"s_PT{i}", [128, 5, 512], BF16) for i in range(2)]
    SCR = [nc.alloc_sbuf_tensor(f"s_SCR{i}", [128, 512], F32) for i in range(4)]
    SQ = nc.alloc_sbuf_tensor("s_SQ", [128, KC, 512], BF16)
    RSTD = nc.alloc_sbuf_tensor("s_RSTD", [128, 512], F32)
    ones_bf = nc.alloc_sbuf_tensor("s_ones_bf", [128, 128], BF16)
    ident = nc.alloc_sbuf_tensor("s_ident", [128, 128], F32)
    cs_f = nc.alloc_sbuf_tensor("s_cs_f", [128, KC, 2], F32)
    cs_b = nc.alloc_sbuf_tensor("s_cs_b", [128, KC, 2], BF16)
    ng = nc.alloc_sbuf_tensor("s_ng", [128, 5, KC], F32)
    bmod = nc.alloc_sbuf_tensor("s_bmod", [128, 2, 48], F32)
    modT = nc.alloc_sbuf_tensor("s_modT", [128, 2, 48, 2], F32)
    gs = nc.alloc_sbuf_tensor("s_gs", [128, 2, 2, KC, 2], F32)
    L0B = nc.alloc_sbuf_tensor("s_L0B", [128, 3584], F32)
    rope = L0B[:, 0:1536].rearrange("p (a t) -> p a t", a=2)
    amask = nc.alloc_sbuf_tensor("s_amask", [128, 4, 2, 128], BF16)
    sink = nc.alloc_sbuf_tensor("s_sink", [128, 16], F32)
    esink = nc.alloc_sbuf_tensor("s_esink", [128, 16], F32)
    XH = nc.alloc_sbuf_tensor("s_XH", [128, KC, 256], F32)
    HH = nc.alloc_sbuf_tensor("s_HH", [128, KC, 256], BF16)
    CKV = L0B[:, 1536:2560].rearrange("p (a b n) -> p a b n", a=2, b=2)
    CKD = nc.alloc_sbuf_tensor("s_CKD", [128, 2, 4, 2, 64], BF16)
    ident_bf = nc.alloc_sbuf_tensor("s_ident_bf", [128, 128], BF16)
    KVO = L0B[:, 2560:3584].rearrange("p (a n) -> p a n", a=2)
    DEN = [nc.alloc_sbuf_tensor(f"s_DEN{i}", [128, 512], F32) for i in range(2)]
    RELU = [nc.alloc_sbuf_tensor(f"s_RELU{i}", [128, 512], BF16) for i in range(2)]
    PS = [nc.alloc_psum_tensor(f"p_PS{i}", [128, 512], F32) for i in range(8)]

    QT = BIG[:, 0:8192].rearrange("p (c t) -> p c t", c=8)
    KT = BIG[:, 8192:14336].rearrange("p (c t) -> p c t", c=4)
    VA = BIG[:, 14336:21504].rearrange("p (b k e) -> p b k e", b=14, k=4)
    ATT = BIG[:, 21504:29696].rearrange("p (c t) -> p c t", c=8)
    HID = BIG[:, 0:32768].rearrange("p (c t) -> p c t", c=32)

    ps_rr = [0]
    ps_mod = [8]

    def next_ps():
        i = ps_rr[0] % ps_mod[0]
        ps_rr[0] += 1
        return PS[i], ("ps", i)

    wb_rr = [0]

    def load_w(src_ap, shape_str=None, **kw):
        i = wb_rr[0] % 3
        wb_rr[0] += 1
        n = 1
        for s in src_ap.shape[1:]:
            n *= s
        dst = WB[i][:, 0:n]
        if shape_str is not None:
            dst = dst.rearrange(shape_str, **kw)
        P.op("pool", lambda e: e.dma_start(out=dst, in_=src_ap), writes=[("wb", i)], dma=True)
        return dst, ("wb", i)

    def dma_in(dst, src, key, eng="sp"):
        P.op(eng, lambda e: e.dma_start(out=dst, in_=src), writes=[key], dma=True)

    dma_in(X[:], x_own.ap(), "X_all")
    dma_in(XH[:], x_halo.ap(), "XH")
    dma_in(cs_f[:], cvec.ap(), "cs_f")
    dma_in(ng[:], ng_d.ap(), "ng")
    dma_in(bmod[:], bmod_d.ap(), "bmod")
    dma_in(rope, rope_d.ap(), "rope")
    dma_in(amask[:], amask_d.ap(), "amask")
    dma_in(ident[:], ident_d.ap(), "ident")
    dma_in(sink[:], sink_d.ap(), "sink")
    dma_in(CKV[:, :, 0, :], ck_d.ap().rearrange("(b p) n -> p b n", p=128), "CK")
    dma_in(CKV[:, :, 1, :], cv_d.ap().rearrange("(b p) n -> p b n", p=128), "CV")
    P.op("dve", lambda e: e.memset(ones_bf[:], 1.0), writes=["ones"])
    P.op("dve", lambda e: e.tensor_copy(ident_bf[:], ident[:]), reads=["ident"], writes=["ident_bf"])
    P.op("act", lambda e: e.activation(esink[:], sink[:], AF.Exp), reads=["sink"], writes=["esink"])
    P.op("act", lambda e: e.activation(cs_b[:], cs_f[:], AF.Silu), reads=["cs_f"], writes=["cs_b"])
    xkeys = [("X", kc, h) for kc in range(KC) for h in range(2)]
    for k in xkeys:
        P.last_write[k] = P.last_write["X_all"]

    def modulation_gen(i):
        psm = PS[7 - i]
        for piece in range(6):
            pk = ("psm", i, piece)
            for hb in range(2):
                cb = piece * 2 + hb
                wv, wk = load_w(w_mod.ap()[i].rearrange("(k p) n -> p k n", p=128)[:, :, cb * 512:(cb + 1) * 512],
                                "p (k n) -> p k n", k=KC)
                for cc in range(4):
                    j = cb * 4 + cc
                    for kc in range(KC):
                        P.op("pe", lambda e, j=j, kc=kc, cc=cc, wv=wv: e.matmul(
                            psm[:, j * 2:(j + 1) * 2], wv[:, kc, cc * 128:(cc + 1) * 128], cs_b[:, kc, :],
                            start=(kc == 0), stop=(kc == KC - 1)),
                            reads=[wk, "cs_b"], writes=[pk])
                if hb == 0:
                    yield
            bm = bass.AP(tensor=bmod, offset=i * 48 + piece * 8, ap=[[96, 128], [1, 8], [0, 2]])
            P.op("dve", lambda e, piece=piece, bm=bm: e.tensor_tensor(
                modT[:, i, piece * 8:(piece + 1) * 8, :], psm[:, piece * 16:(piece + 1) * 16].rearrange("p (j s) -> p j s", s=2), bm, ALU.add),
                reads=[pk, "bmod"], writes=[("modT", i, piece)])
            if piece in (1, 4):
                w = (piece - 1) // 3
                sc = modT[:, i, piece * 8:(piece + 1) * 8, :]
                gv = bass.AP(tensor=ng, offset=(2 * i + w) * KC, ap=[[5 * KC, 128], [1, KC], [0, 2]])
                P.op("dve", lambda e, w=w, sc=sc: e.tensor_scalar(gs[:, i, w], sc, 1.0, 32.0, ALU.add, ALU.mult),
                     reads=[("modT", i, piece)], writes=[("gs0", i, w)])
                P.op("dve", lambda e, w=w, gv=gv: e.tensor_tensor(gs[:, i, w], gs[:, i, w], gv, ALU.mult),
                     reads=[("gs0", i, w), "ng"], writes=[("gs", i, w)])
            yield

    def drain(g):
        for _ in g:
            pass

    def mod_piece(i, piece, kc, s):
        return modT[:, i, piece * 8 + kc, s:s + 1]

    def norm_mod(src, dst, t0, T, i, w, s, skey, dkey, final=False):
        ps, pk = next_ps()
        for kc in range(KC):
            P.op("act", lambda e, kc=kc: e.activation(SQ[:, kc, 0:T], src[:, kc, t0:t0 + T], AF.Square),
                 reads=[skey(kc)], writes=[("SQ", kc)])
        for kc in range(KC):
            P.op("pe", lambda e, kc=kc: e.matmul(ps[:, 0:T], ones_bf[:], SQ[:, kc, 0:T], start=(kc == 0), stop=(kc == KC - 1)),
                 reads=[("SQ", kc), "ones"], writes=[pk])
        NV = int(os.environ.get("NORMVAR", "9"))
        if NV < 2:
            return
        P.op("act", lambda e: e.activation(RSTD[:, 0:T], ps[:, 0:T], AF.Ln, bias=epsb[:, 0:1]), reads=[pk, "epsb"], writes=["RSTD"])
        P.op("act", lambda e: e.activation(RSTD[:, 0:T], RSTD[:, 0:T], AF.Exp, scale=-0.5), reads=["RSTD"], writes=["RSTD"])
        if NV < 3:
            return
        for kc in range(KC):
            sc = SCR[kc % 4]
            sk = ("SCR", kc % 4)
            P.op("dve", lambda e, kc=kc, sc=sc: e.tensor_tensor(sc[:, 0:T], src[:, kc, t0:t0 + T], RSTD[:, 0:T], ALU.mult),
                 reads=[skey(kc), "RSTD"], writes=[sk])
            if NV < 4:
                continue
            if final:
                P.op("dve", lambda e, kc=kc, sc=sc: e.tensor_scalar(dst[:, kc, 0:T], sc[:, 0:T], fng[:, kc:kc + 1], None, ALU.mult),
                     reads=[sk, "fng"], writes=[dkey(kc)])
            else:
                P.op("dve", lambda e, kc=kc, sc=sc: e.tensor_scalar(
                    dst[:, kc, t0:t0 + T], sc[:, 0:T], gs[:, i, w, kc, s:s + 1], mod_piece(i, 3 * w, kc, s), ALU.mult, ALU.add),
                    reads=[sk, ("gs", i, w), ("modT", i, 3 * w)], writes=[dkey(kc)])

    epsb = nc.alloc_sbuf_tensor("s_epsb", [128, 1], F32)
    P.op("dve", lambda e: e.memset(epsb[:], D * EPS), writes=["epsb"])
    fng = nc.alloc_sbuf_tensor("s_fng", [128, KC], F32)
    P.op("dve", lambda e: e.tensor_scalar(fng[:], ng[:, 4, :], 32.0, None, ALU.mult), reads=["ng"], writes=["fng"])

    def gated_residual(ps, pk, oc, h, i, piece):
        s = h
        P.op("dve", lambda e: e.scalar_tensor_tensor(
            X[:, oc, h * 512:(h + 1) * 512], ps[:, :], mod_piece(i, piece, oc, s), X[:, oc, h * 512:(h + 1) * 512],
            ALU.mult, ALU.add), reads=[pk, ("modT", i, piece), ("X", oc, h)], writes=[("X", oc, h)])

    def xkey_h(h):
        return lambda kc: ("X", kc, h)

    def hkey_h(h):
        return lambda kc: ("H", kc, h)

    def mlp(i, gen=None):
        for h in range(2):
            norm_mod(X, H, h * 512, 512, i, 1, h, xkey_h(h), hkey_h(h))
        w1v = w1_d.ap()[i].rearrange("(k p) n -> p k n", p=128)
        for hg in range(8):
            if gen is not None:
                next(gen, None)
            wv, wk = load_w(w1v[:, :, hg * 512:(hg + 1) * 512], "p (k n) -> p k n", k=KC)
            for cc in range(4):
                hc = hg * 4 + cc
                for h in range(2):
                    ps, pk = next_ps()
                    for kc in range(KC):
                        P.op("pe", lambda e, kc=kc, cc=cc, h=h, wv=wv, ps=ps: e.matmul(
                            ps[:, :], wv[:, kc, cc * 128:(cc + 1) * 128], H[:, kc, h * 512:(h + 1) * 512],
                            start=(kc == 0), stop=(kc == KC - 1)), reads=[wk, ("H", kc, h)], writes=[pk])
                    r = RELU[(hc * 2 + h) % 2]
                    rk = ("RELU", (hc * 2 + h) % 2)
                    P.op("act", lambda e, ps=ps, r=r: e.activation(r[:], ps[:, :], AF.Relu), reads=[pk], writes=[rk])
                    P.op("dve", lambda e, r=r, hc=hc, h=h: e.tensor_tensor(HID[:, hc, h * 512:(h + 1) * 512], r[:], r[:], ALU.mult),
                         reads=[rk], writes=[("HID", hc, h)])
        w2v = w2_d.ap()[i].rearrange("(k p) n -> p k n", p=128)
        for oc in range(KC):
            if gen is not None:
                next(gen, None)
            wv, wk = load_w(w2v[:, :, oc * 128:(oc + 1) * 128], "p (k n) -> p k n", k=32)
            for h in range(2):
                ps, pk = next_ps()
                for hc in range(32):
                    P.op("pe", lambda e, hc=hc, h=h, wv=wv, ps=ps: e.matmul(
                        ps[:, :], wv[:, hc, :], HID[:, hc, h * 512:(h + 1) * 512],
                        start=(hc == 0), stop=(hc == 31)), reads=[wk, ("HID", hc, h)], writes=[pk])
                gated_residual(ps, pk, oc, h, i, 5)

    def finish():
        for h in range(2):
            P.op("sp", lambda e, h=h: e.dma_start(out=y_d.ap()[:, :, h * 512:(h + 1) * 512], in_=X[:, :, h * 512:(h + 1) * 512]),
                 reads=[("X", kc, h) for kc in range(KC)], writes=[("y_out", h)], dma=True)
        return P.emit()

    ps_mod[0] = 6
    g0 = modulation_gen(0)
    for _ in range(4):
        next(g0)
    if stage == 1:
        return finish()
    for h in range(2):
        norm_mod(X, H, h * 512, 512, 0, 0, h, xkey_h(h), hkey_h(h))
    norm_mod(XH, HH, 0, 256, 0, 0, 1, lambda kc: "XH", lambda kc: ("HH", kc))

    if stage == 2:
        return finish()

    def proj_fm(wd, col0, jobs):
        wv, wk = load_w(wd.ap().rearrange("(k p) n -> p k n", p=128)[:, :, col0:col0 + 128], "p (k n) -> p k n", k=KC)
        for (ps, pk, off, n, rf, rkeys) in jobs:
            for kc in range(KC):
                P.op("pe", lambda e, kc=kc, off=off, n=n, rf=rf, ps=ps: e.matmul(
                    ps[:, off:off + n], wv[:, kc, :], rf(kc), start=(kc == 0), stop=(kc == KC - 1)),
                    reads=[wk, rkeys(kc)], writes=[pk])

    rhs_p = (0, 512, lambda kc: H[:, kc, 0:512], lambda kc: ("H", kc, 0))
    rhs_s = (0, 512, lambda kc: H[:, kc, 512:1024], lambda kc: ("H", kc, 1))
    rhs_hl = (0, 256, lambda kc: HH[:, kc, 0:256], lambda kc: ("HH", kc))

    def rope_evac(ps1, pk1, ps2, pk2, n, tab0, dst, dkey):
        a, ak = SCR[0], ("SCR", 0)
        b, bk = SCR[1], ("SCR", 1)
        P.op("dve", lambda e: e.tensor_tensor(a[:, 0:n], ps1[:, 0:n], rope[:, 0, tab0:tab0 + n], ALU.mult),
             reads=[pk1, "rope"], writes=[ak])
        P.op("dve", lambda e: e.tensor_tensor(b[:, 0:n], ps2[:, 0:n], rope[:, 1, tab0:tab0 + n], ALU.mult),
             reads=[pk2, "rope"], writes=[bk])
        P.op("dve", lambda e: e.tensor_tensor(dst, a[:, 0:n], b[:, 0:n], ALU.add), reads=[ak, bk], writes=[dkey])

    for c in range(8):
        ps, pk = next_ps()
        ps1, pk1 = next_ps()
        proj_fm(wq_d, c * 128, [(ps, pk) + rhs_p, (ps1, pk1) + rhs_s])
        P.op("act", lambda e, ps=ps, c=c: e.activation(QT[:, c, 0:512], ps[:, :], AF.Copy), reads=[pk], writes=[("QT", c, 0)])
        ps2, pk2 = next_ps()
        proj_fm(wqp_d, c * 128, [(ps2, pk2) + rhs_s])
        rope_evac(ps1, pk1, ps2, pk2, 512, 128, QT[:, c, 512:1024], ("QT", c, 1))
    for kv in range(4):
        ps, pk = next_ps()
        ps1, pk1 = next_ps()
        ps3, pk3 = next_ps()
        proj_fm(wkd_d, kv * 128, [(ps, pk) + rhs_p, (ps1, pk1) + rhs_s, (ps3, pk3) + rhs_hl])
        P.op("act", lambda e, ps=ps, kv=kv: e.activation(KT[:, kv, 0:512], ps[:, :], AF.Copy), reads=[pk], writes=[("KT", kv, 0)])
        ps2, pk2 = next_ps()
        ps4, pk4 = next_ps()
        proj_fm(wkdp_d, kv * 128, [(ps2, pk2) + rhs_s, (ps4, pk4) + rhs_hl])
        rope_evac(ps1, pk1, ps2, pk2, 512, 128, KT[:, kv, 640:1152], ("KT", kv, 1))
        ps1, pk1, ps2, pk2 = ps3, pk3, ps4, pk4
        rope_evac(ps1, pk1, ps2, pk2, 128, 0, KT[:, kv, 512:640], ("KT", kv, 2))
        a, ak = SCR[2], ("SCR", 2)
        b, bk = SCR[3], ("SCR", 3)
        P.op("dve", lambda e, ps1=ps1, a=a: e.tensor_tensor(a[:, 0:128], ps1[:, 128:256], rope[:, 0, 640:768], ALU.mult),
             reads=[pk1, "rope"], writes=[ak])
        P.op("dve", lambda e, ps2=ps2, b=b: e.tensor_tensor(b[:, 0:128], ps2[:, 128:256], rope[:, 1, 640:768], ALU.mult),
             reads=[pk2, "rope"], writes=[bk])
        P.op("dve", lambda e, kv=kv, a=a, b=b: e.tensor_tensor(KT[:, kv, 1152:1280], a[:, 0:128], b[:, 0:128], ALU.add),
             reads=[ak, bk], writes=[("KT", kv, 3)])
    if stage == 3:
        return finish()
    S4 = int(os.environ.get("S4VAR", "0"))
    for kb in range(0 if S4 == 1 else 2):
        for dup in range(2):
            P.op("dve", lambda e, kb=kb, dup=dup: e.tensor_copy(
                CKD[:, kb, :, dup, :], CKV[:, kb, 0, :].rearrange("p (k d) -> p k d", k=4)),
                reads=["CK"], writes=[("CKD", kb, dup)])
        for kv in range(4):
            ps, pk = next_ps()
            P.op("pe", lambda e, kb=kb, kv=kv, ps=ps: e.matmul(
                ps[:, 0:128], CKD[:, kb, kv].rearrange("p a d -> p (a d)"), ident_bf[:], start=True, stop=True),
                reads=[("CKD", kb, 0), ("CKD", kb, 1), "ident_bf"], writes=[pk])
            P.op("act", lambda e, kb=kb, kv=kv, ps=ps: e.activation(KT[:, kv, 1280 + kb * 128:1408 + kb * 128], ps[:, 0:128], AF.Copy),
                 reads=[pk], writes=[("KT", kv, 4 + kb)])
    wkv_v, wkv_k = load_w(wkv_d.ap().rearrange("(k p) n -> p k n", p=128), "p (k n) -> p k n", k=KC)
    vblocks = [(b, (lambda kc, b=b: H[:, kc, b * 128:(b + 1) * 128]), (lambda kc, b=b: ("H", kc, b // 4)), b if b < 4 else b + 1)
               for b in range(8)]
    vblocks += [(8, (lambda kc: HH[:, kc, 0:128]), (lambda kc: ("HH", kc)), 4),
                (9, (lambda kc: HH[:, kc, 128:256]), (lambda kc: ("HH", kc)), 9)]
    for (b, lf, lk, vb) in (vblocks if S4 != 2 else []):
        ps, pk = next_ps()
        for kc in range(KC):
            P.op("pe", lambda e, kc=kc, lf=lf, ps=ps: e.matmul(ps[:, :], lf(kc), wkv_v[:, kc, :], start=(kc == 0), stop=(kc == KC - 1)),
                 reads=[wkv_k, lk(kc)], writes=[pk])
        for dup in range(2):
            eng = "dve" if dup == 0 else "pool"
            if eng == "pool":
                continue
        for dup in range(2):
            P.op("act", lambda e, vb=vb, ps=ps, dup=dup: e.activation(
                VA[:, vb, :, dup * 64:(dup + 1) * 64], ps[:, 256:512].rearrange("p (k d) -> p k d", k=4), AF.Copy),
                reads=[pk], writes=[("VA", vb, dup)])
        if b < 4 and S4 != 3:
            P.op("act", lambda e, b=b, ps=ps: e.activation(KVO[:, b % 2, :], ps[:, :], AF.Copy), reads=[pk], writes=[("KVO", b % 2)])
            P.op("sp", lambda e, b=b: e.dma_start(out=nk_d.ap()[b * 128:(b + 1) * 128, :], in_=KVO[:, b % 2, 0:256]),
                 reads=[("KVO", b % 2)], writes=[("nk_out", b)], dma=True)
            P.op("sp", lambda e, b=b: e.dma_start(out=nv_d.ap()[b * 128:(b + 1) * 128, :], in_=KVO[:, b % 2, 256:512]),
                 reads=[("KVO", b % 2)], writes=[("nv_out", b)], dma=True)
    for kb in range(2):
        for dup in range(2):
            P.op("dve", lambda e, kb=kb, dup=dup: e.tensor_copy(
                VA[:, 10 + kb, :, dup * 64:(dup + 1) * 64], CKV[:, kb, 1, :].rearrange("p (k d) -> p k d", k=4)),
                reads=["CV"], writes=[("VA", 10 + kb, dup)])
    if stage == 4:
        return finish()
    def attention(qtok, half, keyblocks, pbuf):
        for kv in range(4):
            pt = PT[kv % 2]
            pbuf = kv % 2
            nkb = len(keyblocks)
            for j, (ktcol, ktk, vb, mside, qb) in enumerate(keyblocks):
                for hf in range(2):
                    ps, pk = next_ps()
                    P.op("pe", lambda e, hf=hf, kv=kv, ktcol=ktcol, ps=ps: e.matmul(
                        ps[:, 0:256].rearrange("p (c q) -> p c q", c=2),
                        KT[hf * 64:(hf + 1) * 64, kv, ktcol:ktcol + 128],
                        QT[hf * 64:(hf + 1) * 64, 2 * kv:2 * kv + 2, qtok:qtok + 128], start=True, stop=True),
                        reads=[("KT", kv, ktk), ("QT", 2 * kv, half), ("QT", 2 * kv + 1, half)], writes=[pk])
                    P.op("act", lambda e, j=j, hf=hf, ps=ps, pt=pt: e.activation(pt[:, j, hf * 256:(hf + 1) * 256], ps[:, 0:256], AF.Exp, scale=0.125),
                         reads=[pk], writes=[("PT", pbuf, j, hf)])
                if mside is not None:
                    P.op("dve", lambda e, j=j, pt=pt, qb=qb, mside=mside: e.tensor_tensor(
                        pt[:, j, :].rearrange("p (a q) -> p a q", a=4), pt[:, j, :].rearrange("p (a q) -> p a q", a=4),
                        amask[:, qb, mside, :].unsqueeze(1).broadcast_to([128, 4, 128]), ALU.mult),
                        reads=[("PT", pbuf, j, 0), ("PT", pbuf, j, 1), "amask"], writes=[("PT", pbuf, j, 0), ("PT", pbuf, j, 1)])
            pso, pko = next_ps()
            psl, pkl = next_ps()
            for j, (ktcol, ktk, vb, mside, qb) in enumerate(keyblocks):
                P.op("pe", lambda e, j=j, vb=vb, kv=kv, pso=pso, pt=pt, nkb=nkb: e.matmul(pso[:, :], VA[:, vb, kv, :], pt[:, j, :], start=(j == 0), stop=(j == nkb - 1)),
                     reads=[("VA", vb, 0), ("VA", vb, 1), ("PT", pbuf, j, 0), ("PT", pbuf, j, 1)], writes=[pko])
            for j in range(nkb):
                P.op("pe", lambda e, j=j, psl=psl, pt=pt, nkb=nkb: e.matmul(psl[:, :], ones_bf[:], pt[:, j, :], start=(j == 0), stop=(j == nkb - 1)),
                     reads=["ones", ("PT", pbuf, j, 0), ("PT", pbuf, j, 1)], writes=[pkl])
            den = DEN[kv % 2]
            dk = ("DEN", kv % 2)
            es = bass.AP(tensor=esink, offset=4 * kv, ap=[[16, 128], [1, 2], [2, 2], [0, 128]])
            P.op("dve", lambda e, es=es, den=den, psl=psl: e.tensor_tensor(
                den[:, :].rearrange("p (h c q) -> p h c q", h=2, c=2), psl[:, :].rearrange("p (h c q) -> p h c q", h=2, c=2), es, ALU.add),
                reads=[pkl, "esink"], writes=[dk])
            P.op("dve", lambda e, den=den: e.reciprocal(den[:, :], den[:, :]), reads=[dk], writes=[dk])
            for hf in range(2):
                P.op("dve", lambda e, hf=hf, kv=kv, den=den, pso=pso: e.tensor_tensor(
                    ATT[hf * 64:(hf + 1) * 64, 2 * kv:2 * kv + 2, qtok:qtok + 128],
                    pso[hf * 64:(hf + 1) * 64, hf * 256:(hf + 1) * 256].rearrange("p (c q) -> p c q", c=2),
                    den[hf * 64:(hf + 1) * 64, hf * 256:(hf + 1) * 256].rearrange("p (c q) -> p c q", c=2), ALU.mult),
                    reads=[pko, dk], writes=[("ATT", 2 * kv, half), ("ATT", 2 * kv + 1, half)])

    pb = 0
    for s in range(2):
        for qb in range(2):
            kbs = [(s * 256 + j * 128, 0, 2 * s + j, None, 0) for j in range(2)]
            attention(s * 256 + qb * 128, 0, kbs, pb % 2)
            next(g0, None)
            pb += 1
    for qb in range(4):
        kbs = []
        for j in range(3):
            eb = qb + j
            ktk = 2 if eb == 0 else (3 if eb == 5 else 1)
            kbs.append((512 + eb * 128, ktk, 4 + eb, (0 if j == 0 else (1 if j == 2 else None)), qb))
        for kb in range(2):
            kbs.append((1280 + kb * 128, 4 + kb, 10 + kb, None, qb))
        attention(512 + qb * 128, 1, kbs, pb % 2)
        next(g0, None)
        pb += 1

    if stage == 5:
        return finish()
    drain(g0)
    for oc in range(KC):
        wv, wk = load_w(wo_d.ap().rearrange("(k p) n -> p k n", p=128)[:, :, oc * 128:(oc + 1) * 128], "p (k n) -> p k n", k=KC)
        for h in range(2):
            ps, pk = next_ps()
            for kc in range(KC):
                P.op("pe", lambda e, kc=kc, h=h, wv=wv, ps=ps: e.matmul(
                    ps[:, :], wv[:, kc, :], ATT[:, kc, h * 512:(h + 1) * 512], start=(kc == 0), stop=(kc == KC - 1)),
                    reads=[wk, ("ATT", kc, h)], writes=[pk])
            gated_residual(ps, pk, oc, h, 0, 2)
    if stage == 6:
        return finish()
    g1 = modulation_gen(1)
    mlp(0, g1)
    drain(g1)

    if stage == 7:
        return finish()
    P.fence()
    ssm_sc_d = din("ssm_sc", [128, 4, 64])
    s0_d = din("s0", [128, 2, 64])
    bpad_d = din("bpad", [8, 128, 2048])
    cpad_d = din("cpad", [8, 128, 2048])
    dskip_d = din("dskip", [128, KC])
    qsel_d = din("qsel", [128, 4])
    wa_d = din("glu_w_a", [D, D])
    wb_d = din("glu_w_b", [D, D])
    news_d = dout("new_s", [128, 2, 64, 2])
    g_in = nc.dram_tensor("g_in", [128, 128], F32)
    g_out = nc.dram_tensor("g_out", [512, 128], F32)
    tab_d = nc.dram_tensor("tab_scratch", [8, 128, 8192], F32)

    NTAB = 40
    TB = L0B[:, 0:2560].rearrange("p (a m) -> p a m", a=NTAB)
    CF = L0B[:, 2560:3584].bitcast(BF16)
    L0ROW = 3584

    def tb_ap(off, dims):
        return bass.AP(tensor=L0B, offset=off, ap=[[L0ROW, 128]] + dims)

    P0F = PT[0][:].rearrange("p a b -> p (a b)").bitcast(F32)
    P1F = PT[1][:].rearrange("p a b -> p (a b)").bitcast(F32)
    S0 = P0F[:, 0:128].rearrange("p (a m) -> p a m", a=2)
    FIN1 = P0F[:, 128:256].rearrange("p (a m) -> p a m", a=2)
    FIN2 = P0F[:, 256:512].rearrange("p (a m s) -> p a m s", a=2, s=2)
    NS = P0F[:, 512:768].rearrange("p (a m s) -> p a m s", a=2, s=2)
    TF = P0F[:, 768:896].rearrange("p (a m) -> p a m", a=2)
    ST = P0F[:, 896:1024].rearrange("p (a m) -> p a m", a=2)
    WI = P0F[:, 1024:1152].rearrange("p (a m) -> p a m", a=2)
    GG = P1F[:, 0:512].rearrange("p (q a m) -> p q a m", q=4, a=2)
    CH = P1F[:, 512:1024].rearrange("p (q a m) -> p q a m", q=4, a=2)
    NT1 = P1F[:, 1024:1152].rearrange("p (m s) -> p m s", s=2)
    NT2 = P1F[:, 1152:1280].rearrange("p (m s) -> p m s", s=2)
    hpi = nc.alloc_sbuf_tensor("s_hpi", [128, 1], F32)
    dskip = nc.alloc_sbuf_tensor("s_dskip", [128, KC], F32)
    qsel = nc.alloc_sbuf_tensor("s_qsel", [128, 4], F32)
    FTMP = nc.alloc_sbuf_tensor("s_FTMP", [128, 4], F32)
    fence_keys = [("X", kc, h) for kc in range(KC) for h in range(2)]
    (T_LR, T_TH, T_DT, T_R, T_R512, T_FRE, T_FIM, T_IFRE, T_IFIM, T_P1RE, T_P1IM, T_T1, T_T2, T_T3, T_T4) = range(15)
    T_CKC, T_CKS, T_LAMRE, T_LAMIM, T_LOGDT = 15, 25, 35, 36, 37

    def tb(i):
        return TB[:, i, :]

    tbk = lambda i: ("TB", i)

    def vop(eng, fn, reads, writes):
        P.op(eng, fn, reads=reads, writes=writes)

    def tt(eng, out, a, b, op, rk, wk):
        vop(eng, lambda e: e.tensor_tensor(out, a, b, op), rk, wk)

    P.op("sp", lambda e: e.dma_start(out=TB[:, T_LAMRE:T_LAMRE + 4, :], in_=ssm_sc_d.ap()), reads=fence_keys, writes=["ssm_sc"], dma=True)
    for i in range(T_LAMRE, T_LAMRE + 4):
        P.last_write[tbk(i)] = P.last_write["ssm_sc"]
    P.op("sp", lambda e: e.dma_start(out=S0, in_=s0_d.ap()), reads=fence_keys, writes=["S0"], dma=True)
    dma_in(dskip[:], dskip_d.ap(), "dskip")
    dma_in(qsel[:], qsel_d.ap(), "qsel")
    P.op("dve", lambda e: e.memset(hpi[:], math.pi / 2), writes=["hpi"])
    P.op("act", lambda e: e.activation(tb(T_DT), tb(T_LOGDT), AF.Exp), reads=[tbk(T_LOGDT)], writes=[tbk(T_DT)])
    tt("dve", tb(T_LR), tb(T_LAMRE), tb(T_DT), ALU.mult, [tbk(T_LAMRE), tbk(T_DT)], [tbk(T_LR)])
    tt("dve", tb(T_TH), tb(T_LAMIM), tb(T_DT), ALU.mult, [tbk(T_LAMIM), tbk(T_DT)], [tbk(T_TH)])
    P.op("act", lambda e: e.activation(tb(T_R), tb(T_LR), AF.Exp), reads=[tbk(T_LR)], writes=[tbk(T_R)])
    P.op("act", lambda e: e.activation(tb(T_R512), tb(T_LR), AF.Exp, scale=512.0), reads=[tbk(T_LR)], writes=[tbk(T_R512)])
    P.op("act", lambda e: e.activation(tb(T_CKS), tb(T_TH), AF.Sin, scale=1.0 / 64), reads=[tbk(T_TH)], writes=[tbk(T_CKS)])
    P.op("act", lambda e: e.activation(tb(T_CKC), tb(T_TH), AF.Sin, scale=1.0 / 64, bias=hpi[:, 0:1]), reads=[tbk(T_TH), "hpi"], writes=[tbk(T_CKC)])

    def square(ci, si, co, so):
        tt("dve", tb(T_T1), tb(ci), tb(ci), ALU.mult, [tbk(ci)], [tbk(T_T1)])
        tt("dve", tb(T_T2), tb(si), tb(si), ALU.mult, [tbk(si)], [tbk(T_T2)])
        vop("dve", lambda e: e.scalar_tensor_tensor(tb(so), tb(ci), 2.0, tb(si), ALU.mult, ALU.mult), [tbk(ci), tbk(si)], [tbk(so)])
        tt("dve", tb(co), tb(T_T1), tb(T_T2), ALU.subtract, [tbk(T_T1), tbk(T_T2)], [tbk(co)])

    for _ in range(6):
        square(T_CKC, T_CKS, T_CKC, T_CKS)
    for k in range(9):
        square(T_CKC + k, T_CKS + k, T_CKC + k + 1, T_CKS + k + 1)

    def cmul(eng, ore, oim, are, aim, bre, bim, rk, wk, t1, t2, tk1, tk2):
        tt(eng, t1, are, bre, ALU.mult, rk, [tk1])
        tt(eng, t2, aim, bim, ALU.mult, rk, [tk2])
        tt(eng, ore, t1, t2, ALU.subtract, [tk1, tk2], [wk[0]])
        tt(eng, t1, are, bim, ALU.mult, rk, [tk1])
        tt(eng, t2, aim, bre, ALU.mult, rk, [tk2])
        tt(eng, oim, t1, t2, ALU.add, [tk1, tk2], [wk[1]])

    tt("dve", tb(T_T3), tb(T_R), tb(T_CKC), ALU.mult, [tbk(T_R), tbk(T_CKC)], [tbk(T_T3)])
    tt("dve", tb(T_T4), tb(T_R), tb(T_CKS), ALU.mult, [tbk(T_R), tbk(T_CKS)], [tbk(T_T4)])
    vop("dve", lambda e: e.tensor_scalar(tb(T_T3), tb(T_T3), -1.0, None, ALU.add), [tbk(T_T3)], [tbk(T_T3)])
    tt("dve", tb(T_T1), tb(T_LAMRE), tb(T_LAMRE), ALU.mult, [tbk(T_LAMRE)], [tbk(T_T1)])
    tt("dve", tb(T_T2), tb(T_LAMIM), tb(T_LAMIM), ALU.mult, [tbk(T_LAMIM)], [tbk(T_T2)])
    tt("dve", tb(T_T1), tb(T_T1), tb(T_T2), ALU.add, [tbk(T_T1), tbk(T_T2)], [tbk(T_T1)])
    vop("dve", lambda e: e.reciprocal(tb(T_T1), tb(T_T1)), [tbk(T_T1)], [tbk(T_T1)])
    tt("dve", tb(T_FRE), tb(T_T3), tb(T_LAMRE), ALU.mult, [tbk(T_T3), tbk(T_LAMRE)], [tbk(T_FRE)])
    tt("dve", tb(T_T2), tb(T_T4), tb(T_LAMIM), ALU.mult, [tbk(T_T4), tbk(T_LAMIM)], [tbk(T_T2)])
    tt("dve", tb(T_FRE), tb(T_FRE), tb(T_T2), ALU.add, [tbk(T_FRE), tbk(T_T2)], [tbk(T_FRE)])
    tt("dve", tb(T_FRE), tb(T_FRE), tb(T_T1), ALU.mult, [tbk(T_FRE), tbk(T_T1)], [tbk(T_FRE)])
    tt("dve", tb(T_FIM), tb(T_T4), tb(T_LAMRE), ALU.mult, [tbk(T_T4), tbk(T_LAMRE)], [tbk(T_FIM)])
    tt("dve", tb(T_T2), tb(T_T3), tb(T_LAMIM), ALU.mult, [tbk(T_T3), tbk(T_LAMIM)], [tbk(T_T2)])
    tt("dve", tb(T_FIM), tb(T_FIM), tb(T_T2), ALU.subtract, [tbk(T_FIM), tbk(T_T2)], [tbk(T_FIM)])
    tt("dve", tb(T_FIM), tb(T_FIM), tb(T_T1), ALU.mult, [tbk(T_FIM), tbk(T_T1)], [tbk(T_FIM)])
    tt("dve", tb(T_T1), tb(T_FRE), tb(T_FRE), ALU.mult, [tbk(T_FRE)], [tbk(T_T1)])
    tt("dve", tb(T_T2), tb(T_FIM), tb(T_FIM), ALU.mult, [tbk(T_FIM)], [tbk(T_T2)])
    tt("dve", tb(T_T1), tb(T_T1), tb(T_T2), ALU.add, [tbk(T_T1), tbk(T_T2)], [tbk(T_T1)])
    vop("dve", lambda e: e.reciprocal(tb(T_T1), tb(T_T1)), [tbk(T_T1)], [tbk(T_T1)])
    tt("dve", tb(T_IFRE), tb(T_FRE), tb(T_T1), ALU.mult, [tbk(T_FRE), tbk(T_T1)], [tbk(T_IFRE)])
    vop("dve", lambda e: e.scalar_tensor_tensor(tb(T_IFIM), tb(T_FIM), -1.0, tb(T_T1), ALU.mult, ALU.mult), [tbk(T_FIM), tbk(T_T1)], [tbk(T_IFIM)])
    tt("dve", tb(T_P1RE), tb(T_R512), tb(T_CKC + 9), ALU.mult, [tbk(T_R512), tbk(T_CKC + 9)], [tbk(T_P1RE)])
    tt("dve", tb(T_P1IM), tb(T_R512), tb(T_CKS + 9), ALU.mult, [tbk(T_R512), tbk(T_CKS + 9)], [tbk(T_P1IM)])

    for h in range(2):
        norm_mod(X, H, h * 512, 512, 1, 0, h, xkey_h(h), hkey_h(h))

    BF = BIG[:].bitcast(F32)
    UC = BF[:, 0:4096].rearrange("p (a t) -> p a t", a=8)
    US = BF[:, 4096:8192].rearrange("p (a t) -> p a t", a=8)
    EW = [BF[:, 8192 + i * 512:8192 + (i + 1) * 512] for i in range(4)]
    UT = [BF[:, 8192:10240].rearrange("p (a t) -> p a t", a=8), BF[:, 10240:12288].rearrange("p (a t) -> p a t", a=8)]
    SBS = [BIG[:, 24576:25600].rearrange("p (a t) -> p a t", a=2), BIG[:, 29696:30720].rearrange("p (a t) -> p a t", a=2)]
    CFT1 = BF[:, 15360:15488]
    CFT2 = BF[:, 15488:15616]
    sb_i = [0]
    G0 = XH[:].bitcast(BF16)
    G1 = BIG[:, 25600:29696].rearrange("p (c t) -> p c t", c=8)

    def Gv(kc, h):
        return (G0 if h == 0 else G1)[:, kc, :]

    def build_tables(gh):
        P.op("dve", lambda e: e.memset(UC[:, :, 0:1], 1.0), reads=fence_keys, writes=["UC"])
        P.op("dve", lambda e: e.memset(US[:, :, 0:1], 0.0), reads=fence_keys, writes=["US"])
        for k in range(9):
            n = 1 << k
            ckc = tb_ap((T_CKC + k) * 64 + 4 * gh, [[32, 2], [1, 4], [0, n]])
            cks = tb_ap((T_CKS + k) * 64 + 4 * gh, [[32, 2], [1, 4], [0, n]])
            v4 = lambda ap: ap.rearrange("p (d m) t -> p d m t", d=2)
            sc_, ss_ = v4(UC[:, :, 0:n]), v4(US[:, :, 0:n])
            dc_, ds_ = v4(UC[:, :, n:2 * n]), v4(US[:, :, n:2 * n])
            rk = ["UC", "US", tbk(T_CKC + k), tbk(T_CKS + k)]
            if n <= 128:
                t1, t3 = v4(UT[0][:, :, 0:n]), v4(UT[0][:, :, 128:128 + n])
                t2, t4 = v4(UT[1][:, :, 0:n]), v4(UT[1][:, :, 128:128 + n])
                tt("dve", t1, sc_, ckc, ALU.mult, rk, ["UT0a"])
                tt("dve", t2, ss_, cks, ALU.mult, rk, ["UT1a"])
                tt("dve", t3, sc_, cks, ALU.mult, rk, ["UT0b"])
                tt("dve", t4, ss_, ckc, ALU.mult, rk, ["UT1b"])
                tt("dve", dc_, t1, t2, ALU.subtract, ["UT0a", "UT1a"], ["UC"])
                tt("dve", ds_, t3, t4, ALU.add, ["UT0b", "UT1b"], ["US"])
            else:
                t1, t2 = v4(UT[0][:, :, 0:n]), v4(UT[1][:, :, 0:n])
                k0, k1 = ["UT0a", "UT0b"], ["UT1a", "UT1b"]
                tt("dve", t1, sc_, ckc, ALU.mult, rk, k0)
                tt("dve", t2, ss_, cks, ALU.mult, rk, k1)
                tt("dve", dc_, t1, t2, ALU.subtract, k0 + k1, ["UC"])
                tt("dve", t1, sc_, cks, ALU.mult, rk, k0)
                tt("dve", t2, ss_, ckc, ALU.mult, rk, k1)
                tt("dve", ds_, t1, t2, ALU.add, k0 + k1, ["US"])

    def seg_ap(ap512, lo, n, rev):
        v = ap512[:, lo:lo + n]
        return v[:, ::-1] if rev else v

    RS = [BF[:, 10240 + i * 512:10240 + (i + 1) * 512] for i in range(4)]

    def hv(ap512, half):
        return ap512.rearrange("p (s t) -> p s t", s=2) if half == 0 else ap512

    def tabv(T, a, half, d):
        n = 256 if half == 0 else 512
        v = T[:, a, 0:n]
        if d == 1:
            v = v[:, ::-1]
        return v.unsqueeze(1).broadcast_to([128, 2, 256]) if half == 0 else v

    def demod(a, md, half, ps_re, pk_re, ps_im, pk_im, d):
        uc, us = tabv(UC, a, half, d), tabv(US, a, half, d)
        xr, xi = hv(ps_re[:, :], half), hv(ps_im[:, :], half)
        t = [hv(SCR[i][:, :], half) for i in range(4)]
        rk = [pk_re, pk_im, "UC", "US"]
        tt("dve", t[0], xr, uc, ALU.mult, rk, [("SCR", 0)])
        tt("dve", t[1], xi, us, ALU.mult, rk, [("SCR", 1)])
        tt("dve", t[2], xi, uc, ALU.mult, rk, [("SCR", 2)])
        tt("dve", t[3], xr, us, ALU.mult, rk, [("SCR", 3)])
        tt("dve", hv(EW[0], half), t[0], t[1], ALU.add, [("SCR", 0), ("SCR", 1)], [("EW", 0)])
        tt("dve", hv(EW[1], half), t[2], t[3], ALU.subtract, [("SCR", 2), ("SCR", 3)], [("EW", 1)])

    def scans(md, half, d, use_init):
        segs = [(0, 256), (256, 256)] if half == 0 else [(0, 512)]
        for (lo, n) in segs:
            rbc = tb_ap(T_R * 64 + md, [[0, n]])
            for ri in range(2):
                init = WI[:, ri, md:md + 1] if use_init else 0.0
                src = seg_ap(EW[ri], lo, n, d == 1)
                dst = seg_ap(EW[2 + ri], lo, n, d == 1)
                vop("dve", lambda e, dst=dst, rbc=rbc, src=src, init=init: e.tensor_tensor_scan(dst, rbc, src, init, ALU.mult, ALU.add),
                    [("EW", ri), tbk(T_R)] + (["WI"] if use_init else []), [("EW", 2 + ri)])

    def finals(a, md, half, d, dst):
        L = 256 if half == 0 else 512
        ncol = 2 if half == 0 else 1
        c0 = (L - 1) if d == 0 else 0
        wre = EW[2][:, c0::256][:, 0:ncol] if half == 0 else EW[2][:, c0:c0 + 1]
        wim = EW[3][:, c0::256][:, 0:ncol] if half == 0 else EW[3][:, c0:c0 + 1]
        ucl, usl = UC[:, a, L - 1:L], US[:, a, L - 1:L]
        tA, tB = FTMP[:, 0:ncol], FTMP[:, 2:2 + ncol]
        rk = [("EW", 2), ("EW", 3), "UC", "US"]
        vop("dve", lambda e: e.tensor_scalar(tA, wim, usl, None, ALU.mult), rk, ["FTMPA"])
        vop("dve", lambda e: e.tensor_scalar(tB, wre, usl, None, ALU.mult), rk, ["FTMPB"])
        vop("dve", lambda e: e.scalar_tensor_tensor(dst[0], wre, ucl, tA, ALU.mult, ALU.subtract), rk + ["FTMPA"], [dst[2]])
        vop("dve", lambda e: e.scalar_tensor_tensor(dst[1], wim, ucl, tB, ALU.mult, ALU.add), rk + ["FTMPB"], [dst[3]])

    def remod(a, half, d, SB, sbi):
        uc, us = tabv(UC, a, half, d), tabv(US, a, half, d)
        wr, wi = hv(EW[2], half), hv(EW[3], half)
        t = [hv(RS[i], half) for i in range(4)]
        rk = [("EW", 2), ("EW", 3), "UC", "US"]
        tt("dve", t[0], wr, uc, ALU.mult, rk, [("RS", 0)])
        tt("dve", t[1], wi, us, ALU.mult, rk, [("RS", 1)])
        tt("dve", t[2], wi, uc, ALU.mult, rk, [("RS", 2)])
        tt("dve", t[3], wr, us, ALU.mult, rk, [("RS", 3)])
        tt("dve", hv(SB[:, 0, :], half), t[0], t[1], ALU.subtract, [("RS", 0), ("RS", 1)], [("SB", sbi, 0)])
        vop("dve", lambda e: e.scalar_tensor_tensor(hv(SB[:, 1, :], half), t[2], -1.0, t[3], ALU.mult, ALU.subtract),
            [("RS", 2), ("RS", 3)], [("SB", sbi, 1)])

    def x_matmuls(bv, bk, gh, d, m4, half):
        outs = []
        for ri in range(2):
            ps, pk = next_ps()
            col = ((d * 2 + ri) * 4 + m4) * 128
            P.op("pe", lambda e, ps=ps, col=col: e.matmul(ps[:, :], bv[:, col:col + 128], H[:, gh, half * 512:(half + 1) * 512], start=True, stop=True),
                 reads=[bk, ("H", gh, half)], writes=[pk])
            outs += [ps, pk]
        return outs

    ps_mod[0] = 6
    bpv = lambda gh: bpad_d.ap()[gh]
    for gh in range(8):
        build_tables(gh)
        P.op("sp", lambda e, gh=gh: e.dma_start(out=tab_d.ap()[gh], in_=BF[:, 0:8192]), reads=["UC", "US"], writes=[("tab_d", gh)], dma=True)
        bv, bk = load_w(bpv(gh))
        for d in range(2):
            for m4 in range(4):
                a = d * 4 + m4
                md = d * 32 + 4 * gh + m4
                psr, pkr, psi, pki = x_matmuls(bv, bk, gh, d, m4, 1)
                demod(a, md, 1, psr, pkr, psi, pki, d)
                scans(md, 1, d, False)
                finals(a, md, 1, d, (FIN1[:, 0, md:md + 1], FIN1[:, 1, md:md + 1], ("FIN1", md), ("FIN1", md)))
    fin1k = [("FIN1", md) for md in range(64)]
    cmul("dve", TF[:, 0, :], TF[:, 1, :], FIN1[:, 0, :], FIN1[:, 1, :], tb(T_FRE), tb(T_FIM),
         fin1k + [tbk(T_FRE), tbk(T_FIM)], ["TF", "TF"], tb(T_T1), tb(T_T2), tbk(T_T1), tbk(T_T2))
    P.op("sp", lambda e: e.dma_start(out=g_in.ap(), in_=TF.rearrange("p a b -> p (a b)")), reads=["TF"], writes=["g_in"], dma=True)
    P.op("pool", lambda e: e.collective_compute("AllGather", ALU.bypass, replica_groups=[[0, 1, 2, 3], [4, 5, 6, 7]],
                                                  ins=[g_in.ap().opt()], outs=[g_out.ap().opt()]),
         reads=["g_in"], writes=["g_out"], cc=True)
    P.op("sp", lambda e: e.dma_start(out=GG.rearrange("p q a b -> p q (a b)"), in_=g_out.ap().rearrange("(q p) n -> p q n", p=128)),
         reads=["g_out"], writes=["GG"], dma=True)
    for d in range(2):
        sl = slice(d * 32, (d + 1) * 32)
        q0 = 0 if d == 0 else 3
        for ri in range(2):
            vop("dve", lambda e, ri=ri, sl=sl, q0=q0: e.tensor_copy(CH[:, q0, ri, sl], S0[:, ri, sl]), ["S0"], [("CH", q0, d)])
        order = [(0, 1, 0), (1, 2, 1), (2, 3, 2)] if d == 0 else [(3, 2, 3), (2, 1, 2), (1, 0, 1)]
        for (qs, qd, qg) in order:
            cmul("dve", CH[:, qd, 0, sl], CH[:, qd, 1, sl], CH[:, qs, 0, sl], CH[:, qs, 1, sl], TB[:, T_P1RE, sl], TB[:, T_P1IM, sl],
                 [("CH", qs, d), tbk(T_P1RE), tbk(T_P1IM)], [("CH", qd, d), ("CH", qd, d)], TB[:, T_T1, sl], TB[:, T_T2, sl], tbk(T_T1), tbk(T_T2))
            for ri in range(2):
                tt("dve", CH[:, qd, ri, sl], CH[:, qd, ri, sl], GG[:, qg, ri, sl], ALU.add, [("CH", qd, d), "GG"], [("CH", qd, d)])
    chk = [("CH", q, d) for q in range(4) for d in range(2)]
    for ri in range(2):
        vop("dve", lambda e, ri=ri: e.tensor_scalar(ST[:, ri, :], CH[:, 0, ri, :], qsel[:, 0:1], None, ALU.mult), chk + ["qsel"], [("ST", ri)])
        for q in range(1, 4):
            vop("dve", lambda e, ri=ri, q=q: e.scalar_tensor_tensor(ST[:, ri, :], CH[:, q, ri, :], qsel[:, q:q + 1], ST[:, ri, :], ALU.mult, ALU.add),
                chk + ["qsel", ("ST", ri)], [("ST", ri)])
    cmul("dve", TF[:, 0, :], TF[:, 1, :], ST[:, 0, :], ST[:, 1, :], tb(T_IFRE), tb(T_IFIM),
         [("ST", 0), ("ST", 1), tbk(T_IFRE), tbk(T_IFIM), "g_in"], ["TF2", "TF2"], tb(T_T1), tb(T_T2), tbk(T_T1), tbk(T_T2))
    cmul("dve", WI[:, 0, :], WI[:, 1, :], TF[:, 0, :], TF[:, 1, :], tb(T_CKC), tb(T_CKS),
         ["TF2", tbk(T_CKC), tbk(T_CKS)], ["WI", "WI"], tb(T_T1), tb(T_T2), tbk(T_T1), tbk(T_T2))

    cpv = lambda gh: cpad_d.ap()[gh]
    for gh in range(8):
        P.op("sp", lambda e, gh=gh: e.dma_start(out=BF[:, 0:8192], in_=tab_d.ap()[gh]), reads=[("tab_d", gh)], writes=["UC", "US"], dma=True)
        bv, bk = load_w(bpv(gh))
        cv_, ck_ = load_w(cpv(gh))
        for d in range(2):
            for m4 in range(4):
                md = d * 32 + 4 * gh + m4
                cre = cv_[:, ((d * 2 + 0) * 4 + m4) * 128:((d * 2 + 0) * 4 + m4 + 1) * 128]
                cim = cv_[:, ((d * 2 + 1) * 4 + m4) * 128:((d * 2 + 1) * 4 + m4 + 1) * 128]
                ore = CF[:, ((d * 2 + 0) * 4 + m4) * 128:((d * 2 + 0) * 4 + m4 + 1) * 128]
                oim = CF[:, ((d * 2 + 1) * 4 + m4) * 128:((d * 2 + 1) * 4 + m4 + 1) * 128]
                fre, fim = TB[:, T_FRE, md:md + 1], TB[:, T_FIM, md:md + 1]
                rk = [ck_, tbk(T_FRE), tbk(T_FIM)]
                vop("dve", lambda e, cim=cim, fim=fim: e.tensor_scalar(CFT1, cim, fim, None, ALU.mult), rk, ["CFT1"])
                vop("dve", lambda e, cim=cim, fre=fre: e.tensor_scalar(CFT2, cim, fre, None, ALU.mult), rk, ["CFT2"])
                vop("dve", lambda e, ore=ore, cre=cre, fre=fre: e.scalar_tensor_tensor(ore, cre, fre, CFT1, ALU.mult, ALU.subtract),
                    rk + ["CFT1"], [("CF", d, m4, 0)])
                vop("dve", lambda e, oim=oim, cre=cre, fim=fim: e.scalar_tensor_tensor(oim, cre, fim, CFT2, ALU.mult, ALU.add),
                    rk + ["CFT2"], [("CF", d, m4, 1)])
        ysl = [(PS[6], ("ps", 6)), (PS[7], ("ps", 7))]
        cnt = [0, 0]
        for d in range(2):
            for m4 in range(4):
                a = d * 4 + m4
                md = d * 32 + 4 * gh + m4
                for half in range(2):
                    psr, pkr, psi, pki = x_matmuls(bv, bk, gh, d, m4, half)
                    demod(a, md, half, psr, pkr, psi, pki, d)
                    scans(md, half, d, half == 1)
                    sbi = sb_i[0] % 2
                    sb_i[0] += 1
                    SB = SBS[sbi]
                    remod(a, half, d, SB, sbi)
                    if half == 0:
                        finals(a, md, 0, d, (FIN2[:, 0, md, :], FIN2[:, 1, md, :], ("FIN2", md), ("FIN2", md)))
                    psy, pky = ysl[half]
                    for ri in range(2):
                        col = ((d * 2 + ri) * 4 + m4) * 128
                        first = cnt[half] == 0
                        last = cnt[half] == 15
                        cnt[half] += 1
                        P.op("pe", lambda e, psy=psy, col=col, ri=ri, first=first, last=last, SB=SB: e.matmul(
                            psy[:, :], CF[:, col:col + 128], SB[:, ri, :], start=first, stop=last),
                            reads=[("CF", d, m4, ri), ("SB", sbi, ri)], writes=[pky])
        for half in range(2):
            psy, pky = ysl[half]
            yp, t1, t2 = RS[0], RS[1], RS[2]
            hs = slice(half * 512, (half + 1) * 512)
            vop("dve", lambda e, psy=psy, hs=hs, gh=gh: e.scalar_tensor_tensor(yp, H[:, gh, hs], dskip[:, gh:gh + 1], psy[:, :], ALU.mult, ALU.add),
                [pky, ("H", gh, half), "dskip"], [("RS", 0)])
            tt("pool", t1, yp, yp, ALU.mult, [("RS", 0)], [("RS", 1)])
            vop("pool", lambda e: e.tensor_scalar(t1, t1, 0.044715, 1.0, ALU.mult, ALU.add), [("RS", 1)], [("RS", 1)])
            tt("pool", t1, t1, yp, ALU.mult, [("RS", 1), ("RS", 0)], [("RS", 1)])
            P.op("act", lambda e: e.activation(t2, t1, AF.Sigmoid, scale=1.5957691216057308), reads=[("RS", 1)], writes=[("RS", 2)])
            tt("pool", Gv(gh, half), yp, t2, ALU.mult, [("RS", 0), ("RS", 2)], [("G", gh, half)])
    ps_mod[0] = 8
    fin2k = [("FIN2", md) for md in range(64)]
    fre_b = tb_ap(T_FRE * 64, [[1, 64], [0, 2]])
    fim_b = tb_ap(T_FIM * 64, [[1, 64], [0, 2]])
    cmul("dve", NS[:, 0], NS[:, 1], FIN2[:, 0], FIN2[:, 1], fre_b, fim_b, fin2k + [tbk(T_FRE), tbk(T_FIM)],
         ["NS", "NS"], NT1, NT2, "NT1", "NT2")
    P.op("sp", lambda e: e.dma_start(out=news_d.ap(), in_=NS), reads=["NS"], writes=["news_out"], dma=True)

    for oc in range(KC):
        wav, wak = load_w(wa_d.ap().rearrange("(k p) n -> p k n", p=128)[:, :, oc * 128:(oc + 1) * 128], "p (k n) -> p k n", k=KC)
        wbv, wbk = load_w(wb_d.ap().rearrange("(k p) n -> p k n", p=128)[:, :, oc * 128:(oc + 1) * 128], "p (k n) -> p k n", k=KC)
        for h in range(2):
            psa, pka = next_ps()
            psb, pkb = next_ps()
            for (wv_, wk_, ps_, pk_) in ((wav, wak, psa, pka), (wbv, wbk, psb, pkb)):
                for kc in range(KC):
                    P.op("pe", lambda e, kc=kc, h=h, wv_=wv_, ps_=ps_: e.matmul(
                        ps_[:, :], wv_[:, kc, :], Gv(kc, h), start=(kc == 0), stop=(kc == KC - 1)),
                        reads=[wk_, ("G", kc, h)], writes=[pk_])
            sg, pr = SCR[0], SCR[1]
            P.op("act", lambda e, psb=psb: e.activation(sg[:], psb[:, :], AF.Sigmoid), reads=[pkb], writes=[("SCR", 0)])
            tt("dve", pr[:], psa[:, :], sg[:], ALU.mult, [pka, ("SCR", 0)], [("SCR", 1)])
            P.op("dve", lambda e, oc=oc, h=h: e.scalar_tensor_tensor(
                X[:, oc, h * 512:(h + 1) * 512], pr[:], mod_piece(1, 2, oc, h), X[:, oc, h * 512:(h + 1) * 512], ALU.mult, ALU.add),
                reads=[("SCR", 1), ("modT", 1, 2), ("X", oc, h)], writes=[("X", oc, h)])
    mlp(1)
    YF = BF[:, 0:4096].rearrange("p (c t) -> p c t", c=8)
    for h in range(2):
        norm_mod(X, YF, h * 512, 512, 1, 0, h, xkey_h(h), lambda kc: ("YF", kc), final=True)
        P.op("sp", lambda e, h=h: e.dma_start(out=y_d.ap()[:, :, h * 512:(h + 1) * 512], in_=YF[:, :, 0:512]),
             reads=[("YF", kc) for kc in range(KC)], writes=[("y_out", h)], dma=True)
    return P.emit()


def _fm(x_tok):
    T = x_tok.shape[0]
    return np.ascontiguousarray(x_tok.reshape(T, KC, 128).transpose(2, 1, 0))


def _rope_tables(q):
    pos = 512 * q - 128 + np.arange(768)
    row = (pos // 64).astype(np.float32)
    col = (pos % 64).astype(np.float32)
    freqs = (10000.0 ** (-np.arange(16, dtype=np.float32) / 16)).astype(np.float32)
    tab = np.zeros((128, 2, 768), np.float32)
    for p in range(128):
        d = p % 64
        f = freqs[d % 16]
        ang = (row if d < 32 else col) * f
        tab[p, 0] = np.cos(ang)
        tab[p, 1] = np.sin(ang) * (-1.0 if (d % 32) < 16 else 1.0)
    return tab


def _amask(q):
    m = np.zeros((128, 4, 2, 128), np.float32)
    jj = np.arange(128)[:, None]
    rr = np.arange(128)[None, :]
    for qb in range(4):
        lpos0 = 512 * q + 128 * qb - 128
        rpos0 = 512 * q + 128 * qb + 128
        m[:, qb, 0, :] = (jj >= rr) * (1.0 if lpos0 >= 0 else 0.0)
        m[:, qb, 1, :] = (jj <= rr) * (1.0 if rpos0 < 2048 else 0.0)
    return m


def _prep_shared(inp):
    f = np.float32
    sh = {}
    w_qkv = inp["w_qkv"][0]
    wq = w_qkv[:, :1024]
    wk = w_qkv[:, 1024:1280]
    wv = w_qkv[:, 1280:1536]
    d = np.arange(64)
    partner = np.where((d % 32) < 16, d + 16, d - 16)
    qperm = (np.arange(16)[:, None] * 64 + partner[None, :]).reshape(-1)
    kperm = (np.arange(4)[:, None] * 64 + partner[None, :]).reshape(-1)
    dup = (np.arange(4)[:, None, None] * 64 + np.zeros((1, 2, 1), int) + d[None, None, :]).reshape(-1)
    sh["wq"] = np.ascontiguousarray(wq)
    sh["wqp"] = np.ascontiguousarray(wq[:, qperm])
    sh["wkd"] = np.ascontiguousarray(wk[:, dup])
    sh["wkdp"] = np.ascontiguousarray(wk[:, kperm][:, dup])
    sh["wkv"] = np.ascontiguousarray(np.concatenate([wk, wv], axis=1))
    sh["w_o"] = np.ascontiguousarray(inp["w_o"][0])
    sh["w_mod"] = np.ascontiguousarray(inp["w_mod"])
    sh["mlp_w1"] = np.ascontiguousarray(inp["mlp_w1"])
    sh["mlp_w2"] = np.ascontiguousarray(inp["mlp_w2"])
    ngs = np.stack([inp["norm1_g"][0], inp["norm2_g"][0], inp["norm1_g"][1], inp["norm2_g"][1], inp["final_norm_g"]], 0)
    sh["ng"] = np.ascontiguousarray(ngs.reshape(5, KC, 128).transpose(2, 0, 1)).astype(f)
    sh["bmodT"] = np.ascontiguousarray(inp["b_mod"].reshape(2, 48, 128).transpose(2, 0, 1)).astype(f)
    sh["ident"] = np.eye(128, dtype=f)
    sh["sink"] = np.ascontiguousarray(np.broadcast_to(inp["attn_sink"][0][None, :], (128, 16))).astype(f)
    def gp_md(a):
        return a.reshape(2, 32, 2, 64).transpose(2, 3, 0, 1).reshape(128, 64)
    ldt = np.broadcast_to(inp["ssm_log_dt"][0][:, :, None], (2, 64, 64))
    sc = np.zeros((128, 4, 64), f)
    sc[:, 0] = gp_md(inp["ssm_lam_re"][0])
    sc[:, 1] = gp_md(inp["ssm_lam_im"][0])
    sc[:, 2] = gp_md(ldt)
    sh["ssm_sc"] = sc
    bp = np.zeros((8, 8, 16, 2, 2, 4, 2, 64), f)
    cp = np.zeros((8, 2, 64, 2, 2, 4, 8, 16), f)
    Bs = (inp["ssm_b_re"][0], inp["ssm_b_im"][0])
    Cs = (inp["ssm_c_re"][0], inp["ssm_c_im"][0])
    for gh in range(8):
        for m4 in range(4):
            for g2 in range(2):
                g = 8 * gh + 2 * m4 + g2
                gl = 2 * m4 + g2
                for d in range(2):
                    for ri in range(2):
                        bp[gh, gl, :, d, ri, m4, g2, :] = Bs[ri][d, g].T
                        cp[gh, g2, :, d, ri, m4, gl, :] = Cs[ri][d, g].T
    sh["bpad"] = bp.reshape(8, 128, 2048)
    sh["cpad"] = cp.reshape(8, 128, 2048)
    sh["dskip"] = np.ascontiguousarray(inp["ssm_d"][0].reshape(KC, 128).T).astype(f)
    sh["glu_w_a"] = np.ascontiguousarray(inp["glu_w_a"][0])
    sh["glu_w_b"] = np.ascontiguousarray(inp["glu_w_b"][0])
    return sh


def _prep_core(inp, r, sh):
    b, q = r // 4, r % 4
    xs = inp["x_sample"][b]
    own = np.concatenate([inp["x_prompt"][2 * r], inp["x_prompt"][2 * r + 1], xs[512 * q:512 * q + 512]], 0)
    halo = np.zeros((256, D), np.float32)
    if q > 0:
        halo[:128] = xs[512 * q - 128:512 * q]
    if q < 3:
        halo[128:] = xs[512 * q + 512:512 * q + 640]
    cv = np.stack([inp["c_ctx"], inp["c"][b]], 1)
    m = dict(sh)
    m["x_own"] = _fm(own)
    m["x_halo"] = _fm(halo)
    m["cvec"] = np.ascontiguousarray(cv.reshape(KC, 128, 2).transpose(1, 0, 2)).astype(np.float32)
    m["rope"] = _rope_tables(q)
    import ml_dtypes
    m["amask"] = _amask(q).astype(ml_dtypes.bfloat16)
    m["ck"] = np.ascontiguousarray(inp["cache_k"][b, 0].reshape(256, 256))
    m["cv"] = np.ascontiguousarray(inp["cache_v"][b, 0].reshape(256, 256))
    st = inp["state_ssm"][b, 0]
    m["s0"] = np.ascontiguousarray(st.reshape(2, 2, 32, 2, 64).transpose(3, 4, 1, 0, 2).reshape(128, 2, 64)).astype(np.float32)
    qs = np.zeros((128, 4), np.float32)
    qs[:, q] = 1.0
    m["qsel"] = qs
    return m


_NC_CACHE = {}


STAGE = 99


def kernel(**inputs):
    inp = {k: np.asarray(v) for k, v in inputs.items()}
    sh = _prep_shared(inp)
    in_maps = [_prep_core(inp, r, sh) for r in range(NCORE)]
    nc = build_program(STAGE)
    res = run_bass_kernel_spmd(nc, in_maps, core_ids=list(range(NCORE)))
    y_prompt = np.zeros((16, 256, D), np.float32)
    y_sample = np.zeros((2, 2048, D), np.float32)
    new_k = np.zeros((16, 1, 256, 4, 64), np.float32)
    new_v = np.zeros((16, 1, 256, 4, 64), np.float32)
    new_s = np.zeros((16, 1, 2, 2, 64, 64), np.float32)
    for r in range(NCORE):
        o = res.results[r]
        b, q = r // 4, r % 4
        y = np.asarray(o["y"]).transpose(2, 1, 0).reshape(NT, D)
        y_prompt[2 * r] = y[0:256]
        y_prompt[2 * r + 1] = y[256:512]
        y_sample[b, 512 * q:512 * q + 512] = y[512:1024]
        nk = np.asarray(o["new_k"]).reshape(2, 256, 4, 64)
        nv = np.asarray(o["new_v"]).reshape(2, 256, 4, 64)
        new_k[2 * r:2 * r + 2, 0] = nk
        new_v[2 * r:2 * r + 2, 0] = nv
        ns = np.asarray(o["new_s"]).reshape(2, 64, 2, 2, 32, 2)
        new_s[2 * r:2 * r + 2, 0] = ns.transpose(5, 3, 2, 4, 0, 1).reshape(2, 2, 2, 64, 64)
    return (y_prompt, y_sample, new_k, new_v, new_s)
```

```python
import math
import os
import numpy as np
import concourse.bass as bass
import concourse.mybir as mybir
from concourse.bass_utils import run_bass_kernel_spmd

F32 = mybir.dt.float32
BF16 = mybir.dt.bfloat16
ALU = mybir.AluOpType
AF = mybir.ActivationFunctionType

NCORE = 8
D = 1024
KC = 8
NT = 1024
EPS = 1e-6
TWO_PI = 2.0 * math.pi


class Op:
    __slots__ = ("eng", "fn", "kind", "deps", "signal", "sig_val", "dsem", "dval", "prev_dma", "idx")

    def __init__(self, eng, fn, kind):
        self.eng = eng
        self.fn = fn
        self.kind = kind
        self.deps = []
        self.signal = False
        self.sig_val = None
        self.dsem = None
        self.dval = None
        self.prev_dma = None
        self.idx = None


class Prog:
    ENGS = ("pe", "act", "dve", "pool", "sp")
    NDMA = {"sp": 20, "pool": 20, "act": 6}

    def __init__(self):
        self.nc = bass.Bass("TRN2", target_bir_lowering=False)
        self.ops = {e: [] for e in self.ENGS}
        self.last_write = {}
        self.readers = {}
        self.dma_count = {e: 0 for e in self.ENGS}
        self.dma_hist = {e: [] for e in self.ENGS}
        self.cc_ops = []

    def op(self, eng, fn, reads=(), writes=(), dma=False, cc=False):
        kind = "cc" if cc else ("dma" if dma else None)
        o = Op(eng, fn, kind)
        deps = {}
        for k in reads:
            w = self.last_write.get(k)
            if w is not None:
                deps[id(w)] = (w, True)
        for k in writes:
            w = self.last_write.get(k)
            if w is not None and id(w) not in deps:
                deps[id(w)] = (w, False)
            for r in self.readers.get(k, ()):
                if id(r) not in deps:
                    deps[id(r)] = (r, False)
        for d, raw in deps.values():
            if d is o:
                continue
            if d.kind is None and kind is None and d.eng == eng:
                if eng == "pe" or (eng in ("dve", "act") and not raw):
                    continue
            o.deps.append(d)
            if d.kind is None:
                d.signal = True
        for k in reads:
            self.readers.setdefault(k, []).append(o)
        for k in writes:
            self.last_write[k] = o
            self.readers[k] = []
        if kind == "dma":
            n = self.NDMA[eng]
            i = self.dma_count[eng]
            self.dma_count[eng] += 1
            o.idx = i
            hist = self.dma_hist[eng]
            if i >= n:
                o.prev_dma = hist[i - n]
            hist.append(o)
        elif kind == "cc":
            o.idx = len(self.cc_ops)
            self.cc_ops.append(o)
        self.ops[eng].append(o)
        return o

    def fence(self, engs=("pe", "act", "dve", "pool", "sp")):
        deps = []
        for e in ("pe", "act", "dve", "pool"):
            comp = [o for o in self.ops[e] if o.kind is None]
            if comp:
                deps.append(comp[-1])
        for e, n in self.NDMA.items():
            deps += self.dma_hist[e][-n:]
        deps += self.cc_ops[-1:]
        for e in engs:
            o = Op(e, lambda eng: eng.nop(), None)
            o.deps = list(deps)
            for d in o.deps:
                if d.kind is None:
                    d.signal = True
            self.ops[e].append(o)

    def emit(self):
        nc = self.nc
        sems = {e: nc.alloc_semaphore("c_" + e) for e in ("pe", "act", "dve", "pool")}
        dsems = {e: [nc.alloc_semaphore(f"d_{e}{i}") for i in range(n)] for e, n in self.NDMA.items()}
        ccsem = nc.alloc_semaphore("ccsem")
        for e in self.ENGS:
            c = 0
            for o in self.ops[e]:
                if o.kind == "dma":
                    n = self.NDMA[e]
                    o.dsem = dsems[e][o.idx % n]
                    o.dval = 16 * (o.idx // n + 1)
                elif o.kind == "cc":
                    o.dsem = ccsem
                    o.dval = o.idx + 1
                elif o.signal:
                    c += 1
                    o.sig_val = c
        engobj = {"pe": "tensor", "act": "scalar", "dve": "vector", "pool": "gpsimd", "sp": "sync"}
        with nc.Block() as block:
            for e in self.ENGS:
                ops = self.ops[e]

                def body(eng, e=e, ops=ops):
                    waited = {}
                    for o in ops:
                        need = []
                        for d in o.deps:
                            if d.kind is not None:
                                need.append((d.dsem, d.dval))
                            else:
                                need.append((sems[d.eng], d.sig_val))
                        if o.prev_dma is not None:
                            need.append((o.prev_dma.dsem, o.prev_dma.dval))
                        for s, v in need:
                            k = id(s)
                            if waited.get(k, 0) >= v:
                                continue
                            waited[k] = v
                            eng.wait_ge(s, v)
                        ins = o.fn(eng)
                        if o.kind == "dma":
                            ins.then_inc(o.dsem, 16)
                        elif o.kind == "cc":
                            ins.then_inc(o.dsem)
                        elif o.signal:
                            ins.then_inc(sems[e], 1)
                    last = {}
                    for o in ops:
                        if o.kind is not None:
                            last[id(o.dsem)] = (o.dsem, o.dval)
                    for s, v in last.values():
                        if waited.get(id(s), 0) < v:
                            eng.wait_ge(s, v)

                getattr(block, engobj[e])(body)
        return nc


def build_program(stage=99):
    P = Prog()
    nc = P.nc
    dbg = {}

    def din(name, shape, dt=F32):
        return nc.dram_tensor(name, list(shape), dt, kind="ExternalInput")

    def dout(name, shape, dt=F32):
        return nc.dram_tensor(name, list(shape), dt, kind="ExternalOutput")

    x_own = din("x_own", [128, KC, NT])
    x_halo = din("x_halo", [128, KC, 256])
    cvec = din("cvec", [128, KC, 2])
    ng_d = din("ng", [128, 5, KC])
    bmod_d = din("bmodT", [128, 2, 48])
    rope_d = din("rope", [128, 2, 768])
    amask_d = din("amask", [128, 4, 2, 128], BF16)
    ident_d = din("ident", [128, 128])
    sink_d = din("sink", [128, 16])
    ck_d = din("ck", [256, 256])
    cv_d = din("cv", [256, 256])
    w_mod = din("w_mod", [2, D, 6 * D])
    wq_d = din("wq", [D, D])
    wqp_d = din("wqp", [D, D])
    wkd_d = din("wkd", [D, 512])
    wkdp_d = din("wkdp", [D, 512])
    wkv_d = din("wkv", [D, 512])
    wo_d = din("w_o", [D, D])
    w1_d = din("mlp_w1", [2, D, 4 * D])
    w2_d = din("mlp_w2", [2, 4 * D, D])
    y_d = dout("y", [128, KC, NT])
    nk_d = dout("new_k", [512, 256])
    nv_d = dout("new_v", [512, 256])

    X = nc.alloc_sbuf_tensor("s_X", [128, KC, NT], F32)
    H = nc.alloc_sbuf_tensor("s_H", [128, KC, NT], BF16)
    BIG = nc.alloc_sbuf_tensor("s_BIG", [128, 32768], BF16)
    WB = [nc.alloc_sbuf_tensor(f"s_WB{i}", [128, 4096], BF16) for i in range(3)]
    PT = [nc.alloc_sbuf_tensor(f"s_PT{i}", [128, 5, 512], BF16) for i in range(2)]
    SCR = [nc.alloc_sbuf_tensor(f"s_SCR{i}", [128, 512], F32) for i in range(4)]
    SQ = nc.alloc_sbuf_tensor("s_SQ", [128, KC, 512], BF16)
    RSTD = nc.alloc_sbuf_tensor("s_RSTD", [128, 512], F32)
    ones_bf = nc.alloc_sbuf_tensor("s_ones_bf", [128, 128], BF16)
    ident = nc.alloc_sbuf_tensor("s_ident", [128, 128], F32)
    cs_f = nc.alloc_sbuf_tensor("s_cs_f", [128, KC, 2], F32)
    cs_b = nc.alloc_sbuf_tensor("s_cs_b", [128, KC, 2], BF16)
    ng = nc.alloc_sbuf_tensor("s_ng", [128, 5, KC], F32)
    bmod = nc.alloc_sbuf_tensor("s_bmod", [128, 2, 48], F32)
    modT = nc.alloc_sbuf_tensor("s_modT", [128, 2, 48, 2], F32)
    gs = nc.alloc_sbuf_tensor("s_gs", [128, 2, 2, KC, 2], F32)
    L0B = nc.alloc_sbuf_tensor("s_L0B", [128, 3584], F32)
    rope = L0B[:, 0:1536].rearrange("p (a t) -> p a t", a=2)
    amask = nc.alloc_sbuf_tensor("s_amask", [128, 4, 2, 128], BF16)
    sink = nc.alloc_sbuf_tensor("s_sink", [128, 16], F32)
    esink = nc.alloc_sbuf_tensor("s_esink", [128, 16], F32)
    XH = nc.alloc_sbuf_tensor("s_XH", [128, KC, 256], F32)
    HH = nc.alloc_sbuf_tensor("s_HH", [128, KC, 256], BF16)
    CKV = L0B[:, 1536:2560].rearrange("p (a b n) -> p a b n", a=2, b=2)
    CKD = nc.alloc_sbuf_tensor("s_CKD", [128, 2, 4, 2, 64], BF16)
    ident_bf = nc.alloc_sbuf_tensor("s_ident_bf", [128, 128], BF16)
    KVO = L0B[:, 2560:3584].rearrange("p (a n) -> p a n", a=2)
    DEN = [nc.alloc_sbuf_tensor(f"s_DEN{i}", [128, 512], F32) for i in range(2)]
    RELU = [nc.alloc_sbuf_tensor(f"s_RELU{i}", [128, 512], BF16) for i in range(2)]
    PS = [nc.alloc_psum_tensor(f"p_PS{i}", [128, 512], F32) for i in range(8)]

    QT = BIG[:, 0:8192].rearrange("p (c t) -> p c t", c=8)
    KT = BIG[:, 8192:14336].rearrange("p (c t) -> p c t", c=4)
    VA = BIG[:, 14336:21504].rearrange("p (b k e) -> p b k e", b=14, k=4)
    ATT = BIG[:, 21504:29696].rearrange("p (c t) -> p c t", c=8)
    HID = BIG[:, 0:32768].rearrange("p (c t) -> p c t", c=32)

    ps_rr = [0]
    ps_mod = [8]

    def next_ps():
        i = ps_rr[0] % ps_mod[0]
        ps_rr[0] += 1
        return PS[i], ("ps", i)

    wb_rr = [0]

    def load_w(src_ap, shape_str=None, **kw):
        i = wb_rr[0] % 3
        wb_rr[0] += 1
        n = 1
        for s in src_ap.shape[1:]:
            n *= s
        dst = WB[i][:, 0:n]
        if shape_str is not None:
            dst = dst.rearrange(shape_str, **kw)
        P.op("pool", lambda e: e.dma_start(out=dst, in_=src_ap), writes=[("wb", i)], dma=True)
        return dst, ("wb", i)

    def dma_in(dst, src, key, eng="sp"):
        P.op(eng, lambda e: e.dma_start(out=dst, in_=src), writes=[key], dma=True)

    dma_in(X[:], x_own.ap(), "X_all")
    dma_in(XH[:], x_halo.ap(), "XH")
    dma_in(cs_f[:], cvec.ap(), "cs_f")
    dma_in(ng[:], ng_d.ap(), "ng")
    dma_in(bmod[:], bmod_d.ap(), "bmod")
    dma_in(rope, rope_d.ap(), "rope")
    dma_in(amask[:], amask_d.ap(), "amask")
    dma_in(ident[:], ident_d.ap(), "ident")
    dma_in(sink[:], sink_d.ap(), "sink")
    dma_in(CKV[:, :, 0, :], ck_d.ap().rearrange("(b p) n -> p b n", p=128), "CK")
    dma_in(CKV[:, :, 1, :], cv_d.ap().rearrange("(b p) n -> p b n", p=128), "CV")
    P.op("dve", lambda e: e.memset(ones_bf[:], 1.0), writes=["ones"])
    P.op("dve", lambda e: e.tensor_copy(ident_bf[:], ident[:]), reads=["ident"], writes=["ident_bf"])
    P.op("act", lambda e: e.activation(esink[:], sink[:], AF.Exp), reads=["sink"], writes=["esink"])
    P.op("act", lambda e: e.activation(cs_b[:], cs_f[:], AF.Silu), reads=["cs_f"], writes=["cs_b"])
    xkeys = [("X", kc, h) for kc in range(KC) for h in range(2)]
    for k in xkeys:
        P.last_write[k] = P.last_write["X_all"]

    def modulation_gen(i):
        psm = PS[7 - i]
        for piece in range(6):
            pk = ("psm", i, piece)
            for hb in range(2):
                cb = piece * 2 + hb
                wv, wk = load_w(w_mod.ap()[i].rearrange("(k p) n -> p k n", p=128)[:, :, cb * 512:(cb + 1) * 512],
                                "p (k n) -> p k n", k=KC)
                for cc in range(4):
                    j = cb * 4 + cc
                    for kc in range(KC):
                        P.op("pe", lambda e, j=j, kc=kc, cc=cc, wv=wv: e.matmul(
                            psm[:, j * 2:(j + 1) * 2], wv[:, kc, cc * 128:(cc + 1) * 128], cs_b[:, kc, :],
                            start=(kc == 0), stop=(kc == KC - 1)),
                            reads=[wk, "cs_b"], writes=[pk])
                if hb == 0:
                    yield
            bm = bass.AP(tensor=bmod, offset=i * 48 + piece * 8, ap=[[96, 128], [1, 8], [0, 2]])
            P.op("dve", lambda e, piece=piece, bm=bm: e.tensor_tensor(
                modT[:, i, piece * 8:(piece + 1) * 8, :], psm[:, piece * 16:(piece + 1) * 16].rearrange("p (j s) -> p j s", s=2), bm, ALU.add),
                reads=[pk, "bmod"], writes=[("modT", i, piece)])
            if piece in (1, 4):
                w = (piece - 1) // 3
                sc = modT[:, i, piece * 8:(piece + 1) * 8, :]
                gv = bass.AP(tensor=ng, offset=(2 * i + w) * KC, ap=[[5 * KC, 128], [1, KC], [0, 2]])
                P.op("dve", lambda e, w=w, sc=sc: e.tensor_scalar(gs[:, i, w], sc, 1.0, 32.0, ALU.add, ALU.mult),
                     reads=[("modT", i, piece)], writes=[("gs0", i, w)])
                P.op("dve", lambda e, w=w, gv=gv: e.tensor_tensor(gs[:, i, w], gs[:, i, w], gv, ALU.mult),
                     reads=[("gs0", i, w), "ng"], writes=[("gs", i, w)])
            yield

    def drain(g):
        for _ in g:
            pass

    def mod_piece(i, piece, kc, s):
        return modT[:, i, piece * 8 + kc, s:s + 1]

    def norm_mod(src, dst, t0, T, i, w, s, skey, dkey, final=False):
        ps, pk = next_ps()
        for kc in range(KC):
            P.op("act", lambda e, kc=kc: e.activation(SQ[:, kc, 0:T], src[:, kc, t0:t0 + T], AF.Square),
                 reads=[skey(kc)], writes=[("SQ", kc)])
        for kc in range(KC):
            P.op("pe", lambda e, kc=kc: e.matmul(ps[:, 0:T], ones_bf[:], SQ[:, kc, 0:T], start=(kc == 0), stop=(kc == KC - 1)),
                 reads=[("SQ", kc), "ones"], writes=[pk])
        NV = int(os.environ.get("NORMVAR", "9"))
        if NV < 2:
            return
        P.op("act", lambda e: e.activation(RSTD[:, 0:T], ps[:, 0:T], AF.Ln, bias=epsb[:, 0:1]), reads=[pk, "epsb"], writes=["RSTD"])
        P.op("act", lambda e: e.activation(RSTD[:, 0:T], RSTD[:, 0:T], AF.Exp, scale=-0.5), reads=["RSTD"], writes=["RSTD"])
        if NV < 3:
            return
        for kc in range(KC):
            sc = SCR[kc % 4]
            sk = ("SCR", kc % 4)
            P.op("dve", lambda e, kc=kc, sc=sc: e.tensor_tensor(sc[:, 0:T], src[:, kc, t0:t0 + T], RSTD[:, 0:T], ALU.mult),
                 reads=[skey(kc), "RSTD"], writes=[sk])
            if NV < 4:
                continue
            if final:
                P.op("dve", lambda e, kc=kc, sc=sc: e.tensor_scalar(dst[:, kc, 0:T], sc[:, 0:T], fng[:, kc:kc + 1], None, ALU.mult),
                     reads=[sk, "fng"], writes=[dkey(kc)])
            else:
                P.op("dve", lambda e, kc=kc, sc=sc: e.tensor_scalar(
                    dst[:, kc, t0:t0 + T], sc[:, 0:T], gs[:, i, w, kc, s:s + 1], mod_piece(i, 3 * w, kc, s), ALU.mult, ALU.add),
                    reads=[sk, ("gs", i, w), ("modT", i, 3 * w)], writes=[dkey(kc)])

    epsb = nc.alloc_sbuf_tensor("s_epsb", [128, 1], F32)
    P.op("dve", lambda e: e.memset(epsb[:], D * EPS), writes=["epsb"])
    fng = nc.alloc_sbuf_tensor("s_fng", [128, KC], F32)
    P.op("dve", lambda e: e.tensor_scalar(fng[:], ng[:, 4, :], 32.0, None, ALU.mult), reads=["ng"], writes=["fng"])

    def gated_residual(ps, pk, oc, h, i, piece):
        s = h
        P.op("dve", lambda e: e.scalar_tensor_tensor(
            X[:, oc, h * 512:(h + 1) * 512], ps[:, :], mod_piece(i, piece, oc, s), X[:, oc, h * 512:(h + 1) * 512],
            ALU.mult, ALU.add), reads=[pk, ("modT", i, piece), ("X", oc, h)], writes=[("X", oc, h)])

    def xkey_h(h):
        return lambda kc: ("X", kc, h)

    def hkey_h(h):
        return lambda kc: ("H", kc, h)

    def mlp(i, gen=None):
        for h in range(2):
            norm_mod(X, H, h * 512, 512, i, 1, h, xkey_h(h), hkey_h(h))
        w1v = w1_d.ap()[i].rearrange("(k p) n -> p k n", p=128)
        for hg in range(8):
            if gen is not None:
                next(gen, None)
            wv, wk = load_w(w1v[:, :, hg * 512:(hg + 1) * 512], "p (k n) -> p k n", k=KC)
            for cc in range(4):
                hc = hg * 4 + cc
                for h in range(2):
                    ps, pk = next_ps()
                    for kc in range(KC):
                        P.op("pe", lambda e, kc=kc, cc=cc, h=h, wv=wv, ps=ps: e.matmul(
                            ps[:, :], wv[:, kc, cc * 128:(cc + 1) * 128], H[:, kc, h * 512:(h + 1) * 512],
                            start=(kc == 0), stop=(kc == KC - 1)), reads=[wk, ("H", kc, h)], writes=[pk])
                    r = RELU[(hc * 2 + h) % 2]
                    rk = ("RELU", (hc * 2 + h) % 2)
                    P.op("act", lambda e, ps=ps, r=r: e.activation(r[:], ps[:, :], AF.Relu), reads=[pk], writes=[rk])
                    P.op("dve", lambda e, r=r, hc=hc, h=h: e.tensor_tensor(HID[:, hc, h * 512:(h + 1) * 512], r[:], r[:], ALU.mult),
                         reads=[rk], writes=[("HID", hc, h)])
        w2v = w2_d.ap()[i].rearrange("(k p) n -> p k n", p=128)
        for oc in range(KC):
            if gen is not None:
                next(gen, None)
            wv, wk = load_w(w2v[:, :, oc * 128:(oc + 1) * 128], "p (k n) -> p k n", k=32)
            for h in range(2):
                ps, pk = next_ps()
                for hc in range(32):
                    P.op("pe", lambda e, hc=hc, h=h, wv=wv, ps=ps: e.matmul(
                        ps[:, :], wv[:, hc, :], HID[:, hc, h * 512:(h + 1) * 512],
                        start=(hc == 0), stop=(hc == 31)), reads=[wk, ("HID", hc, h)], writes=[pk])
                gated_residual(ps, pk, oc, h, i, 5)

    def finish():
        for h in range(2):
            P.op("sp", lambda e, h=h: e.dma_start(out=y_d.ap()[:, :, h * 512:(h + 1) * 512], in_=X[:, :, h * 512:(h + 1) * 512]),
                 reads=[("X", kc, h) for kc in range(KC)], writes=[("y_out", h)], dma=True)
        return P.emit()

    ps_mod[0] = 6
    g0 = modulation_gen(0)
    for _ in range(4):
        next(g0)
    if stage == 1:
        return finish()
    for h in range(2):
        norm_mod(X, H, h * 512, 512, 0, 0, h, xkey_h(h), hkey_h(h))
    norm_mod(XH, HH, 0, 256, 0, 0, 1, lambda kc: "XH", lambda kc: ("HH", kc))

    if stage == 2:
        return finish()

    def proj_fm(wd, col0, jobs):
        wv, wk = load_w(wd.ap().rearrange("(k p) n -> p k n", p=128)[:, :, col0:col0 + 128], "p (k n) -> p k n", k=KC)
        for (ps, pk, off, n, rf, rkeys) in jobs:
            for kc in range(KC):
                P.op("pe", lambda e, kc=kc, off=off, n=n, rf=rf, ps=ps: e.matmul(
                    ps[:, off:off + n], wv[:, kc, :], rf(kc), start=(kc == 0), stop=(kc == KC - 1)),
                    reads=[wk, rkeys(kc)], writes=[pk])

    rhs_p = (0, 512, lambda kc: H[:, kc, 0:512], lambda kc: ("H", kc, 0))
    rhs_s = (0, 512, lambda kc: H[:, kc, 512:1024], lambda kc: ("H", kc, 1))
    rhs_hl = (0, 256, lambda kc: HH[:, kc, 0:256], lambda kc: ("HH", kc))

    def rope_evac(ps1, pk1, ps2, pk2, n, tab0, dst, dkey):
        a, ak = SCR[0], ("SCR", 0)
        b, bk = SCR[1], ("SCR", 1)
        P.op("dve", lambda e: e.tensor_tensor(a[:, 0:n], ps1[:, 0:n], rope[:, 0, tab0:tab0 + n], ALU.mult),
             reads=[pk1, "rope"], writes=[ak])
        P.op("dve", lambda e: e.tensor_tensor(b[:, 0:n], ps2[:, 0:n], rope[:, 1, tab0:tab0 + n], ALU.mult),
             reads=[pk2, "rope"], writes=[bk])
        P.op("dve", lambda e: e.tensor_tensor(dst, a[:, 0:n], b[:, 0:n], ALU.add), reads=[ak, bk], writes=[dkey])

    for c in range(8):
        ps, pk = next_ps()
        ps1, pk1 = next_ps()
        proj_fm(wq_d, c * 128, [(ps, pk) + rhs_p, (ps1, pk1) + rhs_s])
        P.op("act", lambda e, ps=ps, c=c: e.activation(QT[:, c, 0:512], ps[:, :], AF.Copy), reads=[pk], writes=[("QT", c, 0)])
        ps2, pk2 = next_ps()
        proj_fm(wqp_d, c * 128, [(ps2, pk2) + rhs_s])
        rope_evac(ps1, pk1, ps2, pk2, 512, 128, QT[:, c, 512:1024], ("QT", c, 1))
    for kv in range(4):
        ps, pk = next_ps()
        ps1, pk1 = next_ps()
        ps3, pk3 = next_ps()
        proj_fm(wkd_d, kv * 128, [(ps, pk) + rhs_p, (ps1, pk1) + rhs_s, (ps3, pk3) + rhs_hl])
        P.op("act", lambda e, ps=ps, kv=kv: e.activation(KT[:, kv, 0:512], ps[:, :], AF.Copy), reads=[pk], writes=[("KT", kv, 0)])
        ps2, pk2 = next_ps()
        ps4, pk4 = next_ps()
        proj_fm(wkdp_d, kv * 128, [(ps2, pk2) + rhs_s, (ps4, pk4) + rhs_hl])
        rope_evac(ps1, pk1, ps2, pk2, 512, 128, KT[:, kv, 640:1152], ("KT", kv, 1))
        ps1, pk1, ps2, pk2 = ps3, pk3, ps4, pk4
        rope_evac(ps1, pk1, ps2, pk2, 128, 0, KT[:, kv, 512:640], ("KT", kv, 2))
        a, ak = SCR[2], ("SCR", 2)
        b, bk = SCR[3], ("SCR", 3)
        P.op("dve", lambda e, ps1=ps1, a=a: e.tensor_tensor(a[:, 0:128], ps1[:, 128:256], rope[:, 0, 640:768], ALU.mult),
             reads=[pk1, "rope"], writes=[ak])
        P.op("dve", lambda e, ps2=ps2, b=b: e.tensor_tensor(b[:, 0:128], ps2[:, 128:256], rope[:, 1, 640:768], ALU.mult),
             reads=[pk2, "rope"], writes=[bk])
        P.op("dve", lambda e, kv=kv, a=a, b=b: e.tensor_tensor(KT[:, kv, 1152:1280], a[:, 0:128], b[:, 0:128], ALU.add),
             reads=[ak, bk], writes=[("KT", kv, 3)])
    if stage == 3:
        return finish()
    S4 = int(os.environ.get("S4VAR", "0"))
    for kb in range(0 if S4 == 1 else 2):
        for dup in range(2):
            P.op("dve", lambda e, kb=kb, dup=dup: e.tensor_copy(
                CKD[:, kb, :, dup, :], CKV[:, kb, 0, :].rearrange("p (k d) -> p k d", k=4)),
                reads=["CK"], writes=[("CKD", kb, dup)])
        for kv in range(4):
            ps, pk = next_ps()
            P.op("pe", lambda e, kb=kb, kv=kv, ps=ps: e.matmul(
                ps[:, 0:128], CKD[:, kb, kv].rearrange("p a d -> p (a d)"), ident_bf[:], start=True, stop=True),
                reads=[("CKD", kb, 0), ("CKD", kb, 1), "ident_bf"], writes=[pk])
            P.op("act", lambda e, kb=kb, kv=kv, ps=ps: e.activation(KT[:, kv, 1280 + kb * 128:1408 + kb * 128], ps[:, 0:128], AF.Copy),
                 reads=[pk], writes=[("KT", kv, 4 + kb)])
    wkv_v, wkv_k = load_w(wkv_d.ap().rearrange("(k p) n -> p k n", p=128), "p (k n) -> p k n", k=KC)
    vblocks = [(b, (lambda kc, b=b: H[:, kc, b * 128:(b + 1) * 128]), (lambda kc, b=b: ("H", kc, b // 4)), b if b < 4 else b + 1)
               for b in range(8)]
    vblocks += [(8, (lambda kc: HH[:, kc, 0:128]), (lambda kc: ("HH", kc)), 4),
                (9, (lambda kc: HH[:, kc, 128:256]), (lambda kc: ("HH", kc)), 9)]
    for (b, lf, lk, vb) in (vblocks if S4 != 2 else []):
        ps, pk = next_ps()
        for kc in range(KC):
            P.op("pe", lambda e, kc=kc, lf=lf, ps=ps: e.matmul(ps[:, :], lf(kc), wkv_v[:, kc, :], start=(kc == 0), stop=(kc == KC - 1)),
                 reads=[wkv_k, lk(kc)], writes=[pk])
        for dup in range(2):
            eng = "dve" if dup == 0 else "pool"
            if eng == "pool":
                continue
        for dup in range(2):
            P.op("act", lambda e, vb=vb, ps=ps, dup=dup: e.activation(
                VA[:, vb, :, dup * 64:(dup + 1) * 64], ps[:, 256:512].rearrange("p (k d) -> p k d", k=4), AF.Copy),
                reads=[pk], writes=[("VA", vb, dup)])
        if b < 4 and S4 != 3:
            P.op("act", lambda e, b=b, ps=ps: e.activation(KVO[:, b % 2, :], ps[:, :], AF.Copy), reads=[pk], writes=[("KVO", b % 2)])
            P.op("sp", lambda e, b=b: e.dma_start(out=nk_d.ap()[b * 128:(b + 1) * 128, :], in_=KVO[:, b % 2, 0:256]),
                 reads=[("KVO", b % 2)], writes=[("nk_out", b)], dma=True)
            P.op("sp", lambda e, b=b: e.dma_start(out=nv_d.ap()[b * 128:(b + 1) * 128, :], in_=KVO[:, b % 2, 256:512]),
                 reads=[("KVO", b % 2)], writes=[("nv_out", b)], dma=True)
    for kb in range(2):
        for dup in range(2):
            P.op("dve", lambda e, kb=kb, dup=dup: e.tensor_copy(
                VA[:, 10 + kb, :, dup * 64:(dup + 1) * 64], CKV[:, kb, 1, :].rearrange("p (k d) -> p k d", k=4)),
                reads=["CV"], writes=[("VA", 10 + kb, dup)])
    if stage == 4:
        return finish()
    def attention(qtok, half, keyblocks, pbuf):
        for kv in range(4):
            pt = PT[kv % 2]
            pbuf = kv % 2
            nkb = len(keyblocks)
            for j, (ktcol, ktk, vb, mside, qb) in enumerate(keyblocks):
                for hf in range(2):
                    ps, pk = next_ps()
                    P.op("pe", lambda e, hf=hf, kv=kv, ktcol=ktcol, ps=ps: e.matmul(
                        ps[:, 0:256].rearrange("p (c q) -> p c q", c=2),
                        KT[hf * 64:(hf + 1) * 64, kv, ktcol:ktcol + 128],
                        QT[hf * 64:(hf + 1) * 64, 2 * kv:2 * kv + 2, qtok:qtok + 128], start=True, stop=True),
                        reads=[("KT", kv, ktk), ("QT", 2 * kv, half), ("QT", 2 * kv + 1, half)], writes=[pk])
                    P.op("act", lambda e, j=j, hf=hf, ps=ps, pt=pt: e.activation(pt[:, j, hf * 256:(hf + 1) * 256], ps[:, 0:256], AF.Exp, scale=0.125),
                         reads=[pk], writes=[("PT", pbuf, j, hf)])
                if mside is not None:
                    P.op("dve", lambda e, j=j, pt=pt, qb=qb, mside=mside: e.tensor_tensor(
                        pt[:, j, :].rearrange("p (a q) -> p a q", a=4), pt[:, j, :].rearrange("p (a q) -> p a q", a=4),
                        amask[:, qb, mside, :].unsqueeze(1).broadcast_to([128, 4, 128]), ALU.mult),
                        reads=[("PT", pbuf, j, 0), ("PT", pbuf, j, 1), "amask"], writes=[("PT", pbuf, j, 0), ("PT", pbuf, j, 1)])
            pso, pko = next_ps()
            psl, pkl = next_ps()
            for j, (ktcol, ktk, vb, mside, qb) in enumerate(keyblocks):
                P.op("pe", lambda e, j=j, vb=vb, kv=kv, pso=pso, pt=pt, nkb=nkb: e.matmul(pso[:, :], VA[:, vb, kv, :], pt[:, j, :], start=(j == 0), stop=(j == nkb - 1)),
                     reads=[("VA", vb, 0), ("VA", vb, 1), ("PT", pbuf, j, 0), ("PT", pbuf, j, 1)], writes=[pko])
            for j in range(nkb):
                P.op("pe", lambda e, j=j, psl=psl, pt=pt, nkb=nkb: e.matmul(psl[:, :], ones_bf[:], pt[:, j, :], start=(j == 0), stop=(j == nkb - 1)),
                     reads=["ones", ("PT", pbuf, j, 0), ("PT", pbuf, j, 1)], writes=[pkl])
            den = DEN[kv % 2]
            dk = ("DEN", kv % 2)
            es = bass.AP(tensor=esink, offset=4 * kv, ap=[[16, 128], [1, 2], [2, 2], [0, 128]])
            P.op("dve", lambda e, es=es, den=den, psl=psl: e.tensor_tensor(
                den[:, :].rearrange("p (h c q) -> p h c q", h=2, c=2), psl[:, :].rearrange("p (h c q) -> p h c q", h=2, c=2), es, ALU.add),
                reads=[pkl, "esink"], writes=[dk])
            P.op("dve", lambda e, den=den: e.reciprocal(den[:, :], den[:, :]), reads=[dk], writes=[dk])
            for hf in range(2):
                P.op("dve", lambda e, hf=hf, kv=kv, den=den, pso=pso: e.tensor_tensor(
                    ATT[hf * 64:(hf + 1) * 64, 2 * kv:2 * kv + 2, qtok:qtok + 128],
                    pso[hf * 64:(hf + 1) * 64, hf * 256:(hf + 1) * 256].rearrange("p (c q) -> p c q", c=2),
                    den[hf * 64:(hf + 1) * 64, hf * 256:(hf + 1) * 256].rearrange("p (c q) -> p c q", c=2), ALU.mult),
                    reads=[pko, dk], writes=[("ATT", 2 * kv, half), ("ATT", 2 * kv + 1, half)])

    pb = 0
    for s in range(2):
        for qb in range(2):
            kbs = [(s * 256 + j * 128, 0, 2 * s + j, None, 0) for j in range(2)]
            attention(s * 256 + qb * 128, 0, kbs, pb % 2)
            next(g0, None)
            pb += 1
    for qb in range(4):
        kbs = []
        for j in range(3):
            eb = qb + j
            ktk = 2 if eb == 0 else (3 if eb == 5 else 1)
            kbs.append((512 + eb * 128, ktk, 4 + eb, (0 if j == 0 else (1 if j == 2 else None)), qb))
        for kb in range(2):
            kbs.append((1280 + kb * 128, 4 + kb, 10 + kb, None, qb))
        attention(512 + qb * 128, 1, kbs, pb % 2)
        next(g0, None)
        pb += 1

    if stage == 5:
        return finish()
    drain(g0)
    for oc in range(KC):
        wv, wk = load_w(wo_d.ap().rearrange("(k p) n -> p k n", p=128)[:, :, oc * 128:(oc + 1) * 128], "p (k n) -> p k n", k=KC)
        for h in range(2):
            ps, pk = next_ps()
            for kc in range(KC):
                P.op("pe", lambda e, kc=kc, h=h, wv=wv, ps=ps: e.matmul(
                    ps[:, :], wv[:, kc, :], ATT[:, kc, h * 512:(h + 1) * 512], start=(kc == 0), stop=(kc == KC - 1)),
                    reads=[wk, ("ATT", kc, h)], writes=[pk])
            gated_residual(ps, pk, oc, h, 0, 2)
    if stage == 6:
        return finish()
    g1 = modulation_gen(1)
    mlp(0, g1)
    drain(g1)

    if stage == 7:
        return finish()
    P.fence()
    ssm_sc_d = din("ssm_sc", [128, 4, 64])
    s0_d = din("s0", [128, 2, 64])
    bpad_d = din("bpad", [8, 128, 2048])
    cpad_d = din("cpad", [8, 128, 2048])
    dskip_d = din("dskip", [128, KC])
    qsel_d = din("qsel", [128, 4])
    wa_d = din("glu_w_a", [D, D])
    wb_d = din("glu_w_b", [D, D])
    news_d = dout("new_s", [128, 2, 64, 2])
    g_in = nc.dram_tensor("g_in", [128, 128], F32)
    g_out = nc.dram_tensor("g_out", [512, 128], F32)
    tab_d = nc.dram_tensor("tab_scratch", [8, 128, 8192], F32)

    NTAB = 40
    TB = L0B[:, 0:2560].rearrange("p (a m) -> p a m", a=NTAB)
    CF = L0B[:, 2560:3584].bitcast(BF16)
    L0ROW = 3584

    def tb_ap(off, dims):
        return bass.AP(tensor=L0B, offset=off, ap=[[L0ROW, 128]] + dims)

    P0F = PT[0][:].rearrange("p a b -> p (a b)").bitcast(F32)
    P1F = PT[1][:].rearrange("p a b -> p (a b)").bitcast(F32)
    S0 = P0F[:, 0:128].rearrange("p (a m) -> p a m", a=2)
    FIN1 = P0F[:, 128:256].rearrange("p (a m) -> p a m", a=2)
    FIN2 = P0F[:, 256:512].rearrange("p (a m s) -> p a m s", a=2, s=2)
    NS = P0F[:, 512:768].rearrange("p (a m s) -> p a m s", a=2, s=2)
    TF = P0F[:, 768:896].rearrange("p (a m) -> p a m", a=2)
    ST = P0F[:, 896:1024].rearrange("p (a m) -> p a m", a=2)
    WI = P0F[:, 1024:1152].rearrange("p (a m) -> p a m", a=2)
    GG = P1F[:, 0:512].rearrange("p (q a m) -> p q a m", q=4, a=2)
    CH = P1F[:, 512:1024].rearrange("p (q a m) -> p q a m", q=4, a=2)
    NT1 = P1F[:, 1024:1152].rearrange("p (m s) -> p m s", s=2)
    NT2 = P1F[:, 1152:1280].rearrange("p (m s) -> p m s", s=2)
    hpi = nc.alloc_sbuf_tensor("s_hpi", [128, 1], F32)
    dskip = nc.alloc_sbuf_tensor("s_dskip", [128, KC], F32)
    qsel = nc.alloc_sbuf_tensor("s_qsel", [128, 4], F32)
    FTMP = nc.alloc_sbuf_tensor("s_FTMP", [128, 4], F32)
    fence_keys = [("X", kc, h) for kc in range(KC) for h in range(2)]
    (T_LR, T_TH, T_DT, T_R, T_R512, T_FRE, T_FIM, T_IFRE, T_IFIM, T_P1RE, T_P1IM, T_T1, T_T2, T_T3, T_T4) = range(15)
    T_CKC, T_CKS, T_LAMRE, T_LAMIM, T_LOGDT = 15, 25, 35, 36, 37

    def tb(i):
        return TB[:, i, :]

    tbk = lambda i: ("TB", i)

    def vop(eng, fn, reads, writes):
        P.op(eng, fn, reads=reads, writes=writes)

    def tt(eng, out, a, b, op, rk, wk):
        vop(eng, lambda e: e.tensor_tensor(out, a, b, op), rk, wk)

    P.op("sp", lambda e: e.dma_start(out=TB[:, T_LAMRE:T_LAMRE + 4, :], in_=ssm_sc_d.ap()), reads=fence_keys, writes=["ssm_sc"], dma=True)
    for i in range(T_LAMRE, T_LAMRE + 4):
        P.last_write[tbk(i)] = P.last_write["ssm_sc"]
    P.op("sp", lambda e: e.dma_start(out=S0, in_=s0_d.ap()), reads=fence_keys, writes=["S0"], dma=True)
    dma_in(dskip[:], dskip_d.ap(), "dskip")
    dma_in(qsel[:], qsel_d.ap(), "qsel")
    P.op("dve", lambda e: e.memset(hpi[:], math.pi / 2), writes=["hpi"])
    P.op("act", lambda e: e.activation(tb(T_DT), tb(T_LOGDT), AF.Exp), reads=[tbk(T_LOGDT)], writes=[tbk(T_DT)])
    tt("dve", tb(T_LR), tb(T_LAMRE), tb(T_DT), ALU.mult, [tbk(T_LAMRE), tbk(T_DT)], [tbk(T_LR)])
    tt("dve", tb(T_TH), tb(T_LAMIM), tb(T_DT), ALU.mult, [tbk(T_LAMIM), tbk(T_DT)], [tbk(T_TH)])
    P.op("act", lambda e: e.activation(tb(T_R), tb(T_LR), AF.Exp), reads=[tbk(T_LR)], writes=[tbk(T_R)])
    P.op("act", lambda e: e.activation(tb(T_R512), tb(T_LR), AF.Exp, scale=512.0), reads=[tbk(T_LR)], writes=[tbk(T_R512)])
    P.op("act", lambda e: e.activation(tb(T_CKS), tb(T_TH), AF.Sin, scale=1.0 / 64), reads=[tbk(T_TH)], writes=[tbk(T_CKS)])
    P.op("act", lambda e: e.activation(tb(T_CKC), tb(T_TH), AF.Sin, scale=1.0 / 64, bias=hpi[:, 0:1]), reads=[tbk(T_TH), "hpi"], writes=[tbk(T_CKC)])

    def square(ci, si, co, so):
        tt("dve", tb(T_T1), tb(ci), tb(ci), ALU.mult, [tbk(ci)], [tbk(T_T1)])
        tt("dve", tb(T_T2), tb(si), tb(si), ALU.mult, [tbk(si)], [tbk(T_T2)])
        vop("dve", lambda e: e.scalar_tensor_tensor(tb(so), tb(ci), 2.0, tb(si), ALU.mult, ALU.mult), [tbk(ci), tbk(si)], [tbk(so)])
        tt("dve", tb(co), tb(T_T1), tb(T_T2), ALU.subtract, [tbk(T_T1), tbk(T_T2)], [tbk(co)])

    for _ in range(6):
        square(T_CKC, T_CKS, T_CKC, T_CKS)
    for k in range(9):
        square(T_CKC + k, T_CKS + k, T_CKC + k + 1, T_CKS + k + 1)

    def cmul(eng, ore, oim, are, aim, bre, bim, rk, wk, t1, t2, tk1, tk2):
        tt(eng, t1, are, bre, ALU.mult, rk, [tk1])
        tt(eng, t2, aim, bim, ALU.mult, rk, [tk2])
        tt(eng, ore, t1, t2, ALU.subtract, [tk1, tk2], [wk[0]])
        tt(eng, t1, are, bim, ALU.mult, rk, [tk1])
        tt(eng, t2, aim, bre, ALU.mult, rk, [tk2])
        tt(eng, oim, t1, t2, ALU.add, [tk1, tk2], [wk[1]])

    tt("dve", tb(T_T3), tb(T_R), tb(T_CKC), ALU.mult, [tbk(T_R), tbk(T_CKC)], [tbk(T_T3)])
    tt("dve", tb(T_T4), tb(T_R), tb(T_CKS), ALU.mult, [tbk(T_R), tbk(T_CKS)], [tbk(T_T4)])
    vop("dve", lambda e: e.tensor_scalar(tb(T_T3), tb(T_T3), -1.0, None, ALU.add), [tbk(T_T3)], [tbk(T_T3)])
    tt("dve", tb(T_T1), tb(T_LAMRE), tb(T_LAMRE), ALU.mult, [tbk(T_LAMRE)], [tbk(T_T1)])
    tt("dve", tb(T_T2), tb(T_LAMIM), tb(T_LAMIM), ALU.mult, [tbk(T_LAMIM)], [tbk(T_T2)])
    tt("dve", tb(T_T1), tb(T_T1), tb(T_T2), ALU.add, [tbk(T_T1), tbk(T_T2)], [tbk(T_T1)])
    vop("dve", lambda e: e.reciprocal(tb(T_T1), tb(T_T1)), [tbk(T_T1)], [tbk(T_T1)])
    tt("dve", tb(T_FRE), tb(T_T3), tb(T_LAMRE), ALU.mult, [tbk(T_T3), tbk(T_LAMRE)], [tbk(T_FRE)])
    tt("dve", tb(T_T2), tb(T_T4), tb(T_LAMIM), ALU.mult, [tbk(T_T4), tbk(T_LAMIM)], [tbk(T_T2)])
    tt("dve", tb(T_FRE), tb(T_FRE), tb(T_T2), ALU.add, [tbk(T_FRE), tbk(T_T2)], [tbk(T_FRE)])
    tt("dve", tb(T_FRE), tb(T_FRE), tb(T_T1), ALU.mult, [tbk(T_FRE), tbk(T_T1)], [tbk(T_FRE)])
    tt("dve", tb(T_FIM), tb(T_T4), tb(T_LAMRE), ALU.mult, [tbk(T_T4), tbk(T_LAMRE)], [tbk(T_FIM)])
    tt("dve", tb(T_T2), tb(T_T3), tb(T_LAMIM), ALU.mult, [tbk(T_T3), tbk(T_LAMIM)], [tbk(T_T2)])
    tt("dve", tb(T_FIM), tb(T_FIM), tb(T_T2), ALU.subtract, [tbk(T_FIM), tbk(T_T2)], [tbk(T_FIM)])
    tt("dve", tb(T_FIM), tb(T_FIM), tb(T_T1), ALU.mult, [tbk(T_FIM), tbk(T_T1)], [tbk(T_FIM)])
    tt("dve", tb(T_T1), tb(T_FRE), tb(T_FRE), ALU.mult, [tbk(T_FRE)], [tbk(T_T1)])
    tt("dve", tb(T_T2), tb(T_FIM), tb(T_FIM), ALU.mult, [tbk(T_FIM)], [tbk(T_T2)])
    tt("dve", tb(T_T1), tb(T_T1), tb(T_T2), ALU.add, [tbk(T_T1), tbk(T_T2)], [tbk(T_T1)])
    vop("dve", lambda e: e.reciprocal(tb(T_T1), tb(T_T1)), [tbk(T_T1)], [tbk(T_T1)])
    tt("dve", tb(T_IFRE), tb(T_FRE), tb(T_T1), ALU.mult, [tbk(T_FRE), tbk(T_T1)], [tbk(T_IFRE)])
    vop("dve", lambda e: e.scalar_tensor_tensor(tb(T_IFIM), tb(T_FIM), -1.0, tb(T_T1), ALU.mult, ALU.mult), [tbk(T_FIM), tbk(T_T1)], [tbk(T_IFIM)])
    tt("dve", tb(T_P1RE), tb(T_R512), tb(T_CKC + 9), ALU.mult, [tbk(T_R512), tbk(T_CKC + 9)], [tbk(T_P1RE)])
    tt("dve", tb(T_P1IM), tb(T_R512), tb(T_CKS + 9), ALU.mult, [tbk(T_R512), tbk(T_CKS + 9)], [tbk(T_P1IM)])

    for h in range(2):
        norm_mod(X, H, h * 512, 512, 1, 0, h, xkey_h(h), hkey_h(h))

    BF = BIG[:].bitcast(F32)
    UC = BF[:, 0:4096].rearrange("p (a t) -> p a t", a=8)
    US = BF[:, 4096:8192].rearrange("p (a t) -> p a t", a=8)
    EW = [BF[:, 8192 + i * 512:8192 + (i + 1) * 512] for i in range(4)]
    UT = [BF[:, 8192:10240].rearrange("p (a t) -> p a t", a=8), BF[:, 10240:12288].rearrange("p (a t) -> p a t", a=8)]
    SBS = [BIG[:, 24576:25600].rearrange("p (a t) -> p a t", a=2), BIG[:, 29696:30720].rearrange("p (a t) -> p a t", a=2)]
    CFT1 = BF[:, 15360:15488]
    CFT2 = BF[:, 15488:15616]
    sb_i = [0]
    G0 = XH[:].bitcast(BF16)
    G1 = BIG[:, 25600:29696].rearrange("p (c t) -> p c t", c=8)

    def Gv(kc, h):
        return (G0 if h == 0 else G1)[:, kc, :]

    def build_tables(gh):
        P.op("dve", lambda e: e.memset(UC[:, :, 0:1], 1.0), reads=fence_keys, writes=["UC"])
        P.op("dve", lambda e: e.memset(US[:, :, 0:1], 0.0), reads=fence_keys, writes=["US"])
        for k in range(9):
            n = 1 << k
            ckc = tb_ap((T_CKC + k) * 64 + 4 * gh, [[32, 2], [1, 4], [0, n]])
            cks = tb_ap((T_CKS + k) * 64 + 4 * gh, [[32, 2], [1, 4], [0, n]])
            v4 = lambda ap: ap.rearrange("p (d m) t -> p d m t", d=2)
            sc_, ss_ = v4(UC[:, :, 0:n]), v4(US[:, :, 0:n])
            dc_, ds_ = v4(UC[:, :, n:2 * n]), v4(US[:, :, n:2 * n])
            rk = ["UC", "US", tbk(T_CKC + k), tbk(T_CKS + k)]
            if n <= 128:
                t1, t3 = v4(UT[0][:, :, 0:n]), v4(UT[0][:, :, 128:128 + n])
                t2, t4 = v4(UT[1][:, :, 0:n]), v4(UT[1][:, :, 128:128 + n])
                tt("dve", t1, sc_, ckc, ALU.mult, rk, ["UT0a"])
                tt("dve", t2, ss_, cks, ALU.mult, rk, ["UT1a"])
                tt("dve", t3, sc_, cks, ALU.mult, rk, ["UT0b"])
                tt("dve", t4, ss_, ckc, ALU.mult, rk, ["UT1b"])
                tt("dve", dc_, t1, t2, ALU.subtract, ["UT0a", "UT1a"], ["UC"])
                tt("dve", ds_, t3, t4, ALU.add, ["UT0b", "UT1b"], ["US"])
            else:
                t1, t2 = v4(UT[0][:, :, 0:n]), v4(UT[1][:, :, 0:n])
                k0, k1 = ["UT0a", "UT0b"], ["UT1a", "UT1b"]
                tt("dve", t1, sc_, ckc, ALU.mult, rk, k0)
                tt("dve", t2, ss_, cks, ALU.mult, rk, k1)
                tt("dve", dc_, t1, t2, ALU.subtract, k0 + k1, ["UC"])
                tt("dve", t1, sc_, cks, ALU.mult, rk, k0)
                tt("dve", t2, ss_, ckc, ALU.mult, rk, k1)
                tt("dve", ds_, t1, t2, ALU.add, k0 + k1, ["US"])

    def seg_ap(ap512, lo, n, rev):
        v = ap512[:, lo:lo + n]
        return v[:, ::-1] if rev else v

    RS = [BF[:, 10240 + i * 512:10240 + (i + 1) * 512] for i in range(4)]

    def hv(ap512, half):
        return ap512.rearrange("p (s t) -> p s t", s=2) if half == 0 else ap512

    def tabv(T, a, half, d):
        n = 256 if half == 0 else 512
        v = T[:, a, 0:n]
        if d == 1:
            v = v[:, ::-1]
        return v.unsqueeze(1).broadcast_to([128, 2, 256]) if half == 0 else v

    def demod(a, md, half, ps_re, pk_re, ps_im, pk_im, d):
        uc, us = tabv(UC, a, half, d), tabv(US, a, half, d)
        xr, xi = hv(ps_re[:, :], half), hv(ps_im[:, :], half)
        t = [hv(SCR[i][:, :], half) for i in range(4)]
        rk = [pk_re, pk_im, "UC", "US"]
        tt("dve", t[0], xr, uc, ALU.mult, rk, [("SCR", 0)])
        tt("dve", t[1], xi, us, ALU.mult, rk, [("SCR", 1)])
        tt("dve", t[2], xi, uc, ALU.mult, rk, [("SCR", 2)])
        tt("dve", t[3], xr, us, ALU.mult, rk, [("SCR", 3)])
        tt("dve", hv(EW[0], half), t[0], t[1], ALU.add, [("SCR", 0), ("SCR", 1)], [("EW", 0)])
        tt("dve", hv(EW[1], half), t[2], t[3], ALU.subtract, [("SCR", 2), ("SCR", 3)], [("EW", 1)])

    def scans(md, half, d, use_init):
        segs = [(0, 256), (256, 256)] if half == 0 else [(0, 512)]
        for (lo, n) in segs:
            rbc = tb_ap(T_R * 64 + md, [[0, n]])
            for ri in range(2):
                init = WI[:, ri, md:md + 1] if use_init else 0.0
                src = seg_ap(EW[ri], lo, n, d == 1)
                dst = seg_ap(EW[2 + ri], lo, n, d == 1)
                vop("dve", lambda e, dst=dst, rbc=rbc, src=src, init=init: e.tensor_tensor_scan(dst, rbc, src, init, ALU.mult, ALU.add),
                    [("EW", ri), tbk(T_R)] + (["WI"] if use_init else []), [("EW", 2 + ri)])

    def finals(a, md, half, d, dst):
        L = 256 if half == 0 else 512
        ncol = 2 if half == 0 else 1
        c0 = (L - 1) if d == 0 else 0
        wre = EW[2][:, c0::256][:, 0:ncol] if half == 0 else EW[2][:, c0:c0 + 1]
        wim = EW[3][:, c0::256][:, 0:ncol] if half == 0 else EW[3][:, c0:c0 + 1]
        ucl, usl = UC[:, a, L - 1:L], US[:, a, L - 1:L]
        tA, tB = FTMP[:, 0:ncol], FTMP[:, 2:2 + ncol]
        rk = [("EW", 2), ("EW", 3), "UC", "US"]
        vop("dve", lambda e: e.tensor_scalar(tB, wre, usl, None, ALU.mult), rk, ["FTMPB"])
        vop("dve", lambda e: e.tensor_scalar(tA, wim, usl, None, ALU.mult), rk, ["FTMPA"])
        vop("dve", lambda e: e.scalar_tensor_tensor(dst[1], wim, ucl, tB, ALU.mult, ALU.add), rk + ["FTMPB"], [dst[3]])
        vop("dve", lambda e: e.scalar_tensor_tensor(dst[0], wre, ucl, tA, ALU.mult, ALU.subtract), rk + ["FTMPA"], [dst[2]])

    def remod(a, half, d, SB, sbi):
        uc, us = tabv(UC, a, half, d), tabv(US, a, half, d)
        wr, wi = hv(EW[2], half), hv(EW[3], half)
        t = [hv(RS[i], half) for i in range(4)]
        rk = [("EW", 2), ("EW", 3), "UC", "US"]
        tt("dve", t[0], wr, uc, ALU.mult, rk, [("RS", 0)])
        tt("dve", t[1], wi, us, ALU.mult, rk, [("RS", 1)])
        tt("dve", t[2], wi, uc, ALU.mult, rk, [("RS", 2)])
        tt("dve", t[3], wr, us, ALU.mult, rk, [("RS", 3)])
        tt("dve", hv(SB[:, 0, :], half), t[0], t[1], ALU.subtract, [("RS", 0), ("RS", 1)], [("SB", sbi, 0)])
        vop("dve", lambda e: e.scalar_tensor_tensor(hv(SB[:, 1, :], half), t[2], -1.0, t[3], ALU.mult, ALU.subtract),
            [("RS", 2), ("RS", 3)], [("SB", sbi, 1)])

    def x_matmuls(bv, bk, gh, d, m4, half):
        outs = []
        for ri in range(2):
            ps, pk = next_ps()
            col = ((d * 2 + ri) * 4 + m4) * 128
            P.op("pe", lambda e, ps=ps, col=col: e.matmul(ps[:, :], bv[:, col:col + 128], H[:, gh, half * 512:(half + 1) * 512], start=True, stop=True),
                 reads=[bk, ("H", gh, half)], writes=[pk])
            outs += [ps, pk]
        return outs

    ps_mod[0] = 6
    bpv = lambda gh: bpad_d.ap()[gh]
    for gh in range(8):
        build_tables(gh)
        P.op("sp", lambda e, gh=gh: e.dma_start(out=tab_d.ap()[gh], in_=BF[:, 0:8192]), reads=["UC", "US"], writes=[("tab_d", gh)], dma=True)
        bv, bk = load_w(bpv(gh))
        for d in range(2):
            for m4 in range(4):
                a = d * 4 + m4
                md = d * 32 + 4 * gh + m4
                psr, pkr, psi, pki = x_matmuls(bv, bk, gh, d, m4, 1)
                demod(a, md, 1, psr, pkr, psi, pki, d)
                scans(md, 1, d, False)
                finals(a, md, 1, d, (FIN1[:, 0, md:md + 1], FIN1[:, 1, md:md + 1], ("FIN1", md), ("FIN1", md)))
    fin1k = [("FIN1", md) for md in range(64)]
    cmul("dve", TF[:, 0, :], TF[:, 1, :], FIN1[:, 0, :], FIN1[:, 1, :], tb(T_FRE), tb(T_FIM),
         fin1k + [tbk(T_FRE), tbk(T_FIM)], ["TF", "TF"], tb(T_T1), tb(T_T2), tbk(T_T1), tbk(T_T2))
    P.op("sp", lambda e: e.dma_start(out=g_in.ap(), in_=TF.rearrange("p a b -> p (a b)")), reads=["TF"], writes=["g_in"], dma=True)
    P.op("pool", lambda e: e.collective_compute("AllGather", ALU.bypass, replica_groups=[[0, 1, 2, 3], [4, 5, 6, 7]],
                                                  ins=[g_in.ap().opt()], outs=[g_out.ap().opt()]),
         reads=["g_in"], writes=["g_out"], cc=True)
    P.op("sp", lambda e: e.dma_start(out=GG.rearrange("p q a b -> p q (a b)"), in_=g_out.ap().rearrange("(q p) n -> p q n", p=128)),
         reads=["g_out"], writes=["GG"], dma=True)
    for d in range(2):
        sl = slice(d * 32, (d + 1) * 32)
        q0 = 0 if d == 0 else 3
        for ri in range(2):
            vop("dve", lambda e, ri=ri, sl=sl, q0=q0: e.tensor_copy(CH[:, q0, ri, sl], S0[:, ri, sl]), ["S0"], [("CH", q0, d)])
        order = [(0, 1, 0), (1, 2, 1), (2, 3, 2)] if d == 0 else [(3, 2, 3), (2, 1, 2), (1, 0, 1)]
        for (qs, qd, qg) in order:
            cmul("dve", CH[:, qd, 0, sl], CH[:, qd, 1, sl], CH[:, qs, 0, sl], CH[:, qs, 1, sl], TB[:, T_P1RE, sl], TB[:, T_P1IM, sl],
                 [("CH", qs, d), tbk(T_P1RE), tbk(T_P1IM)], [("CH", qd, d), ("CH", qd, d)], TB[:, T_T1, sl], TB[:, T_T2, sl], tbk(T_T1), tbk(T_T2))
            for ri in range(2):
                tt("dve", CH[:, qd, ri, sl], CH[:, qd, ri, sl], GG[:, qg, ri, sl], ALU.add, [("CH", qd, d), "GG"], [("CH", qd, d)])
    chk = [("CH", q, d) for q in range(4) for d in range(2)]
    for ri in range(2):
        vop("dve", lambda e, ri=ri: e.tensor_scalar(ST[:, ri, :], CH[:, 0, ri, :], qsel[:, 0:1], None, ALU.mult), chk + ["qsel"], [("ST", ri)])
        for q in range(1, 4):
            vop("dve", lambda e, ri=ri, q=q: e.scalar_tensor_tensor(ST[:, ri, :], CH[:, q, ri, :], qsel[:, q:q + 1], ST[:, ri, :], ALU.mult, ALU.add),
                chk + ["qsel", ("ST", ri)], [("ST", ri)])
    cmul("dve", TF[:, 0, :], TF[:, 1, :], ST[:, 0, :], ST[:, 1, :], tb(T_IFRE), tb(T_IFIM),
         [("ST", 0), ("ST", 1), tbk(T_IFRE), tbk(T_IFIM), "g_in"], ["TF2", "TF2"], tb(T_T1), tb(T_T2), tbk(T_T1), tbk(T_T2))
    cmul("dve", WI[:, 0, :], WI[:, 1, :], TF[:, 0, :], TF[:, 1, :], tb(T_CKC), tb(T_CKS),
         ["TF2", tbk(T_CKC), tbk(T_CKS)], ["WI", "WI"], tb(T_T1), tb(T_T2), tbk(T_T1), tbk(T_T2))

    cpv = lambda gh: cpad_d.ap()[gh]
    for gh in range(8):
        P.op("sp", lambda e, gh=gh: e.dma_start(out=BF[:, 0:8192], in_=tab_d.ap()[gh]), reads=[("tab_d", gh)], writes=["UC", "US"], dma=True)
        bv, bk = load_w(bpv(gh))
        cv_, ck_ = load_w(cpv(gh))
        for d in range(2):
            for m4 in range(4):
                md = d * 32 + 4 * gh + m4
                cre = cv_[:, ((d * 2 + 0) * 4 + m4) * 128:((d * 2 + 0) * 4 + m4 + 1) * 128]
                cim = cv_[:, ((d * 2 + 1) * 4 + m4) * 128:((d * 2 + 1) * 4 + m4 + 1) * 128]
                ore = CF[:, ((d * 2 + 0) * 4 + m4) * 128:((d * 2 + 0) * 4 + m4 + 1) * 128]
                oim = CF[:, ((d * 2 + 1) * 4 + m4) * 128:((d * 2 + 1) * 4 + m4 + 1) * 128]
                fre, fim = TB[:, T_FRE, md:md + 1], TB[:, T_FIM, md:md + 1]
                rk = [ck_, tbk(T_FRE), tbk(T_FIM)]
                vop("dve", lambda e, cim=cim, fim=fim: e.tensor_scalar(CFT1, cim, fim, None, ALU.mult), rk, ["CFT1"])
                vop("dve", lambda e, cim=cim, fre=fre: e.tensor_scalar(CFT2, cim, fre, None, ALU.mult), rk, ["CFT2"])
                vop("dve", lambda e, ore=ore, cre=cre, fre=fre: e.scalar_tensor_tensor(ore, cre, fre, CFT1, ALU.mult, ALU.subtract),
                    rk + ["CFT1"], [("CF", d, m4, 0)])
                vop("dve", lambda e, oim=oim, cre=cre, fim=fim: e.scalar_tensor_tensor(oim, cre, fim, CFT2, ALU.mult, ALU.add),
                    rk + ["CFT2"], [("CF", d, m4, 1)])
        ysl = [(PS[6], ("ps", 6)), (PS[7], ("ps", 7))]
        cnt = [0, 0]
        for d in range(2):
            for m4 in range(4):
                a = d * 4 + m4
                md = d * 32 + 4 * gh + m4
                for half in range(2):
                    psr, pkr, psi, pki = x_matmuls(bv, bk, gh, d, m4, half)
                    demod(a, md, half, psr, pkr, psi, pki, d)
                    scans(md, half, d, half == 1)
                    sbi = sb_i[0] % 2
                    sb_i[0] += 1
                    SB = SBS[sbi]
                    remod(a, half, d, SB, sbi)
                    if half == 0:
                        finals(a, md, 0, d, (FIN2[:, 0, md, :], FIN2[:, 1, md, :], ("FIN2", md), ("FIN2", md)))
                    psy, pky = ysl[half]
                    for ri in range(2):
                        col = ((d * 2 + ri) * 4 + m4) * 128
                        first = cnt[half] == 0
                        last = cnt[half] == 15
                        cnt[half] += 1
                        P.op("pe", lambda e, psy=psy, col=col, ri=ri, first=first, last=last, SB=SB: e.matmul(
                            psy[:, :], CF[:, col:col + 128], SB[:, ri, :], start=first, stop=last),
                            reads=[("CF", d, m4, ri), ("SB", sbi, ri)], writes=[pky])
        for half in range(2):
            psy, pky = ysl[half]
            yp, t1, t2 = RS[0], RS[1], RS[2]
            hs = slice(half * 512, (half + 1) * 512)
            vop("dve", lambda e, psy=psy, hs=hs, gh=gh: e.scalar_tensor_tensor(yp, H[:, gh, hs], dskip[:, gh:gh + 1], psy[:, :], ALU.mult, ALU.add),
                [pky, ("H", gh, half), "dskip"], [("RS", 0)])
            tt("pool", t1, yp, yp, ALU.mult, [("RS", 0)], [("RS", 1)])
            vop("pool", lambda e: e.tensor_scalar(t1, t1, 0.044715, 1.0, ALU.mult, ALU.add), [("RS", 1)], [("RS", 1)])
            tt("pool", t1, t1, yp, ALU.mult, [("RS", 1), ("RS", 0)], [("RS", 1)])
            P.op("act", lambda e: e.activation(t2, t1, AF.Sigmoid, scale=1.5957691216057308), reads=[("RS", 1)], writes=[("RS", 2)])
            tt("pool", Gv(gh, half), yp, t2, ALU.mult, [("RS", 0), ("RS", 2)], [("G", gh, half)])
    ps_mod[0] = 8
    fin2k = [("FIN2", md) for md in range(64)]
    fre_b = tb_ap(T_FRE * 64, [[1, 64], [0, 2]])
    fim_b = tb_ap(T_FIM * 64, [[1, 64], [0, 2]])
    cmul("dve", NS[:, 0], NS[:, 1], FIN2[:, 0], FIN2[:, 1], fre_b, fim_b, fin2k + [tbk(T_FRE), tbk(T_FIM)],
         ["NS", "NS"], NT1, NT2, "NT1", "NT2")
    P.op("sp", lambda e: e.dma_start(out=news_d.ap(), in_=NS), reads=["NS"], writes=["news_out"], dma=True)

    for oc in range(KC):
        wav, wak = load_w(wa_d.ap().rearrange("(k p) n -> p k n", p=128)[:, :, oc * 128:(oc + 1) * 128], "p (k n) -> p k n", k=KC)
        wbv, wbk = load_w(wb_d.ap().rearrange("(k p) n -> p k n", p=128)[:, :, oc * 128:(oc + 1) * 128], "p (k n) -> p k n", k=KC)
        for h in range(2):
            psa, pka = next_ps()
            psb, pkb = next_ps()
            for (wv_, wk_, ps_, pk_) in ((wav, wak, psa, pka), (wbv, wbk, psb, pkb)):
                for kc in range(KC):
                    P.op("pe", lambda e, kc=kc, h=h, wv_=wv_, ps_=ps_: e.matmul(
                        ps_[:, :], wv_[:, kc, :], Gv(kc, h), start=(kc == 0), stop=(kc == KC - 1)),
                        reads=[wk_, ("G", kc, h)], writes=[pk_])
            sg, pr = SCR[0], SCR[1]
            P.op("act", lambda e, psb=psb: e.activation(sg[:], psb[:, :], AF.Sigmoid), reads=[pkb], writes=[("SCR", 0)])
            tt("dve", pr[:], psa[:, :], sg[:], ALU.mult, [pka, ("SCR", 0)], [("SCR", 1)])
            P.op("dve", lambda e, oc=oc, h=h: e.scalar_tensor_tensor(
                X[:, oc, h * 512:(h + 1) * 512], pr[:], mod_piece(1, 2, oc, h), X[:, oc, h * 512:(h + 1) * 512], ALU.mult, ALU.add),
                reads=[("SCR", 1), ("modT", 1, 2), ("X", oc, h)], writes=[("X", oc, h)])
    mlp(1)
    YF = BF[:, 0:4096].rearrange("p (c t) -> p c t", c=8)
    for h in range(2):
        norm_mod(X, YF, h * 512, 512, 1, 0, h, xkey_h(h), lambda kc: ("YF", kc), final=True)
        P.op("sp", lambda e, h=h: e.dma_start(out=y_d.ap()[:, :, h * 512:(h + 1) * 512], in_=YF[:, :, 0:512]),
             reads=[("YF", kc) for kc in range(KC)], writes=[("y_out", h)], dma=True)
    return P.emit()


def _fm(x_tok):
    T = x_tok.shape[0]
    return np.ascontiguousarray(x_tok.reshape(T, KC, 128).transpose(2, 1, 0))


def _rope_tables(q):
    pos = 512 * q - 128 + np.arange(768)
    row = (pos // 64).astype(np.float32)
    col = (pos % 64).astype(np.float32)
    freqs = (10000.0 ** (-np.arange(16, dtype=np.float32) / 16)).astype(np.float32)
    tab = np.zeros((128, 2, 768), np.float32)
    for p in range(128):
        d = p % 64
        f = freqs[d % 16]
        ang = (row if d < 32 else col) * f
        tab[p, 0] = np.cos(ang)
        tab[p, 1] = np.sin(ang) * (-1.0 if (d % 32) < 16 else 1.0)
    return tab


def _amask(q):
    m = np.zeros((128, 4, 2, 128), np.float32)
    jj = np.arange(128)[:, None]
    rr = np.arange(128)[None, :]
    for qb in range(4):
        lpos0 = 512 * q + 128 * qb - 128
        rpos0 = 512 * q + 128 * qb + 128
        m[:, qb, 0, :] = (jj >= rr) * (1.0 if lpos0 >= 0 else 0.0)
        m[:, qb, 1, :] = (jj <= rr) * (1.0 if rpos0 < 2048 else 0.0)
    return m


def _prep_shared(inp):
    f = np.float32
    sh = {}
    w_qkv = inp["w_qkv"][0]
    wq = w_qkv[:, :1024]
    wk = w_qkv[:, 1024:1280]
    wv = w_qkv[:, 1280:1536]
    d = np.arange(64)
    partner = np.where((d % 32) < 16, d + 16, d - 16)
    qperm = (np.arange(16)[:, None] * 64 + partner[None, :]).reshape(-1)
    kperm = (np.arange(4)[:, None] * 64 + partner[None, :]).reshape(-1)
    dup = (np.arange(4)[:, None, None] * 64 + np.zeros((1, 2, 1), int) + d[None, None, :]).reshape(-1)
    sh["wq"] = np.ascontiguousarray(wq)
    sh["wqp"] = np.ascontiguousarray(wq[:, qperm])
    sh["wkd"] = np.ascontiguousarray(wk[:, dup])
    sh["wkdp"] = np.ascontiguousarray(wk[:, kperm][:, dup])
    sh["wkv"] = np.ascontiguousarray(np.concatenate([wk, wv], axis=1))
    sh["w_o"] = np.ascontiguousarray(inp["w_o"][0])
    sh["w_mod"] = np.ascontiguousarray(inp["w_mod"])
    sh["mlp_w1"] = np.ascontiguousarray(inp["mlp_w1"])
    sh["mlp_w2"] = np.ascontiguousarray(inp["mlp_w2"])
    ngs = np.stack([inp["norm1_g"][0], inp["norm2_g"][0], inp["norm1_g"][1], inp["norm2_g"][1], inp["final_norm_g"]], 0)
    sh["ng"] = np.ascontiguousarray(ngs.reshape(5, KC, 128).transpose(2, 0, 1)).astype(f)
    sh["bmodT"] = np.ascontiguousarray(inp["b_mod"].reshape(2, 48, 128).transpose(2, 0, 1)).astype(f)
    sh["ident"] = np.eye(128, dtype=f)
    sh["sink"] = np.ascontiguousarray(np.broadcast_to(inp["attn_sink"][0][None, :], (128, 16))).astype(f)
    def gp_md(a):
        return a.reshape(2, 32, 2, 64).transpose(2, 3, 0, 1).reshape(128, 64)
    ldt = np.broadcast_to(inp["ssm_log_dt"][0][:, :, None], (2, 64, 64))
    sc = np.zeros((128, 4, 64), f)
    sc[:, 0] = gp_md(inp["ssm_lam_re"][0])
    sc[:, 1] = gp_md(inp["ssm_lam_im"][0])
    sc[:, 2] = gp_md(ldt)
    sh["ssm_sc"] = sc
    bp = np.zeros((8, 8, 16, 2, 2, 4, 2, 64), f)
    cp = np.zeros((8, 2, 64, 2, 2, 4, 8, 16), f)
    Bs = (inp["ssm_b_re"][0], inp["ssm_b_im"][0])
    Cs = (inp["ssm_c_re"][0], inp["ssm_c_im"][0])
    for gh in range(8):
        for m4 in range(4):
            for g2 in range(2):
                g = 8 * gh + 2 * m4 + g2
                gl = 2 * m4 + g2
                for d in range(2):
                    for ri in range(2):
                        bp[gh, gl, :, d, ri, m4, g2, :] = Bs[ri][d, g].T
                        cp[gh, g2, :, d, ri, m4, gl, :] = Cs[ri][d, g].T
    sh["bpad"] = bp.reshape(8, 128, 2048)
    sh["cpad"] = cp.reshape(8, 128, 2048)
    sh["dskip"] = np.ascontiguousarray(inp["ssm_d"][0].reshape(KC, 128).T).astype(f)
    sh["glu_w_a"] = np.ascontiguousarray(inp["glu_w_a"][0])
    sh["glu_w_b"] = np.ascontiguousarray(inp["glu_w_b"][0])
    return sh


def _prep_core(inp, r, sh):
    b, q = r // 4, r % 4
    xs = inp["x_sample"][b]
    own = np.concatenate([inp["x_prompt"][2 * r], inp["x_prompt"][2 * r + 1], xs[512 * q:512 * q + 512]], 0)
    halo = np.zeros((256, D), np.float32)
    if q > 0:
        halo[:128] = xs[512 * q - 128:512 * q]
    if q < 3:
        halo[128:] = xs[512 * q + 512:512 * q + 640]
    cv = np.stack([inp["c_ctx"], inp["c"][b]], 1)
    m = dict(sh)
    m["x_own"] = _fm(own)
    m["x_halo"] = _fm(halo)
    m["cvec"] = np.ascontiguousarray(cv.reshape(KC, 128, 2).transpose(1, 0, 2)).astype(np.float32)
    m["rope"] = _rope_tables(q)
    import ml_dtypes
    m["amask"] = _amask(q).astype(ml_dtypes.bfloat16)
    m["ck"] = np.ascontiguousarray(inp["cache_k"][b, 0].reshape(256, 256))
    m["cv"] = np.ascontiguousarray(inp["cache_v"][b, 0].reshape(256, 256))
    st = inp["state_ssm"][b, 0]
    m["s0"] = np.ascontiguousarray(st.reshape(2, 2, 32, 2, 64).transpose(3, 4, 1, 0, 2).reshape(128, 2, 64)).astype(np.float32)
    qs = np.zeros((128, 4), np.float32)
    qs[:, q] = 1.0
    m["qsel"] = qs
    return m


_NC_CACHE = {}


STAGE = 99


def kernel(**inputs):
    inp = {k: np.asarray(v) for k, v in inputs.items()}
    sh = _prep_shared(inp)
    in_maps = [_prep_core(inp, r, sh) for r in range(NCORE)]
    nc = build_program(STAGE)
    res = run_bass_kernel_spmd(nc, in_maps, core_ids=list(range(NCORE)))
    y_prompt = np.zeros((16, 256, D), np.float32)
    y_sample = np.zeros((2, 2048, D), np.float32)
    new_k = np.zeros((16, 1, 256, 4, 64), np.float32)
    new_v = np.zeros((16, 1, 256, 4, 64), np.float32)
    new_s = np.zeros((16, 1, 2, 2, 64, 64), np.float32)
    for r in range(NCORE):
        o = res.results[r]
        b, q = r // 4, r % 4
        y = np.asarray(o["y"]).transpose(2, 1, 0).reshape(NT, D)
        y_prompt[2 * r] = y[0:256]
        y_prompt[2 * r + 1] = y[256:512]
        y_sample[b, 512 * q:512 * q + 512] = y[512:1024]
        nk = np.asarray(o["new_k"]).reshape(2, 256, 4, 64)
        nv = np.asarray(o["new_v"]).reshape(2, 256, 4, 64)
        new_k[2 * r:2 * r + 2, 0] = nk
        new_v[2 * r:2 * r + 2, 0] = nv
        ns = np.asarray(o["new_s"]).reshape(2, 64, 2, 2, 32, 2)
        new_s[2 * r:2 * r + 2, 0] = ns.transpose(5, 3, 2, 4, 0, 1).reshape(2, 2, 2, 64, 64)
    return (y_prompt, y_sample, new_k, new_v, new_s)
```
